# Optimizing a Trainium2 kernel written in Bass

```python
import math
import jax, jax.numpy as jnp
from jax import lax
import numpy as np

D_MODEL = 1024
BATCH = 8
SEQ = 2048
DEPTH = 2

CHUNK = 64
N_META = 16
BLOCK_Q = 128
N_PAD = BLOCK_Q - N_META
N_MIXERS = 2
D_FF = 2816
RMS_EPS = 1e-6
NEG_INF = -1e30
DA_HEADS = 8
DA_HEAD_DIM = D_MODEL // DA_HEADS // 2
DA_V_DIM = 2 * DA_HEAD_DIM
DA_QK_WIDTH = DA_HEADS * 2 * DA_HEAD_DIM
DA_IN_WIDTH = 2 * DA_QK_WIDTH + DA_HEADS * DA_V_DIM
GLA_HEADS = 4
GLA_KEY_WIDTH = D_MODEL // 2
GLA_VAL_WIDTH = D_MODEL
GLA_DK = GLA_KEY_WIDTH // GLA_HEADS
GLA_DV = GLA_VAL_WIDTH // GLA_HEADS
GLA_GATE_RANK = 16
GLA_GATE_NORM = 16.0
GLA_IN_WIDTH = 2 * GLA_KEY_WIDTH + 2 * GLA_VAL_WIDTH + GLA_GATE_RANK
N_A = (DEPTH + 1) // 2
N_B = DEPTH // 2

kernel_name = "hybrid_diffattn_gla_macaron"


def rmsnorm(x, g):
    xf = x.astype(jnp.float32)
    y = xf * lax.rsqrt(jnp.mean(xf * xf, axis=-1, keepdims=True) + RMS_EPS)
    return (y * g.astype(jnp.float32)).astype(x.dtype)


def swiglu(h, w_in, w_out):
    gu = h @ w_in
    g, u = gu[..., :D_FF], gu[..., D_FF:]
    return (jax.nn.silu(g) * u) @ w_out


def pad_front(h):
    return jnp.pad(h, ((0, 0), (N_PAD, 0), (0, 0)))


def diff_attention(h, w_in, lam_p, subln_g, w_out, lam_init):
    B, L, _ = h.shape
    hp = pad_front(h)
    P = hp.shape[1]
    qkv = hp @ w_in
    q = qkv[..., :DA_QK_WIDTH].reshape(B, P, DA_HEADS, 2, DA_HEAD_DIM).transpose(0, 3, 2, 1, 4)
    k = qkv[..., DA_QK_WIDTH:2 * DA_QK_WIDTH].reshape(B, P, DA_HEADS, 2, DA_HEAD_DIM).transpose(0, 3, 2, 1, 4)
    v = qkv[..., 2 * DA_QK_WIDTH:].reshape(B, P, DA_HEADS, DA_V_DIM).transpose(0, 2, 1, 3)
    lp = lam_p.astype(jnp.float32)
    lam = jnp.exp(jnp.sum(lp[0] * lp[1])) - jnp.exp(jnp.sum(lp[2] * lp[3])) + lam_init
    pos = jnp.arange(P)
    chunk_id = pos // CHUNK
    k_valid = pos >= N_PAD
    scale = DA_HEAD_DIM ** -0.5
    outs = []
    for s in range(0, P, BLOCK_Q):
        e = s + BLOCK_Q
        sc = jnp.einsum('bmhqd,bmhkd->bmhqk', q[:, :, :, s:e], k[:, :, :, :e]).astype(jnp.float32) * scale
        allowed = k_valid[None, :e] & (chunk_id[None, :e] <= chunk_id[s:e, None])
        p = jax.nn.softmax(jnp.where(allowed, sc, NEG_INF), axis=-1)
        a = p[:, 0] - lam * p[:, 1]
        outs.append(jnp.einsum('bhqk,bhkv->bhqv', a.astype(v.dtype), v[:, :, :e]))
    o = jnp.concatenate(outs, axis=2)[:, :, N_PAD:]
    o = rmsnorm(o, subln_g) * (1.0 - lam_init)
    return o.transpose(0, 2, 1, 3).reshape(B, L, DA_HEADS * DA_V_DIM) @ w_out


def gla_mixer(h, w_in, w_gate2, b_gate2, norm_g, w_out):
    B, L, _ = h.shape
    hp = pad_front(h)
    P = hp.shape[1]
    NC = P // CHUNK
    proj = hp @ w_in
    o1 = GLA_KEY_WIDTH
    o2 = 2 * GLA_KEY_WIDTH
    o3 = o2 + GLA_VAL_WIDTH
    o4 = o3 + GLA_VAL_WIDTH
    q, k, v, og, glr = proj[..., :o1], proj[..., o1:o2], proj[..., o2:o3], proj[..., o3:o4], proj[..., o4:]
    gk = jax.nn.log_sigmoid((glr @ w_gate2 + b_gate2).astype(jnp.float32)) / GLA_GATE_NORM
    valid = (jnp.arange(P) >= N_PAD)[None, :, None]
    gk = jnp.where(valid, gk, 0.0)
    q = (q.astype(jnp.float32) * GLA_DK ** -0.5).reshape(B, NC, CHUNK, GLA_HEADS, GLA_DK)
    k = k.astype(jnp.float32).reshape(B, NC, CHUNK, GLA_HEADS, GLA_DK)
    v = v.astype(jnp.float32).reshape(B, NC, CHUNK, GLA_HEADS, GLA_DV)
    G = jnp.cumsum(gk.reshape(B, NC, CHUNK, GLA_HEADS, GLA_DK), axis=2)
    G_end = G[:, :, -1:]
    k_dec = k * jnp.exp(G_end - G)
    kv = jnp.einsum('bnchk,bnchv->bnhkv', k_dec, v)
    decay = jnp.exp(G_end[:, :, 0])

    def step(S, inp):
        kv_c, d_c, q_c = inp
        S = d_c[..., None] * S + kv_c
        return S, jnp.einsum('bchk,bhkv->bchv', q_c, S)

    S0 = jnp.zeros((B, GLA_HEADS, GLA_DK, GLA_DV), jnp.float32)
    _, o = lax.scan(step, S0, (jnp.moveaxis(kv, 1, 0), jnp.moveaxis(decay, 1, 0), jnp.moveaxis(q, 1, 0)))
    o = jnp.moveaxis(o, 0, 1).reshape(B, P, GLA_HEADS, GLA_DV)[:, N_PAD:].astype(h.dtype)
    gate = jax.nn.silu(og[:, N_PAD:].reshape(B, L, GLA_HEADS, GLA_DV))
    o = rmsnorm(o, norm_g) * gate
    return o.reshape(B, L, GLA_VAL_WIDTH) @ w_out


def setup_inputs(seed: int = 0) -> dict:
    key = jax.random.key(seed)
    ks = jax.random.split(key, 24)
    nrm = lambda k, shape, fan_in: jax.random.normal(k, shape, jnp.float32) * fan_in ** -0.5
    gain = lambda k, shape: 1.0 + 0.02 * jax.random.normal(k, shape, jnp.float32)
    return {
        "x": jax.random.normal(ks[0], (BATCH, SEQ, D_MODEL), jnp.float32),
        "meta": jax.random.normal(ks[1], (N_META, D_MODEL), jnp.float32),
        "ffn1_norm": gain(ks[2], (DEPTH, D_MODEL)),
        "ffn1_w_in": nrm(ks[3], (DEPTH, D_MODEL, 2 * D_FF), D_MODEL),
        "ffn1_w_out": nrm(ks[4], (DEPTH, D_FF, D_MODEL), D_FF),
        "mix_norm": gain(ks[5], (DEPTH, D_MODEL)),
        "ffn2_norm": gain(ks[6], (DEPTH, D_MODEL)),
        "ffn2_w_in": nrm(ks[7], (DEPTH, D_MODEL, 2 * D_FF), D_MODEL),
        "ffn2_w_out": nrm(ks[8], (DEPTH, D_FF, D_MODEL), D_FF),
        "da_w_in": nrm(ks[9], (N_A, D_MODEL, DA_IN_WIDTH), D_MODEL),
        "da_lambda": 0.1 * jax.random.normal(ks[10], (N_A, 4, DA_HEAD_DIM), jnp.float32),
        "da_subln": gain(ks[11], (N_A, DA_V_DIM)),
        "da_w_out": nrm(ks[12], (N_A, DA_HEADS * DA_V_DIM, D_MODEL), DA_HEADS * DA_V_DIM),
        "gla_w_in": nrm(ks[13], (N_B, D_MODEL, GLA_IN_WIDTH), D_MODEL),
        "gla_w_gate2": nrm(ks[14], (N_B, GLA_GATE_RANK, GLA_KEY_WIDTH), GLA_GATE_RANK),
        "gla_b_gate2": 0.1 * jax.random.normal(ks[15], (N_B, GLA_KEY_WIDTH), jnp.float32),
        "gla_norm": gain(ks[16], (N_B, GLA_DV)),
        "gla_w_out": nrm(ks[17], (N_B, GLA_VAL_WIDTH, D_MODEL), GLA_VAL_WIDTH),
        "final_norm": gain(ks[18], (D_MODEL,)),
    }


def reference(x, meta, ffn1_norm, ffn1_w_in, ffn1_w_out, mix_norm, ffn2_norm, ffn2_w_in, ffn2_w_out,
              da_w_in, da_lambda, da_subln, da_w_out,
              gla_w_in, gla_w_gate2, gla_b_gate2, gla_norm, gla_w_out, final_norm):
    B = x.shape[0]
    meta_b = jnp.broadcast_to(meta[None].astype(x.dtype), (B, N_META, D_MODEL))
    h = jnp.concatenate([meta_b, x], axis=1)
    for i in range(DEPTH):
        h = h + 0.5 * swiglu(rmsnorm(h, ffn1_norm[i]), ffn1_w_in[i], ffn1_w_out[i])
        hn = rmsnorm(h, mix_norm[i])
        j = i // N_MIXERS
        if i % N_MIXERS == 0:
            lam_init = 0.8 - 0.6 * math.exp(-0.3 * i)
            h = h + diff_attention(hn, da_w_in[j], da_lambda[j], da_subln[j], da_w_out[j], lam_init)
        else:
            h = h + gla_mixer(hn, gla_w_in[j], gla_w_gate2[j], gla_b_gate2[j], gla_norm[j], gla_w_out[j])
        h = h + 0.5 * swiglu(rmsnorm(h, ffn2_norm[i]), ffn2_w_in[i], ffn2_w_out[i])
    return rmsnorm(h, final_norm)[:, N_META:]
```

```python
import contextlib
import math

import numpy as np
import concourse.bass as bass
import concourse.mybir as mybir
from concourse.bass_utils import run_bass_kernel_spmd
from concourse.alu_op_type import AluOpType as ALU

F32 = mybir.dt.float32
BF16 = mybir.dt.bfloat16
AF = mybir.ActivationFunctionType

ENG_ATTR = {"pe": "tensor", "act": "scalar", "dve": "vector", "pool": "gpsimd", "sp": "sync"}

T = 2176
NT = 17
D = 1024
DC = 8
DFF = 2816
NFC = 22
HALF = 11
NPAD = 112
TG = [(112, 128), (128, 640), (640, 1152), (1152, 1664), (1664, 2176)]
EPS = 1e-6


def tiles_of(c0, c1):
    return range(c0 // 128, (c1 + 127) // 128)


class Prog:
    def __init__(self, nc):
        self.nc = nc
        self.instrs = []
        self.last_writer = {}
        self.readers = {}

    def op(self, eng, fn, reads=(), writes=(), dma=None):
        idx = len(self.instrs)
        deps = {}
        for r in reads:
            w = self.last_writer.get(r)
            if w is not None:
                deps[w] = "raw"
        for wr in writes:
            w = self.last_writer.get(wr)
            if w is not None and w not in deps:
                deps[w] = "waw"
            for rd in self.readers.get(wr, ()):
                if rd not in deps:
                    deps[rd] = "war"
        deps.pop(idx, None)
        for r in reads:
            self.readers.setdefault(r, []).append(idx)
        for wr in writes:
            self.last_writer[wr] = idx
            self.readers[wr] = []
        self.instrs.append(dict(eng=eng, fn=fn, deps=deps, dma=dma, signal=False, cnt=None,
                                semkey=(("dma", dma) if dma is not None else ("eng", eng))))
        return idx

    def emit(self, final_wait_dma=()):
        nc = self.nc
        ins = self.instrs
        need = []
        for i, I in enumerate(ins):
            per = {}
            for d, kind in I["deps"].items():
                Dd = ins[d]
                if Dd["eng"] == I["eng"] and Dd["dma"] is None and I["dma"] is None:
                    if kind != "raw" or I["eng"] == "pe":
                        continue
                k = Dd["semkey"]
                if d > per.get(k, -1):
                    per[k] = d
            need.append(per)
            for d in per.values():
                ins[d]["signal"] = True
        cnt = {}
        for I in ins:
            key = I["semkey"]
            if I["dma"] is not None:
                cnt[key] = cnt.get(key, 0) + 16
            elif I["signal"]:
                cnt[key] = cnt.get(key, 0) + 1
            I["cnt"] = cnt.get(key, 0)
        st = contextlib.ExitStack()
        sems = {}
        for key in cnt:
            sems[key] = st.enter_context(nc.semaphore("s_" + "_".join(map(str, key))))
        per_eng = {e: [] for e in ENG_ATTR}
        for i, I in enumerate(ins):
            per_eng[I["eng"]].append(i)
        with st, nc.Block() as block:
            def make(engname):
                def body(eng):
                    seen = {}
                    for i in per_eng[engname]:
                        I = ins[i]
                        for k, d in need[i].items():
                            v = ins[d]["cnt"]
                            if v > seen.get(k, 0):
                                eng.wait_ge(sems[k], v)
                                seen[k] = v
                        r = I["fn"](eng)
                        if I["dma"] is not None:
                            r.then_inc(sems[I["semkey"]], 16)
                        elif I["signal"]:
                            r.then_inc(sems[I["semkey"]], 1)
                    if engname == "sp":
                        for k, v in cnt.items():
                            if k[0] == "dma" and k[1] in final_wait_dma:
                                eng.wait_ge(sems[k], v)
                return body
            for engname, attr in ENG_ATTR.items():
                if per_eng[engname] or engname == "sp":
                    getattr(block, attr)(make(engname))


IN_SPECS = [
    ("x", [2048, 1024]), ("meta", [16, 1024]),
    ("ffn1_norm", [2, 1024]), ("ffn1_w_in", [2, 1024, 5632]), ("ffn1_w_out", [2, 2816, 1024]),
    ("mix_norm", [2, 1024]), ("ffn2_norm", [2, 1024]), ("ffn2_w_in", [2, 1024, 5632]),
    ("ffn2_w_out", [2, 2816, 1024]),
    ("da_w_in", [1, 1024, 3072]), ("da_lambda", [1, 4, 64]), ("da_subln", [1, 128]),
    ("da_w_out", [1, 1024, 1024]),
    ("gla_w_in", [1, 1024, 3088]), ("gla_w_gate2", [1, 16, 512]), ("gla_b_gate2", [1, 512]),
    ("gla_norm", [1, 256]), ("gla_w_out", [1, 1024, 1024]), ("final_norm", [1, 1024]),
    ("c_ident", [128, 128]), ("c_tri", [128, 128]), ("c_cind", [128, 2]), ("c_valid", [128, 128]),
]


def host_consts():
    ident = np.eye(128, dtype=np.float32)
    s = np.arange(128)[:, None]
    t = np.arange(128)[None, :]
    tri = ((s > t) & ((s // 64) == (t // 64))).astype(np.float32) * (-1.0 / 16.0)
    cind = np.zeros((128, 2), np.float32)
    cind[:64, 0] = -1.0 / 16.0
    cind[64:, 1] = -1.0 / 16.0
    valid = np.zeros((128, 128), np.float32)
    valid[NPAD:, :] = 1.0
    return dict(c_ident=ident, c_tri=tri, c_cind=cind, c_valid=valid)


class Builder:
    def __init__(self, n_sub=6, debug=False):
        self.n_sub = n_sub
        self.debug = debug
        nc = bass.Bass("TRN2", target_bir_lowering=False)
        self.nc = nc
        self.din = {}
        for name, shape in IN_SPECS:
            self.din[name] = nc.dram_tensor(name, shape, F32, kind="ExternalInput")
        if debug:
            self.dout = nc.dram_tensor("out", [T, D], F32, kind="ExternalOutput")
            self.ddbg = nc.dram_tensor("dbg", [128, HALF * T], F32, kind="ExternalOutput")
        else:
            self.dout = nc.dram_tensor("out", [2048, D], F32, kind="ExternalOutput")
        self.P = Prog(nc)
        self.gbi = 0
        self.ring_i = 0
        self.psT_i = 0
        self.mm1_i = 0
        self.mm2_i = 0

    def dram(self, name, off, dims):
        return bass.AP(self.din[name], off, dims)

    def alloc(self, st):
        nc = self.nc
        sb = lambda n, s, d: st.enter_context(nc.sbuf_tensor(n, s, d))
        self.h = sb("h", [128, NT, D], F32)
        self.hnT = sb("hnT", [128, DC, T], BF16)
        self.arB = sb("arB", [128, HALF * T], BF16)
        self.wout = sb("wout", [128, HALF * D], BF16)
        self.ring = sb("ring", [128, 4, 2048], BF16)
        self.gb = sb("gb", [128, 1, D], F32)
        self.hntok = sb("hntok", [128, 2, D], BF16)
        self.sqj = sb("sqj", [128, D], BF16)
        self.stats = sb("stats", [128, 80], F32)
        self.stage = sb("stage", [128, 2, 512], F32)
        self.scr = sb("scr", [128, 3072], BF16)
        self.ident = sb("ident", [128, 128], BF16)
        self.cst = sb("cst", [128, 256], BF16)
        self.psall = st.enter_context(nc.psum_tensor("psall", [128, 8, 512], F32))
        self.ps = [self.psall[:, i, :] for i in range(8)]
        self.actT = self.arB[:, :].rearrange("p (f t) -> p f t", t=T)

    def load_inputs(self):
        P, h = self.P, self.h
        P.op("pool", lambda e: e.dma_start(out=self.ident[:, :], in_=self.din["c_ident"].ap()),
             writes=["ident"], dma="const")
        P.op("dve", lambda e: e.memset(h[:, 0, :], 0.0), writes=[("h", 0)])
        P.op("sp", lambda e: e.dma_start(out=h[NPAD:128, 0, :], in_=self.din["meta"].ap()),
             writes=[("h", 0)], dma="x0")
        xv = self.din["x"].ap().rearrange("(j p) d -> p j d", p=128)
        for j in range(4):
            P.op("sp", lambda e, j=j: e.dma_start(out=h[:, 1 + 4 * j:5 + 4 * j, :], in_=xv[:, 4 * j:4 * j + 4, :]),
                 writes=[("h", 1 + 4 * j + i) for i in range(4)], dma=f"x{j % 2}")

    def load_gain(self, name, l):
        b = 0
        self.P.op("sp", lambda e: e.dma_start(out=self.gb[:, b, :], in_=self.dram(name, l * D, [[0, 128], [1, D]])),
                  writes=[("gb", b)], dma=f"gb{b}")
        return b

    def norm_tile(self, tt, g):
        P, h = self.P, self.h
        ss = self.stats[:, tt:tt + 1]
        sd = self.stats[:, 20 + tt:21 + tt]
        rs = self.stats[:, 40 + tt:41 + tt]
        P.op("act", lambda e: e.activation(out=self.sqj[:, :], in_=h[:, tt, :], func=AF.Square, accum_out=ss),
             reads=[("h", tt)], writes=["sqj", ("ss", tt)])
        P.op("act", lambda e: e.activation(out=sd, in_=ss, func=AF.Sqrt, scale=1.0 / D, bias=self.eps_ap),
             reads=[("ss", tt), "eps"], writes=[("sd", tt)])
        P.op("dve", lambda e: e.reciprocal(out=rs, in_=sd), reads=[("sd", tt)], writes=[("rs", tt)])
        b = tt % 2
        P.op("dve", lambda e: e.scalar_tensor_tensor(out=self.hntok[:, b, :], in0=h[:, tt, :], scalar=rs,
                                                     in1=self.gb[:, g, :], op0=ALU.mult, op1=ALU.mult),
             reads=[("h", tt), ("rs", tt), ("gb", g)], writes=[("hntok", b)])
        pb = 0 if self.psT_i == 0 else 2
        self.psT_i ^= 1
        psT = self.ps[pb].bitcast(BF16)
        for dc in range(DC):
            P.op("pe", lambda e, dc=dc: e.transpose(out=psT[:, dc * 128:(dc + 1) * 128],
                                                    in_=self.hntok[:, b, dc * 128:(dc + 1) * 128],
                                                    identity=self.ident[:, :]),
                 reads=[("hntok", b), "ident"], writes=[("ps", pb)])
        P.op("act", lambda e: e.copy(out=self.hnT[:, :, tt * 128:(tt + 1) * 128],
                                     in_=psT.rearrange("p (c t) -> p c t", c=DC)),
             reads=[("ps", pb)], writes=[("hnT", tt)])

    def ffn(self, l, which, after_tile=None):
        P = self.P
        w_in, w_out = f"ffn{which}_w_in", f"ffn{which}_w_out"
        actT, ring, wout, hnT, ps = self.actT, self.ring, self.wout, self.hnT, self.ps
        woutv = wout[:, :].rearrange("p (f d) -> p f d", d=D)
        self.fence_arB()
        for half in range(2):
            for fcl in range(HALF):
                fc = half * HALF + fcl
                slot = self.ring_i
                self.ring_i = (self.ring_i + 1) % 4
                for part in range(2):
                    src = self.dram(w_in, l * D * 2 * DFF + part * DFF + fc * 128,
                                    [[2 * DFF, 128], [128 * 2 * DFF, DC], [1, 128]])
                    P.op("pool", lambda e, slot=slot, part=part, src=src: e.dma_start(
                        out=ring[:, slot, part * 1024:(part + 1) * 1024].rearrange("p (c f) -> p c f", f=128), in_=src),
                        writes=[("ring", slot, part)], dma=f"ring{slot}")
                src = self.dram(w_out, l * DFF * D + fc * 128 * D, [[D, 128], [1, D]])
                P.op("pool", lambda e, fcl=fcl, src=src: e.dma_start(out=woutv[:, fcl, :], in_=src),
                     writes=[("wout", fcl)], dma=f"wout{fcl % 2}")
                for (c0, c1) in TG:
                    n = c1 - c0
                    tl = list(tiles_of(c0, c1))
                    pg = 0 if self.mm1_i == 0 else 2
                    pu = pg + 1
                    sgb = self.mm1_i
                    self.mm1_i ^= 1
                    for part, pbank in ((0, pg), (1, pu)):
                        for dc in range(DC):
                            P.op("pe", lambda e, dc=dc, part=part, pbank=pbank, slot=slot, c0=c0, c1=c1, n=n: e.matmul(
                                ps[pbank][:, 0:n], lhsT=ring[:, slot, part * 1024 + dc * 128:part * 1024 + (dc + 1) * 128],
                                rhs=hnT[:, dc, c0:c1], start=(dc == 0), stop=(dc == DC - 1)),
                                reads=[("ring", slot, part)] + [("hnT", t) for t in tl], writes=[("ps", pbank)])
                    sg = self.stage[:, sgb, 0:n]
                    P.op("act", lambda e, pg=pg, n=n, sg=sg: e.activation(out=sg, in_=ps[pg][:, 0:n], func=AF.Silu),
                         reads=[("ps", pg)], writes=[("stage", sgb)])
                    P.op("dve", lambda e, pu=pu, n=n, sg=sg, fcl=fcl, c0=c0, c1=c1: e.tensor_tensor(
                        out=actT[:, fcl, c0:c1], in0=sg, in1=ps[pu][:, 0:n], op=ALU.mult),
                        reads=[("stage", sgb), ("ps", pu)], writes=[("actT", fcl, t) for t in tl])
            for tt in range(NT):
                pa = 4 if self.mm2_i == 0 else 6
                self.mm2_i ^= 1
                for dh in range(2):
                    for fcl in range(HALF):
                        P.op("pe", lambda e, dh=dh, fcl=fcl, tt=tt, pa=pa: e.matmul(
                            ps[pa + dh][:, :], lhsT=actT[:, fcl, tt * 128:(tt + 1) * 128],
                            rhs=woutv[:, fcl, dh * 512:(dh + 1) * 512], start=(fcl == 0), stop=(fcl == HALF - 1)),
                            reads=[("actT", fcl, tt), ("wout", fcl)], writes=[("ps", pa + dh)])
                for dh in range(2):
                    hs = self.h[:, tt, dh * 512:(dh + 1) * 512]
                    P.op("dve", lambda e, dh=dh, pa=pa, hs=hs: e.scalar_tensor_tensor(
                        out=hs, in0=ps[pa + dh][:, :], scalar=0.5, in1=hs, op0=ALU.mult, op1=ALU.add),
                        reads=[("ps", pa + dh), ("h", tt)], writes=[("h", tt)])
                if half == 1 and after_tile is not None:
                    after_tile(tt)


    def fence_arB(self, extra_writes=()):
        keys = [("actT", f, 0) for f in range(HALF)] + [("oT", c, 0) for c in range(8)] + ["qT", "kT", "vh"]
        self.P.op("pool", lambda e: e.memset(self.actT[:, :, 0:NPAD], 0.0), writes=keys + list(extra_writes))

    def diff_attn(self, l):
        P, ps, hnT, ring = self.P, self.ps, self.hnT, self.ring
        lam_init = 0.8 - 0.6 * math.exp(-0.3 * l)
        oT = self.arB[:, 0:8 * T].rearrange("p (c t) -> p c t", t=T)
        qT = self.arB[:, 8 * T:9 * T]
        kT = self.arB[:, 9 * T:10 * T]
        vh = self.arB[:, 10 * T:11 * T].rearrange("p (j v) -> p j v", v=128)
        woutv = self.wout[:, :].rearrange("p (f d) -> p f d", d=D)
        pT = self.scr[:, 0:2048].rearrange("p (s m n) -> p s m n", s=2, m=2)
        sqs = self.scr[:, 2048:2560]
        ones = self.cst[:, 0:128]
        valid = self.cst[:, 128:256]
        st = self.stats
        TGA = [(0, 512), (512, 1024), (1024, 1536), (1536, 2048), (2048, 2176)]
        self.fence_arB()
        P.op("pool", lambda e: e.memset(ones, 1.0), writes=["ones"])
        P.op("pool", lambda e: e.dma_start(out=valid, in_=self.din["c_valid"].ap()), writes=["valid"], dma="const")
        lamb = self.sqj.bitcast(F32)[:, 0:256]
        P.op("sp", lambda e: e.dma_start(out=lamb, in_=self.dram("da_lambda", 0, [[0, 128], [1, 256]])),
             writes=["sqj"], dma="misc")
        pr = self.sqj.bitcast(F32)[:, 256:384]
        P.op("dve", lambda e: e.tensor_tensor(out=pr[:, 0:64], in0=lamb[:, 0:64], in1=lamb[:, 64:128], op=ALU.mult),
             reads=["sqj"], writes=["lam_p1"])
        P.op("dve", lambda e: e.tensor_tensor(out=pr[:, 64:128], in0=lamb[:, 128:192], in1=lamb[:, 192:256], op=ALU.mult),
             reads=["sqj"], writes=["lam_p2"])
        P.op("dve", lambda e: e.reduce_sum(out=st[:, 58:60], in_=pr.rearrange("p (a b) -> p a b", a=2),
                                           axis=mybir.AxisListType.X), reads=["lam_p1", "lam_p2"], writes=["lam_s"])
        P.op("act", lambda e: e.activation(out=st[:, 60:62], in_=st[:, 58:60], func=AF.Exp), reads=["lam_s"], writes=["lam_e"])
        P.op("dve", lambda e: e.tensor_tensor(out=st[:, 62:63], in0=st[:, 61:62], in1=st[:, 60:61], op=ALU.subtract),
             reads=["lam_e"], writes=["lam_d"])
        nlam = st[:, 57:58]
        P.op("dve", lambda e: e.tensor_scalar(out=nlam, in0=st[:, 62:63], scalar1=-lam_init, scalar2=None, op0=ALU.add),
             reads=["lam_d"], writes=["nlam"])
        gcol = st[:, 65:66]
        P.op("sp", lambda e: e.dma_start(out=st[:, 64:65], in_=self.dram("da_subln", 0, [[1, 128], [1, 1]])),
             writes=["gcol_raw"], dma="misc")
        P.op("dve", lambda e: e.tensor_scalar(out=gcol, in0=st[:, 64:65], scalar1=1.0 - lam_init, scalar2=None, op0=ALU.mult),
             reads=["gcol_raw"], writes=["gcol"])
        for c in range(8):
            src = self.dram("da_w_out", c * 128 * D, [[D, 128], [1, D]])
            P.op("pool", lambda e, c=c, src=src: e.dma_start(out=woutv[:, c, :], in_=src),
                 writes=[("wout", c)], dma=f"wout{c % 2}")
        WROW = 3072
        pbank = [0]

        def nextbank():
            b = pbank[0]
            pbank[0] = (b + 1) % 4
            return b

        for hd in range(8):
            sa = self.ring_i
            self.ring_i = (self.ring_i + 1) % 4
            sb_ = self.ring_i
            self.ring_i = (self.ring_i + 1) % 4
            for part, col0 in ((0, hd * 128), (1, 1024 + hd * 128)):
                src = self.dram("da_w_in", col0, [[WROW, 128], [128 * WROW, DC], [1, 128]])
                P.op("pool", lambda e, part=part, src=src, sa=sa: e.dma_start(
                    out=ring[:, sa, part * 1024:(part + 1) * 1024].rearrange("p (c f) -> p c f", f=128), in_=src),
                    writes=[("ring", sa, part)], dma=f"ring{sa}")
            src = self.dram("da_w_in", 2048 + hd * 128, [[WROW, 128], [128 * WROW, DC], [1, 128]])
            P.op("pool", lambda e, src=src, sb_=sb_: e.dma_start(
                out=ring[:, sb_, 0:1024].rearrange("p (c f) -> p c f", f=128), in_=src),
                writes=[("ring", sb_, 0)], dma=f"ring{sb_}")
            for part, dst, key in ((1, kT, "kT"), (0, qT, "qT")):
                for (c0, c1) in TGA:
                    n = c1 - c0
                    b = nextbank()
                    for dc in range(DC):
                        P.op("pe", lambda e, dc=dc, part=part, b=b, c0=c0, c1=c1, n=n, sa=sa: e.matmul(
                            ps[b][:, 0:n], lhsT=ring[:, sa, part * 1024 + dc * 128:part * 1024 + (dc + 1) * 128],
                            rhs=hnT[:, dc, c0:c1], start=(dc == 0), stop=(dc == DC - 1)),
                            reads=[("ring", sa, part)] + [("hnT", t) for t in tiles_of(c0, c1)], writes=[("ps", b)])
                    if part == 0:
                        P.op("act", lambda e, b=b, n=n, c0=c0, c1=c1: e.activation(
                            out=qT[:, c0:c1], in_=ps[b][:, 0:n], func=AF.Copy, scale=0.125),
                            reads=[("ps", b)], writes=["qT"])
                    else:
                        P.op("dve", lambda e, b=b, n=n, c0=c0, c1=c1: e.tensor_copy(out=kT[:, c0:c1], in_=ps[b][:, 0:n]),
                             reads=[("ps", b)], writes=["kT"])
            for t0 in range(0, NT, 4):
                tl = list(range(t0, min(t0 + 4, NT)))
                b = nextbank()
                for j, tt in enumerate(tl):
                    for dc in range(DC):
                        P.op("pe", lambda e, dc=dc, b=b, j=j, tt=tt, sb_=sb_: e.matmul(
                            ps[b][:, j * 128:(j + 1) * 128], lhsT=hnT[:, dc, tt * 128:(tt + 1) * 128],
                            rhs=ring[:, sb_, dc * 128:(dc + 1) * 128], start=(dc == 0), stop=(dc == DC - 1)),
                            reads=[("ring", sb_, 0), ("hnT", tt)], writes=[("ps", b)])
                nn = len(tl)
                P.op("dve", lambda e, b=b, t0=t0, nn=nn: e.tensor_copy(
                    out=vh[:, t0:t0 + nn, :], in_=ps[b][:, 0:nn * 128].rearrange("p (j v) -> p j v", v=128)),
                    reads=[("ps", b)], writes=["vh"])
            pairs = []
            for g in range(5):
                q0 = max(g * 512, NPAD)
                q1 = min((g + 1) * 512, T)
                nkb = q1 // 128
                for kb in range(nkb):
                    pairs.append((g, kb, q0, q1, nkb))

            def issue_S(i):
                g, kb, q0, q1, nkb = pairs[i]
                sset = i % 2
                c0 = max(kb * 128, q0)
                n = q1 - c0
                for m in range(2):
                    P.op("pe", lambda e, m=m, sset=sset, kb=kb, c0=c0, q1=q1, n=n: e.matmul(
                        ps[2 * sset + m][:, 0:n], lhsT=kT[m * 64:(m + 1) * 64, kb * 128:(kb + 1) * 128],
                        rhs=qT[m * 64:(m + 1) * 64, c0:q1], start=True, stop=True),
                        reads=["kT", "qT"], writes=[("ps", 2 * sset + m)])
                P.op("act", lambda e, sset=sset, n=n: e.activation(
                    out=pT[:, sset, :, 0:n], in_=self.psall[:, 2 * sset:2 * sset + 2, 0:n], func=AF.Exp),
                    reads=[("ps", 2 * sset), ("ps", 2 * sset + 1)], writes=[("pT", sset)])
                if kb >= 1 and kb * 128 >= q0:
                    P.op("pool", lambda e, sset=sset: e.memset(pT[64:128, sset, :, 0:64], 0.0), writes=[("pT", sset)])

            def issue_PV(i):
                g, kb, q0, q1, nkb = pairs[i]
                sset = i % 2
                c0 = max(kb * 128, q0)
                n = q1 - c0
                off = c0 - q0
                den_l = valid if kb == 0 else ones
                den_key = "valid" if kb == 0 else "ones"
                for m in range(2):
                    P.op("pe", lambda e, m=m, sset=sset, kb=kb, n=n, off=off, nkb=nkb: e.matmul(
                        ps[4 + m][:, off:off + n], lhsT=vh[:, kb, :], rhs=pT[:, sset, m, 0:n],
                        start=(kb == 0), stop=(kb == nkb - 1), skip_group_check=True),
                        reads=["vh", ("pT", sset)], writes=[("ps", 4 + m)])
                for m in range(2):
                    P.op("pe", lambda e, m=m, sset=sset, kb=kb, n=n, off=off, nkb=nkb, den_l=den_l: e.matmul(
                        ps[6 + m][:, off:off + n], lhsT=den_l, rhs=pT[:, sset, m, 0:n],
                        start=(kb == 0), stop=(kb == nkb - 1), skip_group_check=True),
                        reads=[den_key, ("pT", sset)], writes=[("ps", 6 + m)])

            def issue_evac(i):
                g, kb, q0, q1, nkb = pairs[i]
                nq = q1 - q0
                tl = list(tiles_of(q0, q1))
                for m in range(2):
                    P.op("dve", lambda e, m=m, nq=nq: e.reciprocal(out=self.stage[:, m, 0:nq], in_=ps[6 + m][:, 0:nq]),
                         reads=[("ps", 6 + m)], writes=[("stage", m)])
                for m in range(2):
                    P.op("dve", lambda e, m=m, nq=nq: e.tensor_tensor(
                        out=self.stage[:, m, 0:nq], in0=ps[4 + m][:, 0:nq], in1=self.stage[:, m, 0:nq], op=ALU.mult),
                        reads=[("ps", 4 + m), ("stage", m)], writes=[("stage", m)])
                P.op("dve", lambda e, nq=nq, q0=q0, q1=q1, hd=hd: e.scalar_tensor_tensor(
                    out=oT[:, hd, q0:q1], in0=self.stage[:, 1, 0:nq], scalar=nlam, in1=self.stage[:, 0, 0:nq],
                    op0=ALU.mult, op1=ALU.add),
                    reads=[("stage", 0), ("stage", 1), "nlam"], writes=[("oT", hd, t) for t in tl])
                P.op("pool", lambda e, nq=nq, q0=q0, q1=q1, hd=hd: e.tensor_tensor(
                    out=sqs[:, 0:nq], in0=oT[:, hd, q0:q1], in1=oT[:, hd, q0:q1], op=ALU.mult),
                    reads=[("oT", hd, t) for t in tl], writes=["sqs"])
                b = nextbank()
                P.op("pe", lambda e, b=b, nq=nq: e.matmul(ps[b][:, 0:nq], lhsT=ones, rhs=sqs[:, 0:nq], start=True, stop=True),
                     reads=["ones", "sqs"], writes=[("ps", b)])
                P.op("act", lambda e, b=b, nq=nq: e.activation(out=self.stage[:, 0, 0:nq], in_=ps[b][:, 0:nq], func=AF.Sqrt,
                                                               scale=1.0 / 128, bias=self.eps_ap),
                     reads=[("ps", b), "eps"], writes=[("stage", 0)])
                P.op("dve", lambda e, nq=nq: e.reciprocal(out=self.stage[:, 0, 0:nq], in_=self.stage[:, 0, 0:nq]),
                     reads=[("stage", 0)], writes=[("stage", 0)])
                P.op("dve", lambda e, nq=nq, q0=q0, q1=q1, hd=hd: e.scalar_tensor_tensor(
                    out=oT[:, hd, q0:q1], in0=oT[:, hd, q0:q1], scalar=gcol, in1=self.stage[:, 0, 0:nq],
                    op0=ALU.mult, op1=ALU.mult),
                    reads=[("oT", hd, t) for t in tl] + [("stage", 0), "gcol"], writes=[("oT", hd, t) for t in tl])

            issue_S(0)
            for i in range(len(pairs)):
                if i + 1 < len(pairs):
                    issue_S(i + 1)
                issue_PV(i)
                if pairs[i][1] == pairs[i][4] - 1:
                    issue_evac(i)
        for tt in range(NT):
            pa = 4 if self.mm2_i == 0 else 6
            self.mm2_i ^= 1
            for dh in range(2):
                for c in range(8):
                    P.op("pe", lambda e, dh=dh, c=c, tt=tt, pa=pa: e.matmul(
                        ps[pa + dh][:, :], lhsT=oT[:, c, tt * 128:(tt + 1) * 128],
                        rhs=woutv[:, c, dh * 512:(dh + 1) * 512], start=(c == 0), stop=(c == 7)),
                        reads=[("oT", c, tt), ("wout", c)], writes=[("ps", pa + dh)])
            for dh in range(2):
                hs = self.h[:, tt, dh * 512:(dh + 1) * 512]
                P.op("dve", lambda e, dh=dh, pa=pa, hs=hs: e.tensor_tensor(out=hs, in0=ps[pa + dh][:, :], in1=hs, op=ALU.add),
                     reads=[("ps", pa + dh), ("h", tt)], writes=[("h", tt)])


    def gla(self, l):
        P, ps, hnT, ring = self.P, self.ps, self.hnT, self.ring
        oT = self.arB[:, 0:8 * T].rearrange("p (c t) -> p c t", t=T)
        qT = self.arB[:, 8 * T:9 * T]
        kdec = self.arB[:, 9 * T:10 * T].rearrange("p (j v) -> p j v", v=128)
        woutv = self.wout[:, :].rearrange("p (f d) -> p f d", d=D)
        vh = self.wout[:, 0:NT * 256].rearrange("p (j v) -> p j v", v=256)
        VHK = [("wout", f) for f in range(5)]
        S32 = self.wout[:, 5120:6144].bitcast(F32).rearrange("p (s v) -> p s v", s=2)
        Sb = self.wout[:, 6144:6656].rearrange("p (s v) -> p s v", s=2)
        glrT = self.scr[0:32, 0:T]
        wglr = self.scr[:, T:T + 128].rearrange("p (c r) -> p c r", r=16)
        sqs = self.scr[:, 2304:2816]
        wg2 = self.hntok[0:17, 0, 0:512]
        gbf = self.gb[:, 0, :]
        tri = gbf[:, 0:128]
        cind = gbf[:, 128:130]
        decay = gbf[:, 130:130 + 4 * 34].rearrange("p (h n) -> p h n", h=4)
        ones = self.cst[:, 0:128]
        st = self.stats
        one_col = st[:, 69:70]
        gcol = st[:, 66:68]
        stage = self.stage
        WROW = 3088
        TGA = [(0, 512), (512, 1024), (1024, 1536), (1536, 2048), (2048, 2176)]
        GB = ("gb", 0)
        self.fence_arB()
        P.op("pool", lambda e: e.memset(ones, 1.0), writes=["ones"])
        P.op("dve", lambda e: e.memset(one_col, 1.0), writes=["one_col"])
        P.op("dve", lambda e: e.memset(Sb[:, :, :], 0.0), writes=[("Sb", 0), ("Sb", 1), ("wout", 6)])
        P.op("sp", lambda e: e.dma_start(out=tri, in_=self.din["c_tri"].ap()), writes=["tri", GB], dma="misc")
        P.op("sp", lambda e: e.dma_start(out=cind, in_=self.din["c_cind"].ap()), writes=["cind", GB], dma="misc")
        for a in range(2):
            P.op("sp", lambda e, a=a: e.dma_start(out=gcol[:, a:a + 1], in_=self.dram("gla_norm", a * 128, [[1, 128], [1, 1]])),
                 writes=["gcolg"], dma="misc")
        P.op("pool", lambda e: e.memset(glrT, 1.0), writes=["glrT", ("pT", 0), ("pT", 1), "sqs"])
        P.op("pool", lambda e: e.dma_start(out=wglr, in_=self.dram("gla_w_in", 3072, [[WROW, 128], [128 * WROW, DC], [1, 16]])),
             writes=["wglr", ("pT", 0), ("pT", 1), "sqs"], dma="const")
        P.op("pool", lambda e: e.dma_start(out=wg2[0:16, :], in_=self.din["gla_w_gate2"].ap()[0]),
             writes=[("hntok", 0)], dma="const")
        P.op("pool", lambda e: e.dma_start(out=wg2[16:17, :], in_=self.din["gla_b_gate2"].ap()),
             writes=[("hntok", 0)], dma="const")
        pbank = [0]

        def nextbank():
            b = pbank[0]
            pbank[0] = (b + 1) % 4
            return b

        for (c0, c1) in TGA:
            n = c1 - c0
            b = nextbank()
            for dc in range(DC):
                P.op("pe", lambda e, dc=dc, b=b, c0=c0, c1=c1, n=n: e.matmul(
                    ps[b][0:16, 0:n], lhsT=wglr[:, dc, :], rhs=hnT[:, dc, c0:c1], start=(dc == 0), stop=(dc == DC - 1)),
                    reads=["wglr"] + [("hnT", t) for t in tiles_of(c0, c1)], writes=[("ps", b)])
            P.op("dve", lambda e, b=b, n=n, c0=c0, c1=c1: e.tensor_copy(out=glrT[0:16, c0:c1], in_=ps[b][0:16, 0:n]),
                 reads=[("ps", b)], writes=["glrT"])
        for hd in range(4):
            sa = self.ring_i
            self.ring_i = (self.ring_i + 1) % 4
            sb_ = self.ring_i
            self.ring_i = (self.ring_i + 1) % 4
            for part, col0 in ((0, hd * 128), (1, 512 + hd * 128)):
                src = self.dram("gla_w_in", col0, [[WROW, 128], [128 * WROW, DC], [1, 128]])
                P.op("pool", lambda e, part=part, src=src, sa=sa: e.dma_start(
                    out=ring[:, sa, part * 1024:(part + 1) * 1024].rearrange("p (c f) -> p c f", f=128), in_=src),
                    writes=[("ring", sa, part)], dma=f"ring{sa}")
            src = self.dram("gla_w_in", 1024 + hd * 256, [[WROW, 128], [128 * WROW, DC], [1, 256]])
            P.op("pool", lambda e, src=src, sb_=sb_: e.dma_start(
                out=ring[:, sb_, :].rearrange("p (c f) -> p c f", f=256), in_=src),
                writes=[("ring", sb_, 0), ("ring", sb_, 1)], dma=f"ring{sb_}")
            for (c0, c1) in TGA:
                n = c1 - c0
                b = nextbank()
                for dc in range(DC):
                    P.op("pe", lambda e, dc=dc, b=b, c0=c0, c1=c1, n=n, sa=sa: e.matmul(
                        ps[b][:, 0:n], lhsT=ring[:, sa, dc * 128:(dc + 1) * 128], rhs=hnT[:, dc, c0:c1],
                        start=(dc == 0), stop=(dc == DC - 1)),
                        reads=[("ring", sa, 0)] + [("hnT", t) for t in tiles_of(c0, c1)], writes=[("ps", b)])
                P.op("act", lambda e, b=b, n=n, c0=c0, c1=c1: e.activation(
                    out=qT[:, c0:c1], in_=ps[b][:, 0:n], func=AF.Copy, scale=128 ** -0.5),
                    reads=[("ps", b)], writes=["qT"])
            for tt in range(NT):
                b = nextbank()
                par = tt % 2
                for dc in range(DC):
                    P.op("pe", lambda e, dc=dc, b=b, tt=tt, sa=sa: e.matmul(
                        ps[b][:, 0:128], lhsT=hnT[:, dc, tt * 128:(tt + 1) * 128],
                        rhs=ring[:, sa, 1024 + dc * 128:1024 + (dc + 1) * 128], start=(dc == 0), stop=(dc == DC - 1)),
                        reads=[("ring", sa, 1), ("hnT", tt)], writes=[("ps", b)])
                for dc in range(DC):
                    P.op("pe", lambda e, dc=dc, b=b, tt=tt, sb_=sb_: e.matmul(
                        ps[b][:, 128:384], lhsT=hnT[:, dc, tt * 128:(tt + 1) * 128],
                        rhs=ring[:, sb_, dc * 256:(dc + 1) * 256], start=(dc == 0), stop=(dc == DC - 1)),
                        reads=[("ring", sb_, 0), ("ring", sb_, 1), ("hnT", tt)], writes=[("ps", b)])
                P.op("pe", lambda e, b=b, tt=tt, hd=hd: e.matmul(
                    ps[b][:, 384:512], lhsT=glrT[0:17, tt * 128:(tt + 1) * 128], rhs=wg2[:, hd * 128:(hd + 1) * 128],
                    start=True, stop=True), reads=["glrT", ("hntok", 0)], writes=[("ps", b)])
                sp_ = stage[:, par, 0:128]
                eR = stage[:, par, 128:256]
                P.op("act", lambda e, b=b, sp_=sp_: e.activation(out=sp_, in_=ps[b][:, 384:512], func=AF.Exp, scale=-1.0),
                     reads=[("ps", b)], writes=[("stage", par)])
                P.op("act", lambda e, sp_=sp_: e.activation(out=sp_, in_=sp_, func=AF.Ln, bias=one_col),
                     reads=[("stage", par), "one_col"], writes=[("stage", par)])
                b2 = nextbank()
                P.op("pe", lambda e, b2=b2, sp_=sp_: e.matmul(ps[b2][:, 0:128], lhsT=tri, rhs=sp_, start=True, stop=True),
                     reads=["tri", GB, ("stage", par)], writes=[("ps", b2)])
                P.op("pe", lambda e, b2=b2, sp_=sp_: e.matmul(ps[b2][:, 128:130], lhsT=sp_, rhs=cind, start=True, stop=True),
                     reads=["cind", GB, ("stage", par)], writes=[("ps", b2)])
                P.op("act", lambda e, b2=b2, eR=eR: e.activation(out=eR, in_=ps[b2][:, 0:128], func=AF.Exp),
                     reads=[("ps", b2)], writes=[("stage_e", par)])
                P.op("act", lambda e, b2=b2, hd=hd, tt=tt: e.activation(
                    out=decay[:, hd, 2 * tt:2 * tt + 2], in_=ps[b2][:, 128:130], func=AF.Exp),
                    reads=[("ps", b2)], writes=["decay"])
                P.op("dve", lambda e, b=b, eR=eR, tt=tt: e.tensor_tensor(out=kdec[:, tt, :], in0=ps[b][:, 0:128], in1=eR, op=ALU.mult),
                     reads=[("ps", b), ("stage_e", par)], writes=["kT"])
                P.op("dve", lambda e, b=b, tt=tt: e.tensor_copy(out=vh[:, tt, :], in_=ps[b][:, 128:384]),
                     reads=[("ps", b)], writes=VHK)
            P.op("dve", lambda e: e.memset(S32[:, 1, :], 0.0), writes=[("S32", 1), ("wout", 5)])
            for n_ in range(34):
                tt, hf = n_ // 2, n_ % 2
                cur, prev = n_ % 2, 1 - (n_ % 2)
                kb = 4 + (n_ % 2)
                P.op("pe", lambda e, tt=tt, hf=hf, kb=kb: e.matmul(
                    ps[kb][:, 0:256], lhsT=kdec[hf * 64:(hf + 1) * 64, tt, :], rhs=vh[hf * 64:(hf + 1) * 64, tt, :],
                    start=True, stop=True), reads=["kT"] + VHK, writes=[("ps", kb)])
                P.op("dve", lambda e, cur=cur, prev=prev, kb=kb, hd=hd, n_=n_: e.scalar_tensor_tensor(
                    out=S32[:, cur, :], in0=S32[:, prev, :], scalar=decay[:, hd, n_:n_ + 1], in1=ps[kb][:, 0:256],
                    op0=ALU.mult, op1=ALU.add),
                    reads=[("S32", prev), "decay", GB, ("ps", kb)], writes=[("S32", cur)])
                P.op("act", lambda e, cur=cur: e.copy(out=Sb[:, cur, :], in_=S32[:, cur, :]),
                     reads=[("S32", cur)], writes=[("Sb", cur)])
                grp = n_ // 8
                ob = 6 if grp % 2 == 0 else 2
                for a in range(2):
                    P.op("pe", lambda e, a=a, cur=cur, n_=n_, ob=ob: e.matmul(
                        ps[ob + a][:, (n_ % 8) * 64:(n_ % 8) * 64 + 64], lhsT=Sb[:, cur, a * 128:(a + 1) * 128],
                        rhs=qT[:, n_ * 64:(n_ + 1) * 64], start=True, stop=True),
                        reads=[("Sb", cur), "qT"], writes=[("ps", ob + a)])
                if n_ % 8 == 7 or n_ == 33:
                    c0 = grp * 512
                    c1 = (n_ + 1) * 64
                    n = c1 - c0
                    for a in range(2):
                        P.op("dve" if a == 0 else "act", (lambda e, a=a, ob=ob, c0=c0, c1=c1, n=n, hd=hd:
                             (e.tensor_copy(out=oT[:, hd * 2 + a, c0:c1], in_=ps[ob + a][:, 0:n]) if a == 0 else
                              e.copy(out=oT[:, hd * 2 + a, c0:c1], in_=ps[ob + a][:, 0:n]))),
                             reads=[("ps", ob + a)], writes=[("oT", hd * 2 + a, t) for t in tiles_of(c0, c1)])
        extra = [("S32", 0), ("S32", 1), ("Sb", 0), ("Sb", 1)]
        for c in range(8):
            src = self.dram("gla_w_out", c * 128 * D, [[D, 128], [1, D]])
            P.op("pool", lambda e, c=c, src=src: e.dma_start(out=woutv[:, c, :], in_=src),
                 writes=[("wout", c)] + (extra if c in (5, 6) else []), dma=f"wout{c % 2}")
        for hd in range(4):
            for (c0, c1) in TGA:
                n = c1 - c0
                tl = list(tiles_of(c0, c1))
                b = nextbank()
                for a in range(2):
                    c = hd * 2 + a
                    P.op("pool", lambda e, c=c, c0=c0, c1=c1, n=n: e.tensor_tensor(
                        out=sqs[:, 0:n], in0=oT[:, c, c0:c1], in1=oT[:, c, c0:c1], op=ALU.mult),
                        reads=[("oT", c, t) for t in tl], writes=["sqs"])
                    P.op("pe", lambda e, a=a, b=b, n=n: e.matmul(ps[b][:, 0:n], lhsT=ones, rhs=sqs[:, 0:n],
                                                                start=(a == 0), stop=(a == 1)),
                         reads=["ones", "sqs"], writes=[("ps", b)])
                P.op("act", lambda e, b=b, n=n: e.activation(out=stage[:, 0, 0:n], in_=ps[b][:, 0:n], func=AF.Sqrt,
                                                             scale=1.0 / 256, bias=self.eps_ap),
                     reads=[("ps", b), "eps"], writes=[("stage", 0)])
                P.op("dve", lambda e, n=n: e.reciprocal(out=stage[:, 0, 0:n], in_=stage[:, 0, 0:n]),
                     reads=[("stage", 0)], writes=[("stage", 0)])
                for a in range(2):
                    c = hd * 2 + a
                    P.op("dve", lambda e, a=a, c=c, c0=c0, c1=c1, n=n: e.scalar_tensor_tensor(
                        out=oT[:, c, c0:c1], in0=oT[:, c, c0:c1], scalar=gcol[:, a:a + 1], in1=stage[:, 0, 0:n],
                        op0=ALU.mult, op1=ALU.mult),
                        reads=[("oT", c, t) for t in tl] + [("stage", 0), "gcolg"], writes=[("oT", c, t) for t in tl])
        for c in range(8):
            if c % 2 == 0:
                sg_ = self.ring_i
                self.ring_i = (self.ring_i + 1) % 4
            part = c % 2
            src = self.dram("gla_w_in", 2048 + c * 128, [[WROW, 128], [128 * WROW, DC], [1, 128]])
            P.op("pool", lambda e, part=part, src=src, sg_=sg_: e.dma_start(
                out=ring[:, sg_, part * 1024:(part + 1) * 1024].rearrange("p (c f) -> p c f", f=128), in_=src),
                writes=[("ring", sg_, part)], dma=f"ring{sg_}")
            for gi, (c0, c1) in enumerate(TGA):
                n = c1 - c0
                tl = list(tiles_of(c0, c1))
                b = nextbank()
                for dc in range(DC):
                    P.op("pe", lambda e, dc=dc, b=b, c0=c0, c1=c1, n=n, sg_=sg_, part=part: e.matmul(
                        ps[b][:, 0:n], lhsT=ring[:, sg_, part * 1024 + dc * 128:part * 1024 + (dc + 1) * 128],
                        rhs=hnT[:, dc, c0:c1], start=(dc == 0), stop=(dc == DC - 1)),
                        reads=[("ring", sg_, part)] + [("hnT", t) for t in tl], writes=[("ps", b)])
                sgb = gi % 2
                P.op("act", lambda e, b=b, n=n, sgb=sgb: e.activation(out=stage[:, sgb, 0:n], in_=ps[b][:, 0:n], func=AF.Silu),
                     reads=[("ps", b)], writes=[("stage", sgb)])
                P.op("pool", lambda e, c=c, c0=c0, c1=c1, n=n, sgb=sgb: e.tensor_tensor(
                    out=oT[:, c, c0:c1], in0=oT[:, c, c0:c1], in1=stage[:, sgb, 0:n], op=ALU.mult),
                    reads=[("oT", c, t) for t in tl] + [("stage", sgb)], writes=[("oT", c, t) for t in tl])
        for tt in range(NT):
            pa = 4 if self.mm2_i == 0 else 6
            self.mm2_i ^= 1
            for dh in range(2):
                for c in range(8):
                    P.op("pe", lambda e, dh=dh, c=c, tt=tt, pa=pa: e.matmul(
                        ps[pa + dh][:, :], lhsT=oT[:, c, tt * 128:(tt + 1) * 128],
                        rhs=woutv[:, c, dh * 512:(dh + 1) * 512], start=(c == 0), stop=(c == 7)),
                        reads=[("oT", c, tt), ("wout", c)], writes=[("ps", pa + dh)])
            for dh in range(2):
                hs = self.h[:, tt, dh * 512:(dh + 1) * 512]
                P.op("dve", lambda e, dh=dh, pa=pa, hs=hs: e.tensor_tensor(out=hs, in0=ps[pa + dh][:, :], in1=hs, op=ALU.add),
                     reads=[("ps", pa + dh), ("h", tt)], writes=[("h", tt)])

    def final(self, tt, g):
        P, h = self.P, self.h
        ss = self.stats[:, tt:tt + 1]
        sd = self.stats[:, 20 + tt:21 + tt]
        rs = self.stats[:, 40 + tt:41 + tt]
        P.op("act", lambda e: e.activation(out=self.sqj[:, :], in_=h[:, tt, :], func=AF.Square, accum_out=ss),
             reads=[("h", tt)], writes=["sqj", ("ss", tt)])
        P.op("act", lambda e: e.activation(out=sd, in_=ss, func=AF.Sqrt, scale=1.0 / D, bias=self.eps_ap),
             reads=[("ss", tt), "eps"], writes=[("sd", tt)])
        P.op("dve", lambda e: e.reciprocal(out=rs, in_=sd), reads=[("sd", tt)], writes=[("rs", tt)])
        b = tt % 2
        P.op("dve", lambda e: e.scalar_tensor_tensor(out=h[:, tt, :], in0=h[:, tt, :], scalar=rs,
                                                     in1=self.gb[:, g, :], op0=ALU.mult, op1=ALU.mult),
             reads=[("h", tt), ("rs", tt), ("gb", g)], writes=[("h", tt)])
        P.op("sp", lambda e: e.dma_start(out=self.dout.ap()[(tt - 1) * 128:tt * 128, :], in_=h[:, tt, :]),
             reads=[("h", tt)], dma="out")

    def dump_h(self):
        hv = self.dout.ap().rearrange("(j p) d -> p j d", p=128)
        self.P.op("sp", lambda e: e.dma_start(out=hv, in_=self.h[:, :, :]),
                  reads=[("h", t) for t in range(NT)], dma="out")
        keys = [("actT", f, t) for f in range(HALF) for t in range(NT)] + [("oT", c, t) for c in range(8) for t in range(NT)] + ["qT", "kT", "vh"]
        if getattr(self, "dbg_src", "arB") == "hnT":
            for f in range(DC):
                self.P.op("pool", lambda e, f=f: e.dma_start(out=self.ddbg.ap()[:, f * T:(f + 1) * T], in_=self.hnT[:, f, :]),
                          reads=[("hnT", t) for t in range(NT)], dma="out")
            self.P.op("pool", lambda e: e.dma_start(out=self.ddbg.ap()[:, 8 * T:8 * T + 80], in_=self.stats[:, 0:80]),
                      reads=[("rs", t) for t in range(NT)], dma="out")
            self.P.op("pool", lambda e: e.dma_start(out=self.ddbg.ap()[:, 9 * T:9 * T + 1024], in_=self.gb[:, 0, :]),
                      reads=[("gb", 0)], dma="out")
            return
        for f in range(HALF):
            self.P.op("pool", lambda e, f=f: e.dma_start(out=self.ddbg.ap()[:, f * T:(f + 1) * T], in_=self.arB[:, f * T:(f + 1) * T]),
                      reads=keys, dma="out")

    def build(self):
        nc = self.nc
        st = contextlib.ExitStack()
        with st:
            self.alloc(st)
            P = self.P
            self.eps_ap = self.stats[:, 63:64]
            P.op("dve", lambda e: e.memset(self.eps_ap, EPS), writes=["eps"])
            self.load_inputs()
            subs = [("ffn", 0, 1), ("da", 0, 0), ("ffn", 0, 2), ("ffn", 1, 1), ("gla", 1, 0), ("ffn", 1, 2)]
            norm_name = {("ffn", 1): "ffn1_norm", ("ffn", 2): "ffn2_norm", ("da", 0): "mix_norm", ("gla", 0): "mix_norm"}
            subs = subs[:self.n_sub]
            for si, (kind, l, which) in enumerate(subs):
                g = self.load_gain(norm_name[(kind, which)], l)
                for tt in range(NT):
                    self.norm_tile(tt, g)
                if getattr(self, "skip_last_body", False) and si == len(subs) - 1:
                    self.dbg_src = "hnT"
                    break
                if kind == "ffn":
                    self.ffn(l, which)
                elif kind == "da":
                    self.diff_attn(l)
                else:
                    self.gla(l)
            if self.debug:
                self.dump_h()
            else:
                g = self.load_gain("final_norm", 0)
                for tt in range(1, NT):
                    self.final(tt, g)
            P.emit(final_wait_dma=("out",))
        return nc


_CACHE = {}


def kernel(**inputs):
    n = 8
    if "nc" not in _CACHE:
        _CACHE["nc"] = Builder().build()
    nc = _CACHE["nc"]
    consts = host_consts()
    shared = {}
    for name, shape in IN_SPECS:
        if name == "x" or name.startswith("c_"):
            continue
        shared[name] = np.ascontiguousarray(np.asarray(inputs[name], dtype=np.float32).reshape(shape))
    x = np.asarray(inputs["x"], dtype=np.float32)
    in_maps = []
    for c in range(n):
        m = dict(shared)
        m.update(consts)
        m["x"] = np.ascontiguousarray(x[c])
        in_maps.append(m)
    res = run_bass_kernel_spmd(nc, in_maps, core_ids=list(range(n)))
    return np.stack([np.asarray(r["out"], dtype=np.float32) for r in res.results], axis=0)
```

```python
import contextlib
import math

import numpy as np
import concourse.bass as bass
import concourse.mybir as mybir
from concourse.bass_utils import run_bass_kernel_spmd
from concourse.alu_op_type import AluOpType as ALU

F32 = mybir.dt.float32
BF16 = mybir.dt.bfloat16
AF = mybir.ActivationFunctionType

ENG_ATTR = {"pe": "tensor", "act": "scalar", "dve": "vector", "pool": "gpsimd", "sp": "sync"}

T = 2176
NT = 17
D = 1024
DC = 8
DFF = 2816
NFC = 22
HALF = 11
NPAD = 112
TG = [(112, 128), (128, 640), (640, 1152), (1152, 1664), (1664, 2176)]
EPS = 1e-6


def tiles_of(c0, c1):
    return range(c0 // 128, (c1 + 127) // 128)


class Prog:
    def __init__(self, nc):
        self.nc = nc
        self.instrs = []
        self.last_writer = {}
        self.readers = {}

    def op(self, eng, fn, reads=(), writes=(), dma=None, noembed=False):
        idx = len(self.instrs)
        deps = {}
        for r in reads:
            w = self.last_writer.get(r)
            if w is not None:
                deps[w] = "raw"
        for wr in writes:
            w = self.last_writer.get(wr)
            if w is not None and w not in deps:
                deps[w] = "waw"
            for rd in self.readers.get(wr, ()):
                if rd not in deps:
                    deps[rd] = "war"
        deps.pop(idx, None)
        for r in reads:
            self.readers.setdefault(r, []).append(idx)
        for wr in writes:
            self.last_writer[wr] = idx
            self.readers[wr] = []
        self.instrs.append(dict(eng=eng, fn=fn, deps=deps, dma=dma, signal=False, cnt=None, noembed=noembed,
                                semkey=(("dma", dma) if dma is not None else ("eng", eng))))
        return idx

    def emit(self, final_wait_dma="out"):
        nc = self.nc
        ins = self.instrs
        need = []
        for i, I in enumerate(ins):
            per = {}
            for d, kind in I["deps"].items():
                Dd = ins[d]
                if Dd["eng"] == I["eng"] and Dd["dma"] is None and I["dma"] is None:
                    if kind != "raw" or I["eng"] == "pe":
                        continue
                k = Dd["semkey"]
                if d > per.get(k, -1):
                    per[k] = d
            need.append(per)
            for d in per.values():
                ins[d]["signal"] = True
        cnt = {}
        for I in ins:
            key = I["semkey"]
            if I["dma"] is not None:
                cnt[key] = cnt.get(key, 0) + 16
            elif I["signal"]:
                cnt[key] = cnt.get(key, 0) + 1
            I["cnt"] = cnt.get(key, 0)
        st = contextlib.ExitStack()
        sems = {}
        for key in cnt:
            sems[key] = st.enter_context(nc.semaphore("s_" + "_".join(map(str, key))))
        per_eng = {e: [] for e in ENG_ATTR}
        for i, I in enumerate(ins):
            per_eng[I["eng"]].append(i)
        with st, nc.Block() as block:
            def make(engname):
                def body(eng):
                    seen = {}
                    for i in per_eng[engname]:
                        I = ins[i]
                        waits = {}
                        for k, d in need[i].items():
                            v = ins[d]["cnt"]
                            if v > seen.get(k, 0) and v > waits.get(k, (0, 0))[0]:
                                waits[k] = (v, d)
                        if I["dma"] is not None:
                            prev = I["cnt"] - 16
                            k = I["semkey"]
                            if prev > seen.get(k, 0) and prev > waits.get(k, (0, 0))[0]:
                                waits[k] = (prev, -1)
                        embed = None
                        if waits and I["dma"] is None and not I["noembed"]:
                            embed = max(waits, key=lambda k: waits[k][1])
                        for k, (v, d) in waits.items():
                            if k != embed:
                                eng.wait_ge(sems[k], v)
                            seen[k] = v
                        r = I["fn"](eng)
                        if embed is not None:
                            r = r._wait_ge(sems[embed], waits[embed][0])
                        if I["dma"] is not None:
                            r.then_inc(sems[I["semkey"]], 16)
                        elif I["signal"]:
                            r.then_inc(sems[I["semkey"]], 1)
                    if engname == "sp":
                        for k, v in cnt.items():
                            if k[0] == "dma" and k[1].startswith(final_wait_dma):
                                eng.wait_ge(sems[k], v)
                return body
            for engname, attr in ENG_ATTR.items():
                if per_eng[engname] or engname == "sp":
                    getattr(block, attr)(make(engname))


IN_SPECS = [
    ("x", [2048, 1024]), ("meta", [16, 1024]),
    ("ffn1_norm", [2, 1024]), ("ffn1_w_in", [2, 1024, 5632]), ("ffn1_w_out", [2, 2816, 1024]),
    ("mix_norm", [2, 1024]), ("ffn2_norm", [2, 1024]), ("ffn2_w_in", [2, 1024, 5632]),
    ("ffn2_w_out", [2, 2816, 1024]),
    ("da_w_in", [1, 1024, 3072]), ("da_lambda", [1, 4, 64]), ("da_subln", [1, 128]),
    ("da_w_out", [1, 1024, 1024]),
    ("gla_w_in", [1, 1024, 3088]), ("gla_w_gate2", [1, 16, 512]), ("gla_b_gate2", [1, 512]),
    ("gla_norm", [1, 256]), ("gla_w_out", [1, 1024, 1024]), ("final_norm", [1, 1024]),
    ("c_ident", [128, 128]), ("c_tri", [128, 128]), ("c_cind", [128, 2]), ("c_valid", [128, 128]),
]


def host_consts():
    ident = np.eye(128, dtype=np.float32)
    s = np.arange(128)[:, None]
    t = np.arange(128)[None, :]
    tri = ((s > t) & ((s // 64) == (t // 64))).astype(np.float32) * (-1.0 / 16.0)
    cind = np.zeros((128, 2), np.float32)
    cind[:64, 0] = -1.0 / 16.0
    cind[64:, 1] = -1.0 / 16.0
    valid = np.zeros((128, 128), np.float32)
    valid[NPAD:, :] = 1.0
    return dict(c_ident=ident, c_tri=tri, c_cind=cind, c_valid=valid)


class Builder:
    def __init__(self, n_sub=6, debug=False):
        self.n_sub = n_sub
        self.debug = debug
        nc = bass.Bass("TRN2", target_bir_lowering=False)
        self.nc = nc
        self.din = {}
        for name, shape in IN_SPECS:
            self.din[name] = nc.dram_tensor(name, shape, F32, kind="ExternalInput")
        if debug:
            self.dout = nc.dram_tensor("out", [T, D], F32, kind="ExternalOutput")
            self.ddbg = nc.dram_tensor("dbg", [128, HALF * T], F32, kind="ExternalOutput")
        else:
            self.dout = nc.dram_tensor("out", [2048, D], F32, kind="ExternalOutput")
        self.P = Prog(nc)
        self.gbi = 0
        self.ring_i = 0
        self.psT_i = 0
        self.mm1_i = 0
        self.mm2_i = 0

    def dram(self, name, off, dims):
        return bass.AP(self.din[name], off, dims)

    def alloc(self, st):
        nc = self.nc
        sb = lambda n, s, d: st.enter_context(nc.sbuf_tensor(n, s, d))
        self.h = sb("h", [128, NT, D], F32)
        self.hnT = sb("hnT", [128, DC, T], BF16)
        self.arB = sb("arB", [128, HALF * T], BF16)
        self.wout = sb("wout", [128, HALF * D], BF16)
        self.ring = sb("ring", [128, 4, 2048], BF16)
        self.gb = sb("gb", [128, 1, D], F32)
        self.hntok = sb("hntok", [128, 2, D], BF16)
        self.sqj = sb("sqj", [128, D], BF16)
        self.stats = sb("stats", [128, 80], F32)
        self.stage = sb("stage", [128, 2, 512], F32)
        self.scr = sb("scr", [128, 3072], BF16)
        self.ident = sb("ident", [128, 128], BF16)
        self.cst = sb("cst", [128, 256], BF16)
        self.psall = st.enter_context(nc.psum_tensor("psall", [128, 8, 512], F32))
        self.ps = [self.psall[:, i, :] for i in range(8)]
        self.actT = self.arB[:, :].rearrange("p (f t) -> p f t", t=T)

    def load_inputs(self):
        P, h = self.P, self.h
        P.op("pool", lambda e: e.dma_start(out=self.ident[:, :], in_=self.din["c_ident"].ap()),
             writes=["ident"], dma="const_1")
        P.op("dve", lambda e: e.memset(h[:, 0, :], 0.0), writes=[("h", 0)])
        P.op("sp", lambda e: e.dma_start(out=h[NPAD:128, 0, :], in_=self.din["meta"].ap()),
             writes=[("h", 0)], dma="x0")
        xv = self.din["x"].ap().rearrange("(j p) d -> p j d", p=128)
        for j in range(4):
            P.op("sp", lambda e, j=j: e.dma_start(out=h[:, 1 + 4 * j:5 + 4 * j, :], in_=xv[:, 4 * j:4 * j + 4, :]),
                 writes=[("h", 1 + 4 * j + i) for i in range(4)], dma=f"x{j + 1}")

    def load_gain(self, name, l):
        b = 0
        self.P.op("sp", lambda e: e.dma_start(out=self.gb[:, b, :], in_=self.dram(name, l * D, [[0, 128], [1, D]])),
                  writes=[("gb", b)], dma=f"gb{b}")
        return b

    def norm_tile(self, tt, g):
        P, h = self.P, self.h
        ss = self.stats[:, tt:tt + 1]
        sd = self.stats[:, 20 + tt:21 + tt]
        rs = self.stats[:, 40 + tt:41 + tt]
        P.op("act", lambda e: e.activation(out=self.sqj[:, :], in_=h[:, tt, :], func=AF.Square, accum_out=ss),
             reads=[("h", tt)], writes=["sqj", ("ss", tt)], noembed=True)
        P.op("act", lambda e: e.activation(out=sd, in_=ss, func=AF.Sqrt, scale=1.0 / D, bias=self.eps_ap),
             reads=[("ss", tt), "eps"], writes=[("sd", tt)])
        P.op("dve", lambda e: e.reciprocal(out=rs, in_=sd), reads=[("sd", tt)], writes=[("rs", tt)])
        b = tt % 2
        P.op("dve", lambda e: e.scalar_tensor_tensor(out=self.hntok[:, b, :], in0=h[:, tt, :], scalar=rs,
                                                     in1=self.gb[:, g, :], op0=ALU.mult, op1=ALU.mult),
             reads=[("h", tt), ("rs", tt), ("gb", g)], writes=[("hntok", b)])
        pb = 0 if self.psT_i == 0 else 2
        self.psT_i ^= 1
        psT = self.ps[pb].bitcast(BF16)
        for dc in range(DC):
            P.op("pe", lambda e, dc=dc: e.transpose(out=psT[:, dc * 128:(dc + 1) * 128],
                                                    in_=self.hntok[:, b, dc * 128:(dc + 1) * 128],
                                                    identity=self.ident[:, :]),
                 reads=[("hntok", b), "ident"], writes=[("ps", pb)])
        P.op("act", lambda e: e.copy(out=self.hnT[:, :, tt * 128:(tt + 1) * 128],
                                     in_=psT.rearrange("p (c t) -> p c t", c=DC)),
             reads=[("ps", pb)], writes=[("hnT", tt)])

    def ffn(self, l, which, after_tile=None):
        P = self.P
        w_in, w_out = f"ffn{which}_w_in", f"ffn{which}_w_out"
        actT, ring, wout, hnT, ps = self.actT, self.ring, self.wout, self.hnT, self.ps
        woutv = wout[:, :].rearrange("p (f d) -> p f d", d=D)
        self.fence_arB()
        for half in range(2):
            for fcl in range(HALF):
                fc = half * HALF + fcl
                slot = self.ring_i
                self.ring_i = (self.ring_i + 1) % 4
                for part in range(2):
                    src = self.dram(w_in, l * D * 2 * DFF + part * DFF + fc * 128,
                                    [[2 * DFF, 128], [128 * 2 * DFF, DC], [1, 128]])
                    P.op("pool", lambda e, slot=slot, part=part, src=src: e.dma_start(
                        out=ring[:, slot, part * 1024:(part + 1) * 1024].rearrange("p (c f) -> p c f", f=128), in_=src),
                        writes=[("ring", slot, part)], dma=f"ring{slot}p{part}")
                src = self.dram(w_out, l * DFF * D + fc * 128 * D, [[D, 128], [1, D]])
                P.op("pool", lambda e, fcl=fcl, src=src: e.dma_start(out=woutv[:, fcl, :], in_=src),
                     writes=[("wout", fcl)], dma=f"wout{fcl}")
                for (c0, c1) in TG:
                    n = c1 - c0
                    tl = list(tiles_of(c0, c1))
                    pg = 0 if self.mm1_i == 0 else 2
                    pu = pg + 1
                    sgb = self.mm1_i
                    self.mm1_i ^= 1
                    for part, pbank in ((0, pg), (1, pu)):
                        for dc in range(DC):
                            P.op("pe", lambda e, dc=dc, part=part, pbank=pbank, slot=slot, c0=c0, c1=c1, n=n: e.matmul(
                                ps[pbank][:, 0:n], lhsT=ring[:, slot, part * 1024 + dc * 128:part * 1024 + (dc + 1) * 128],
                                rhs=hnT[:, dc, c0:c1], start=(dc == 0), stop=(dc == DC - 1)),
                                reads=[("ring", slot, part)] + [("hnT", t) for t in tl], writes=[("ps", pbank)])
                    sg = self.stage[:, sgb, 0:n]
                    P.op("act", lambda e, pg=pg, n=n, sg=sg: e.activation(out=sg, in_=ps[pg][:, 0:n], func=AF.Silu),
                         reads=[("ps", pg)], writes=[("stage", sgb)])
                    P.op("dve", lambda e, pu=pu, n=n, sg=sg, fcl=fcl, c0=c0, c1=c1: e.tensor_tensor(
                        out=actT[:, fcl, c0:c1], in0=sg, in1=ps[pu][:, 0:n], op=ALU.mult),
                        reads=[("stage", sgb), ("ps", pu)], writes=[("actT", fcl, t) for t in tl])
            for tt in range(NT):
                pa = 4 if self.mm2_i == 0 else 6
                self.mm2_i ^= 1
                for dh in range(2):
                    for fcl in range(HALF):
                        P.op("pe", lambda e, dh=dh, fcl=fcl, tt=tt, pa=pa: e.matmul(
                            ps[pa + dh][:, :], lhsT=actT[:, fcl, tt * 128:(tt + 1) * 128],
                            rhs=woutv[:, fcl, dh * 512:(dh + 1) * 512], start=(fcl == 0), stop=(fcl == HALF - 1)),
                            reads=[("actT", fcl, tt), ("wout", fcl)], writes=[("ps", pa + dh)])
                for dh in range(2):
                    hs = self.h[:, tt, dh * 512:(dh + 1) * 512]
                    P.op("dve", lambda e, dh=dh, pa=pa, hs=hs: e.scalar_tensor_tensor(
                        out=hs, in0=ps[pa + dh][:, :], scalar=0.5, in1=hs, op0=ALU.mult, op1=ALU.add),
                        reads=[("ps", pa + dh), ("h", tt)], writes=[("h", tt)])
                if half == 1 and after_tile is not None:
                    after_tile(tt)


    def fence_arB(self, extra_writes=()):
        keys = [("actT", f, 0) for f in range(HALF)] + [("oT", c, 0) for c in range(8)] + ["qT", "kT", "vh"]
        self.P.op("pool", lambda e: e.memset(self.actT[:, :, 0:NPAD], 0.0), writes=keys + list(extra_writes))

    def diff_attn(self, l):
        P, ps, hnT, ring = self.P, self.ps, self.hnT, self.ring
        lam_init = 0.8 - 0.6 * math.exp(-0.3 * l)
        oT = self.arB[:, 0:8 * T].rearrange("p (c t) -> p c t", t=T)
        qT = self.arB[:, 8 * T:9 * T]
        kT = self.arB[:, 9 * T:10 * T]
        vh = self.arB[:, 10 * T:11 * T].rearrange("p (j v) -> p j v", v=128)
        woutv = self.wout[:, :].rearrange("p (f d) -> p f d", d=D)
        pT = self.scr[:, 0:2048].rearrange("p (s m n) -> p s m n", s=2, m=2)
        sqs = self.scr[:, 2048:2560]
        ones = self.cst[:, 0:128]
        valid = self.cst[:, 128:256]
        st = self.stats
        TGA = [(0, 512), (512, 1024), (1024, 1536), (1536, 2048), (2048, 2176)]
        self.fence_arB()
        P.op("pool", lambda e: e.memset(ones, 1.0), writes=["ones"])
        P.op("pool", lambda e: e.dma_start(out=valid, in_=self.din["c_valid"].ap()), writes=["valid"], dma="const_2")
        lamb = self.sqj.bitcast(F32)[:, 0:256]
        P.op("sp", lambda e: e.dma_start(out=lamb, in_=self.dram("da_lambda", 0, [[0, 128], [1, 256]])),
             writes=["sqj"], dma="misc_3")
        pr = self.sqj.bitcast(F32)[:, 256:384]
        P.op("dve", lambda e: e.tensor_tensor(out=pr[:, 0:64], in0=lamb[:, 0:64], in1=lamb[:, 64:128], op=ALU.mult),
             reads=["sqj"], writes=["lam_p1"])
        P.op("dve", lambda e: e.tensor_tensor(out=pr[:, 64:128], in0=lamb[:, 128:192], in1=lamb[:, 192:256], op=ALU.mult),
             reads=["sqj"], writes=["lam_p2"])
        P.op("dve", lambda e: e.reduce_sum(out=st[:, 58:60], in_=pr.rearrange("p (a b) -> p a b", a=2),
                                           axis=mybir.AxisListType.X), reads=["lam_p1", "lam_p2"], writes=["lam_s"])
        P.op("act", lambda e: e.activation(out=st[:, 60:62], in_=st[:, 58:60], func=AF.Exp), reads=["lam_s"], writes=["lam_e"])
        P.op("dve", lambda e: e.tensor_tensor(out=st[:, 62:63], in0=st[:, 61:62], in1=st[:, 60:61], op=ALU.subtract),
             reads=["lam_e"], writes=["lam_d"])
        nlam = st[:, 57:58]
        P.op("dve", lambda e: e.tensor_scalar(out=nlam, in0=st[:, 62:63], scalar1=-lam_init, scalar2=None, op0=ALU.add),
             reads=["lam_d"], writes=["nlam"])
        gcol = st[:, 65:66]
        P.op("sp", lambda e: e.dma_start(out=st[:, 64:65], in_=self.dram("da_subln", 0, [[1, 128], [1, 1]])),
             writes=["gcol_raw"], dma="misc_4")
        P.op("dve", lambda e: e.tensor_scalar(out=gcol, in0=st[:, 64:65], scalar1=(1.0 - lam_init) / math.sqrt(EPS), scalar2=None, op0=ALU.mult),
             reads=["gcol_raw"], writes=["gcol"])
        for c in range(8):
            src = self.dram("da_w_out", c * 128 * D, [[D, 128], [1, D]])
            P.op("pool", lambda e, c=c, src=src: e.dma_start(out=woutv[:, c, :], in_=src),
                 writes=[("wout", c)], dma=f"wout{c}")
        WROW = 3072
        pbank = [0]

        def nextbank():
            b = pbank[0]
            pbank[0] = (b + 1) % 4
            return b

        for hd in range(8):
            sa = self.ring_i
            self.ring_i = (self.ring_i + 1) % 4
            sb_ = self.ring_i
            self.ring_i = (self.ring_i + 1) % 4
            for part, col0 in ((0, hd * 128), (1, 1024 + hd * 128)):
                src = self.dram("da_w_in", col0, [[WROW, 128], [128 * WROW, DC], [1, 128]])
                P.op("pool", lambda e, part=part, src=src, sa=sa: e.dma_start(
                    out=ring[:, sa, part * 1024:(part + 1) * 1024].rearrange("p (c f) -> p c f", f=128), in_=src),
                    writes=[("ring", sa, part)], dma=f"ring{sa}p{part}")
            src = self.dram("da_w_in", 2048 + hd * 128, [[WROW, 128], [128 * WROW, DC], [1, 128]])
            P.op("pool", lambda e, src=src, sb_=sb_: e.dma_start(
                out=ring[:, sb_, 0:1024].rearrange("p (c f) -> p c f", f=128), in_=src),
                writes=[("ring", sb_, 0)], dma=f"ring{sb_}p0")
            for part, dst, key in ((1, kT, "kT"), (0, qT, "qT")):
                for (c0, c1) in TGA:
                    n = c1 - c0
                    b = nextbank()
                    for dc in range(DC):
                        P.op("pe", lambda e, dc=dc, part=part, b=b, c0=c0, c1=c1, n=n, sa=sa: e.matmul(
                            ps[b][:, 0:n], lhsT=ring[:, sa, part * 1024 + dc * 128:part * 1024 + (dc + 1) * 128],
                            rhs=hnT[:, dc, c0:c1], start=(dc == 0), stop=(dc == DC - 1)),
                            reads=[("ring", sa, part)] + [("hnT", t) for t in tiles_of(c0, c1)], writes=[("ps", b)])
                    if part == 0:
                        P.op("act", lambda e, b=b, n=n, c0=c0, c1=c1: e.activation(
                            out=qT[:, c0:c1], in_=ps[b][:, 0:n], func=AF.Copy, scale=0.125),
                            reads=[("ps", b)], writes=["qT"])
                    else:
                        P.op("dve", lambda e, b=b, n=n, c0=c0, c1=c1: e.tensor_copy(out=kT[:, c0:c1], in_=ps[b][:, 0:n]),
                             reads=[("ps", b)], writes=["kT"])
            for t0 in range(0, NT, 4):
                tl = list(range(t0, min(t0 + 4, NT)))
                b = nextbank()
                for j, tt in enumerate(tl):
                    for dc in range(DC):
                        P.op("pe", lambda e, dc=dc, b=b, j=j, tt=tt, sb_=sb_: e.matmul(
                            ps[b][:, j * 128:(j + 1) * 128], lhsT=hnT[:, dc, tt * 128:(tt + 1) * 128],
                            rhs=ring[:, sb_, dc * 128:(dc + 1) * 128], start=(dc == 0), stop=(dc == DC - 1)),
                            reads=[("ring", sb_, 0), ("hnT", tt)], writes=[("ps", b)])
                nn = len(tl)
                P.op("dve", lambda e, b=b, t0=t0, nn=nn: e.tensor_copy(
                    out=vh[:, t0:t0 + nn, :], in_=ps[b][:, 0:nn * 128].rearrange("p (j v) -> p j v", v=128)),
                    reads=[("ps", b)], writes=["vh"])
            pairs = []
            for g in range(5):
                q0 = max(g * 512, NPAD)
                q1 = min((g + 1) * 512, T)
                nkb = q1 // 128
                for kb in range(nkb):
                    pairs.append((g, kb, q0, q1, nkb))

            def issue_S(i):
                g, kb, q0, q1, nkb = pairs[i]
                sset = i % 2
                c0 = max(kb * 128, q0)
                n = q1 - c0
                for m in range(2):
                    P.op("pe", lambda e, m=m, sset=sset, kb=kb, c0=c0, q1=q1, n=n: e.matmul(
                        ps[2 * sset + m][:, 0:n], lhsT=kT[m * 64:(m + 1) * 64, kb * 128:(kb + 1) * 128],
                        rhs=qT[m * 64:(m + 1) * 64, c0:q1], start=True, stop=True),
                        reads=["kT", "qT"], writes=[("ps", 2 * sset + m)])
                P.op("act", lambda e, sset=sset, n=n: e.activation(
                    out=pT[:, sset, :, 0:n], in_=self.psall[:, 2 * sset:2 * sset + 2, 0:n], func=AF.Exp),
                    reads=[("ps", 2 * sset), ("ps", 2 * sset + 1)], writes=[("pT", sset)])
                if kb >= 1 and kb * 128 >= q0:
                    P.op("pool", lambda e, sset=sset: e.memset(pT[64:128, sset, :, 0:64], 0.0), writes=[("pT", sset)])

            def issue_PV(i):
                g, kb, q0, q1, nkb = pairs[i]
                sset = i % 2
                c0 = max(kb * 128, q0)
                n = q1 - c0
                off = c0 - q0
                den_l = valid if kb == 0 else ones
                den_key = "valid" if kb == 0 else "ones"
                diag = kb >= 1 and kb * 128 >= q0
                segs = [(0, n, 128)]
                for which in range(2):
                    for m in range(2):
                        bank = (4 if which == 0 else 6) + m
                        for (a0, a1, kk) in segs:
                            if a1 <= a0:
                                continue
                            lhs = vh[0:kk, kb, :] if which == 0 else den_l[0:kk, :]
                            P.op("pe", lambda e, m=m, sset=sset, kb=kb, off=off, nkb=nkb, bank=bank, a0=a0, a1=a1, kk=kk, lhs=lhs:
                                 e.matmul(ps[bank][:, off + a0:off + a1], lhsT=lhs, rhs=pT[0:kk, sset, m, a0:a1],
                                          start=(kb == 0), stop=(kb == nkb - 1), skip_group_check=True),
                                 reads=["vh" if which == 0 else den_key, ("pT", sset)], writes=[("ps", bank)])

            def issue_evac_a(i, hd=hd):
                g, kb, q0, q1, nkb = pairs[i]
                nq = q1 - q0
                d0s = self.hntok[:, 0, :].bitcast(F32)[:, 0:nq]
                d1s = self.hntok[:, 1, :].bitcast(F32)[:, 0:nq]
                o1s = self.sqj.bitcast(F32)[:, 0:nq]
                wv = self.stage[:, 1, 0:nq]
                t0 = self.stage[:, 0, 0:nq]
                P.op("act", lambda e: e.copy(out=d1s, in_=ps[7][:, 0:nq]), reads=[("ps", 7)], writes=[("hntok", 1)])
                P.op("act", lambda e: e.copy(out=d0s, in_=ps[6][:, 0:nq]), reads=[("ps", 6)], writes=[("hntok", 0)])
                P.op("dve", lambda e: e.tensor_copy(out=o1s, in_=ps[5][:, 0:nq]), reads=[("ps", 5)], writes=["sqj"])
                P.op("dve", lambda e: e.tensor_tensor(out=t0, in0=ps[4][:, 0:nq], in1=d1s, op=ALU.mult),
                     reads=[("ps", 4), ("hntok", 1)], writes=[("stage", 0)])
                P.op("dve", lambda e: e.tensor_tensor(out=wv, in0=o1s, in1=d0s, op=ALU.mult),
                     reads=["sqj", ("hntok", 0)], writes=[("stage", 1)])
                P.op("dve", lambda e: e.scalar_tensor_tensor(out=wv, in0=wv, scalar=nlam, in1=t0, op0=ALU.mult, op1=ALU.add),
                     reads=[("stage", 0), ("stage", 1), "nlam"], writes=[("stage", 1)])
                P.op("pool", lambda e: e.tensor_tensor(out=d0s, in0=d0s, in1=d1s, op=ALU.mult),
                     reads=[("hntok", 0), ("hntok", 1)], writes=[("hntok", 0)])
                P.op("pool", lambda e: e.tensor_tensor(out=d0s, in0=d0s, in1=d0s, op=ALU.mult),
                     reads=[("hntok", 0)], writes=[("hntok", 0)])
                P.op("pool", lambda e: e.tensor_tensor(out=sqs[:, 0:nq], in0=wv, in1=wv, op=ALU.mult),
                     reads=[("stage", 1)], writes=["sqs"])

            def issue_evac_b(i, hd=hd):
                g, kb, q0, q1, nkb = pairs[i]
                nq = q1 - q0
                tl = list(tiles_of(q0, q1))
                d0s = self.hntok[:, 0, :].bitcast(F32)[:, 0:nq]
                wv = self.stage[:, 1, 0:nq]
                t0 = self.stage[:, 0, 0:nq]
                b = nextbank()
                P.op("pe", lambda e: e.matmul(ps[b][:, 0:nq], lhsT=ones, rhs=sqs[:, 0:nq], start=True, stop=True),
                     reads=["ones", "sqs"], writes=[("ps", b)])
                P.op("dve", lambda e: e.tensor_scalar(out=t0, in0=ps[b][:, 0:nq], scalar1=1.0 / (128 * EPS), scalar2=None, op0=ALU.mult),
                     reads=[("ps", b)], writes=[("stage", 0)])
                P.op("dve", lambda e: e.tensor_tensor(out=t0, in0=t0, in1=d0s, op=ALU.add),
                     reads=[("stage", 0), ("hntok", 0)], writes=[("stage", 0)])
                P.op("act", lambda e: e.activation(out=t0, in_=t0, func=AF.Ln), reads=[("stage", 0)], writes=[("stage", 0)])
                P.op("act", lambda e: e.activation(out=t0, in_=t0, func=AF.Exp, scale=-0.5), reads=[("stage", 0)], writes=[("stage", 0)])
                P.op("dve", lambda e: e.scalar_tensor_tensor(out=oT[:, hd, q0:q1], in0=wv, scalar=gcol, in1=t0,
                                                             op0=ALU.mult, op1=ALU.mult),
                     reads=[("stage", 0), ("stage", 1), "gcol"], writes=[("oT", hd, t) for t in tl])

            issue_S(0)
            pend = None
            for i in range(len(pairs)):
                if i + 1 < len(pairs):
                    issue_S(i + 1)
                issue_PV(i)
                if pend is not None and i >= pend + 3:
                    issue_evac_b(pend)
                    pend = None
                if pairs[i][1] == pairs[i][4] - 1:
                    if pend is not None:
                        issue_evac_b(pend)
                    issue_evac_a(i)
                    pend = i
            if pend is not None:
                issue_evac_b(pend)
        for tt in range(NT):
            pa = 4 if self.mm2_i == 0 else 6
            self.mm2_i ^= 1
            for dh in range(2):
                for c in range(8):
                    P.op("pe", lambda e, dh=dh, c=c, tt=tt, pa=pa: e.matmul(
                        ps[pa + dh][:, :], lhsT=oT[:, c, tt * 128:(tt + 1) * 128],
                        rhs=woutv[:, c, dh * 512:(dh + 1) * 512], start=(c == 0), stop=(c == 7)),
                        reads=[("oT", c, tt), ("wout", c)], writes=[("ps", pa + dh)])
            for dh in range(2):
                hs = self.h[:, tt, dh * 512:(dh + 1) * 512]
                P.op("dve", lambda e, dh=dh, pa=pa, hs=hs: e.tensor_tensor(out=hs, in0=ps[pa + dh][:, :], in1=hs, op=ALU.add),
                     reads=[("ps", pa + dh), ("h", tt)], writes=[("h", tt)])


    def gla(self, l):
        P, ps, hnT, ring = self.P, self.ps, self.hnT, self.ring
        oT = self.arB[:, 0:8 * T].rearrange("p (c t) -> p c t", t=T)
        qT = self.arB[:, 8 * T:9 * T]
        kdec = self.arB[:, 9 * T:10 * T].rearrange("p (j v) -> p j v", v=128)
        woutv = self.wout[:, :].rearrange("p (f d) -> p f d", d=D)
        vh = self.wout[:, 0:NT * 256].rearrange("p (j v) -> p j v", v=256)
        VHK = [("wout", f) for f in range(5)]
        S32 = self.wout[:, 5120:6144].bitcast(F32).rearrange("p (s v) -> p s v", s=2)
        Sb = self.wout[:, 6144:6912].rearrange("p (s v) -> p s v", s=3)
        glrT = self.scr[0:32, 0:T]
        wglr = self.scr[:, T:T + 128].rearrange("p (c r) -> p c r", r=16)
        sqs = self.scr[:, 2304:2816]
        wg2 = self.hntok[0:17, 0, 0:512]
        gbf = self.gb[:, 0, :]
        tri = gbf[:, 0:128]
        cind = gbf[:, 128:130]
        decay = gbf[:, 130:130 + 4 * 34].rearrange("p (h n) -> p h n", h=4)
        ones = self.cst[:, 0:128]
        st = self.stats
        one_col = st[:, 69:70]
        gcol = st[:, 66:68]
        stage = self.stage
        WROW = 3088
        TGA = [(0, 512), (512, 1024), (1024, 1536), (1536, 2048), (2048, 2176)]
        GB = ("gb", 0)
        self.fence_arB()
        P.op("pool", lambda e: e.memset(ones, 1.0), writes=["ones"])
        P.op("dve", lambda e: e.memset(one_col, 1.0), writes=["one_col"])
        P.op("dve", lambda e: e.memset(Sb[:, :, :], 0.0), writes=[("Sb", 0), ("Sb", 1), ("Sb", 2), ("wout", 6)])
        P.op("sp", lambda e: e.dma_start(out=tri, in_=self.din["c_tri"].ap()), writes=["tri", GB], dma="misc_5")
        P.op("sp", lambda e: e.dma_start(out=cind, in_=self.din["c_cind"].ap()), writes=["cind", GB], dma="misc_6")
        for a in range(2):
            P.op("sp", lambda e, a=a: e.dma_start(out=gcol[:, a:a + 1], in_=self.dram("gla_norm", a * 128, [[1, 128], [1, 1]])),
                 writes=["gcolg"], dma="misc_7")
        P.op("pool", lambda e: e.memset(glrT, 1.0), writes=["glrT", ("pT", 0), ("pT", 1), "sqs"])
        P.op("pool", lambda e: e.dma_start(out=wglr, in_=self.dram("gla_w_in", 3072, [[WROW, 128], [128 * WROW, DC], [1, 16]])),
             writes=["wglr", ("pT", 0), ("pT", 1), "sqs"], dma="const_8")
        P.op("pool", lambda e: e.dma_start(out=wg2[0:16, :], in_=self.din["gla_w_gate2"].ap()[0]),
             writes=[("hntok", 0)], dma="const_9")
        P.op("pool", lambda e: e.dma_start(out=wg2[16:17, :], in_=self.din["gla_b_gate2"].ap()),
             writes=[("hntok", 0)], dma="const_10")
        pbank = [0]

        def nextbank():
            b = pbank[0]
            pbank[0] = (b + 1) % 4
            return b

        for (c0, c1) in TGA:
            n = c1 - c0
            b = nextbank()
            for dc in range(DC):
                P.op("pe", lambda e, dc=dc, b=b, c0=c0, c1=c1, n=n: e.matmul(
                    ps[b][0:16, 0:n], lhsT=wglr[:, dc, :], rhs=hnT[:, dc, c0:c1], start=(dc == 0), stop=(dc == DC - 1)),
                    reads=["wglr"] + [("hnT", t) for t in tiles_of(c0, c1)], writes=[("ps", b)])
            P.op("dve", lambda e, b=b, n=n, c0=c0, c1=c1: e.tensor_copy(out=glrT[0:16, c0:c1], in_=ps[b][0:16, 0:n]),
                 reads=[("ps", b)], writes=["glrT"])
        for hd in range(4):
            sa = self.ring_i
            self.ring_i = (self.ring_i + 1) % 4
            sb_ = self.ring_i
            self.ring_i = (self.ring_i + 1) % 4
            for part, col0 in ((0, hd * 128), (1, 512 + hd * 128)):
                src = self.dram("gla_w_in", col0, [[WROW, 128], [128 * WROW, DC], [1, 128]])
                P.op("pool", lambda e, part=part, src=src, sa=sa: e.dma_start(
                    out=ring[:, sa, part * 1024:(part + 1) * 1024].rearrange("p (c f) -> p c f", f=128), in_=src),
                    writes=[("ring", sa, part)], dma=f"ring{sa}p{part}")
            src = self.dram("gla_w_in", 1024 + hd * 256, [[WROW, 128], [128 * WROW, DC], [1, 256]])
            P.op("pool", lambda e, src=src, sb_=sb_: e.dma_start(
                out=ring[:, sb_, :].rearrange("p (c f) -> p c f", f=256), in_=src),
                writes=[("ring", sb_, 0), ("ring", sb_, 1)], dma=f"ring{sb_}p0")
            for (c0, c1) in TGA:
                n = c1 - c0
                b = nextbank()
                for dc in range(DC):
                    P.op("pe", lambda e, dc=dc, b=b, c0=c0, c1=c1, n=n, sa=sa: e.matmul(
                        ps[b][:, 0:n], lhsT=ring[:, sa, dc * 128:(dc + 1) * 128], rhs=hnT[:, dc, c0:c1],
                        start=(dc == 0), stop=(dc == DC - 1)),
                        reads=[("ring", sa, 0)] + [("hnT", t) for t in tiles_of(c0, c1)], writes=[("ps", b)])
                P.op("act", lambda e, b=b, n=n, c0=c0, c1=c1: e.activation(
                    out=qT[:, c0:c1], in_=ps[b][:, 0:n], func=AF.Copy, scale=128 ** -0.5),
                    reads=[("ps", b)], writes=["qT"])
            def front(tt, hd=hd, sa=sa, sb_=sb_):
                b = tt % 3
                par = tt % 2
                for dc in range(DC):
                    P.op("pe", lambda e, dc=dc: e.matmul(
                        ps[b][:, 0:128], lhsT=hnT[:, dc, tt * 128:(tt + 1) * 128],
                        rhs=ring[:, sa, 1024 + dc * 128:1024 + (dc + 1) * 128], start=(dc == 0), stop=(dc == DC - 1)),
                        reads=[("ring", sa, 1), ("hnT", tt)], writes=[("ps", b)])
                for dc in range(DC):
                    P.op("pe", lambda e, dc=dc: e.matmul(
                        ps[b][:, 128:384], lhsT=hnT[:, dc, tt * 128:(tt + 1) * 128],
                        rhs=ring[:, sb_, dc * 256:(dc + 1) * 256], start=(dc == 0), stop=(dc == DC - 1)),
                        reads=[("ring", sb_, 0), ("ring", sb_, 1), ("hnT", tt)], writes=[("ps", b)])
                P.op("pe", lambda e: e.matmul(
                    ps[b][:, 384:512], lhsT=glrT[0:17, tt * 128:(tt + 1) * 128], rhs=wg2[:, hd * 128:(hd + 1) * 128],
                    start=True, stop=True), reads=["glrT", ("hntok", 0)], writes=[("ps", b)])
                sp_ = stage[:, par, 0:128]
                P.op("act", lambda e: e.activation(out=sp_, in_=ps[b][:, 384:512], func=AF.Exp, scale=-1.0),
                     reads=[("ps", b)], writes=[("stage", par)])
                P.op("act", lambda e: e.activation(out=sp_, in_=sp_, func=AF.Ln, bias=one_col),
                     reads=[("stage", par), "one_col"], writes=[("stage", par)])

            def back(tt, hd=hd):
                b = tt % 3
                par = tt % 2
                b2 = 3
                sp_ = stage[:, par, 0:128]
                eR = stage[:, par, 128:256]
                P.op("pe", lambda e: e.matmul(ps[b2][:, 0:128], lhsT=tri, rhs=sp_, start=True, stop=True),
                     reads=["tri", GB, ("stage", par)], writes=[("ps", b2)])
                P.op("pe", lambda e: e.matmul(ps[b2][:, 128:130], lhsT=sp_, rhs=cind, start=True, stop=True),
                     reads=["cind", GB, ("stage", par)], writes=[("ps", b2)])
                P.op("act", lambda e: e.activation(out=eR, in_=ps[b2][:, 0:128], func=AF.Exp),
                     reads=[("ps", b2)], writes=[("stage_e", par)])
                P.op("act", lambda e: e.activation(out=decay[:, hd, 2 * tt:2 * tt + 2], in_=ps[b2][:, 128:130], func=AF.Exp),
                     reads=[("ps", b2)], writes=["decay"])
                P.op("dve", lambda e: e.tensor_tensor(out=kdec[:, tt, :], in0=ps[b][:, 0:128], in1=eR, op=ALU.mult),
                     reads=[("ps", b), ("stage_e", par)], writes=["kT"])
                P.op("dve", lambda e: e.tensor_copy(out=vh[:, tt, :], in_=ps[b][:, 128:384]),
                     reads=[("ps", b)], writes=VHK)

            for tt in range(NT):
                front(tt)
                if tt >= 1:
                    back(tt - 1)
            back(NT - 1)
            P.op("dve", lambda e: e.memset(S32[:, 1, :], 0.0), writes=[("S32", 1), ("wout", 5)])

            def kvmm(n_):
                tt, hf = n_ // 2, n_ % 2
                kb = 3 + (n_ % 3)
                P.op("pe", lambda e: e.matmul(
                    ps[kb][:, 0:256], lhsT=kdec[hf * 64:(hf + 1) * 64, tt, :], rhs=vh[hf * 64:(hf + 1) * 64, tt, :],
                    start=True, stop=True), reads=["kT"] + VHK, writes=[("ps", kb)])

            def state(n_, hd=hd):
                cur, prev = n_ % 2, 1 - (n_ % 2)
                kb = 3 + (n_ % 3)
                sbi = n_ % 3
                P.op("dve", lambda e: e.scalar_tensor_tensor(
                    out=S32[:, cur, :], in0=S32[:, prev, :], scalar=decay[:, hd, n_:n_ + 1], in1=ps[kb][:, 0:256],
                    op0=ALU.mult, op1=ALU.add),
                    reads=[("S32", prev), "decay", GB, ("ps", kb)], writes=[("S32", cur)])
                P.op("act", lambda e: e.copy(out=Sb[:, sbi, :], in_=S32[:, cur, :]),
                     reads=[("S32", cur)], writes=[("Sb", sbi)])

            def omm(n_, hd=hd):
                sbi = n_ % 3
                grp = n_ // 8
                ob = 6 if grp % 2 == 0 else 0
                for a in range(2):
                    P.op("pe", lambda e, a=a: e.matmul(
                        ps[ob + a][:, (n_ % 8) * 64:(n_ % 8) * 64 + 64], lhsT=Sb[:, sbi, a * 128:(a + 1) * 128],
                        rhs=qT[:, n_ * 64:(n_ + 1) * 64], start=True, stop=True),
                        reads=[("Sb", sbi), "qT"], writes=[("ps", ob + a)])
                if n_ % 8 == 7 or n_ == 33:
                    c0 = grp * 512
                    c1 = (n_ + 1) * 64
                    n = c1 - c0
                    for a in range(2):
                        P.op("dve" if a == 0 else "act", (lambda e, a=a:
                             (e.tensor_copy(out=oT[:, hd * 2 + a, c0:c1], in_=ps[ob + a][:, 0:n]) if a == 0 else
                              e.copy(out=oT[:, hd * 2 + a, c0:c1], in_=ps[ob + a][:, 0:n]))),
                             reads=[("ps", ob + a)], writes=[("oT", hd * 2 + a, t) for t in tiles_of(c0, c1)])

            kvmm(0)
            kvmm(1)
            for n_ in range(34):
                state(n_)
                if n_ + 2 < 34:
                    kvmm(n_ + 2)
                omm(n_)
        extra = [("S32", 0), ("S32", 1), ("Sb", 0), ("Sb", 1), ("Sb", 2)]
        for c in range(8):
            src = self.dram("gla_w_out", c * 128 * D, [[D, 128], [1, D]])
            P.op("pool", lambda e, c=c, src=src: e.dma_start(out=woutv[:, c, :], in_=src),
                 writes=[("wout", c)] + (extra if c in (5, 6) else []), dma=f"wout{c}")
        for hd in range(4):
            for (c0, c1) in TGA:
                n = c1 - c0
                tl = list(tiles_of(c0, c1))
                b = nextbank()
                for a in range(2):
                    c = hd * 2 + a
                    P.op("pool", lambda e, c=c, c0=c0, c1=c1, n=n: e.tensor_tensor(
                        out=sqs[:, 0:n], in0=oT[:, c, c0:c1], in1=oT[:, c, c0:c1], op=ALU.mult),
                        reads=[("oT", c, t) for t in tl], writes=["sqs"])
                    P.op("pe", lambda e, a=a, b=b, n=n: e.matmul(ps[b][:, 0:n], lhsT=ones, rhs=sqs[:, 0:n],
                                                                start=(a == 0), stop=(a == 1)),
                         reads=["ones", "sqs"], writes=[("ps", b)])
                P.op("act", lambda e, b=b, n=n: e.activation(out=stage[:, 0, 0:n], in_=ps[b][:, 0:n], func=AF.Ln,
                                                             scale=1.0 / 256, bias=self.eps_ap),
                     reads=[("ps", b), "eps"], writes=[("stage", 0)])
                P.op("act", lambda e, n=n: e.activation(out=stage[:, 0, 0:n], in_=stage[:, 0, 0:n], func=AF.Exp, scale=-0.5),
                     reads=[("stage", 0)], writes=[("stage", 0)])
                for a in range(2):
                    c = hd * 2 + a
                    P.op("dve", lambda e, a=a, c=c, c0=c0, c1=c1, n=n: e.scalar_tensor_tensor(
                        out=oT[:, c, c0:c1], in0=oT[:, c, c0:c1], scalar=gcol[:, a:a + 1], in1=stage[:, 0, 0:n],
                        op0=ALU.mult, op1=ALU.mult),
                        reads=[("oT", c, t) for t in tl] + [("stage", 0), "gcolg"], writes=[("oT", c, t) for t in tl])
        for c in range(8):
            if c % 2 == 0:
                sg_ = self.ring_i
                self.ring_i = (self.ring_i + 1) % 4
            part = c % 2
            src = self.dram("gla_w_in", 2048 + c * 128, [[WROW, 128], [128 * WROW, DC], [1, 128]])
            P.op("pool", lambda e, part=part, src=src, sg_=sg_: e.dma_start(
                out=ring[:, sg_, part * 1024:(part + 1) * 1024].rearrange("p (c f) -> p c f", f=128), in_=src),
                writes=[("ring", sg_, part)], dma=f"ring{sg_}p{part}")
            for gi, (c0, c1) in enumerate(TGA):
                n = c1 - c0
                tl = list(tiles_of(c0, c1))
                b = nextbank()
                for dc in range(DC):
                    P.op("pe", lambda e, dc=dc, b=b, c0=c0, c1=c1, n=n, sg_=sg_, part=part: e.matmul(
                        ps[b][:, 0:n], lhsT=ring[:, sg_, part * 1024 + dc * 128:part * 1024 + (dc + 1) * 128],
                        rhs=hnT[:, dc, c0:c1], start=(dc == 0), stop=(dc == DC - 1)),
                        reads=[("ring", sg_, part)] + [("hnT", t) for t in tl], writes=[("ps", b)])
                sgb = gi % 2
                P.op("act", lambda e, b=b, n=n, sgb=sgb: e.activation(out=stage[:, sgb, 0:n], in_=ps[b][:, 0:n], func=AF.Silu),
                     reads=[("ps", b)], writes=[("stage", sgb)])
                P.op("pool", lambda e, c=c, c0=c0, c1=c1, n=n, sgb=sgb: e.tensor_tensor(
                    out=oT[:, c, c0:c1], in0=oT[:, c, c0:c1], in1=stage[:, sgb, 0:n], op=ALU.mult),
                    reads=[("oT", c, t) for t in tl] + [("stage", sgb)], writes=[("oT", c, t) for t in tl])
        for tt in range(NT):
            pa = 4 if self.mm2_i == 0 else 6
            self.mm2_i ^= 1
            for dh in range(2):
                for c in range(8):
                    P.op("pe", lambda e, dh=dh, c=c, tt=tt, pa=pa: e.matmul(
                        ps[pa + dh][:, :], lhsT=oT[:, c, tt * 128:(tt + 1) * 128],
                        rhs=woutv[:, c, dh * 512:(dh + 1) * 512], start=(c == 0), stop=(c == 7)),
                        reads=[("oT", c, tt), ("wout", c)], writes=[("ps", pa + dh)])
            for dh in range(2):
                hs = self.h[:, tt, dh * 512:(dh + 1) * 512]
                P.op("dve", lambda e, dh=dh, pa=pa, hs=hs: e.tensor_tensor(out=hs, in0=ps[pa + dh][:, :], in1=hs, op=ALU.add),
                     reads=[("ps", pa + dh), ("h", tt)], writes=[("h", tt)])

    def final(self, tt, g):
        P, h = self.P, self.h
        ss = self.stats[:, tt:tt + 1]
        sd = self.stats[:, 20 + tt:21 + tt]
        rs = self.stats[:, 40 + tt:41 + tt]
        P.op("act", lambda e: e.activation(out=self.sqj[:, :], in_=h[:, tt, :], func=AF.Square, accum_out=ss),
             reads=[("h", tt)], writes=["sqj", ("ss", tt)], noembed=True)
        P.op("act", lambda e: e.activation(out=sd, in_=ss, func=AF.Sqrt, scale=1.0 / D, bias=self.eps_ap),
             reads=[("ss", tt), "eps"], writes=[("sd", tt)])
        P.op("dve", lambda e: e.reciprocal(out=rs, in_=sd), reads=[("sd", tt)], writes=[("rs", tt)])
        b = tt % 2
        P.op("dve", lambda e: e.scalar_tensor_tensor(out=h[:, tt, :], in0=h[:, tt, :], scalar=rs,
                                                     in1=self.gb[:, g, :], op0=ALU.mult, op1=ALU.mult),
             reads=[("h", tt), ("rs", tt), ("gb", g)], writes=[("h", tt)])
        P.op("sp", lambda e: e.dma_start(out=self.dout.ap()[(tt - 1) * 128:tt * 128, :], in_=h[:, tt, :]),
             reads=[("h", tt)], dma=f"out{tt}")

    def dump_h(self):
        hv = self.dout.ap().rearrange("(j p) d -> p j d", p=128)
        self.P.op("sp", lambda e: e.dma_start(out=hv, in_=self.h[:, :, :]),
                  reads=[("h", t) for t in range(NT)], dma="out")
        keys = [("actT", f, t) for f in range(HALF) for t in range(NT)] + [("oT", c, t) for c in range(8) for t in range(NT)] + ["qT", "kT", "vh"]
        if getattr(self, "dbg_src", "arB") == "hnT":
            for f in range(DC):
                self.P.op("pool", lambda e, f=f: e.dma_start(out=self.ddbg.ap()[:, f * T:(f + 1) * T], in_=self.hnT[:, f, :]),
                          reads=[("hnT", t) for t in range(NT)], dma="out")
            self.P.op("pool", lambda e: e.dma_start(out=self.ddbg.ap()[:, 8 * T:8 * T + 80], in_=self.stats[:, 0:80]),
                      reads=[("rs", t) for t in range(NT)], dma="out")
            self.P.op("pool", lambda e: e.dma_start(out=self.ddbg.ap()[:, 9 * T:9 * T + 1024], in_=self.gb[:, 0, :]),
                      reads=[("gb", 0)], dma="out")
            return
        for f in range(HALF):
            self.P.op("pool", lambda e, f=f: e.dma_start(out=self.ddbg.ap()[:, f * T:(f + 1) * T], in_=self.arB[:, f * T:(f + 1) * T]),
                      reads=keys, dma="out")

    def build(self):
        nc = self.nc
        st = contextlib.ExitStack()
        with st:
            self.alloc(st)
            P = self.P
            self.eps_ap = self.stats[:, 63:64]
            P.op("dve", lambda e: e.memset(self.eps_ap, EPS), writes=["eps"])
            self.load_inputs()
            subs = [("ffn", 0, 1), ("da", 0, 0), ("ffn", 0, 2), ("ffn", 1, 1), ("gla", 1, 0), ("ffn", 1, 2)]
            norm_name = {("ffn", 1): "ffn1_norm", ("ffn", 2): "ffn2_norm", ("da", 0): "mix_norm", ("gla", 0): "mix_norm"}
            subs = subs[:self.n_sub]
            for si, (kind, l, which) in enumerate(subs):
                g = self.load_gain(norm_name[(kind, which)], l)
                for tt in range(NT):
                    self.norm_tile(tt, g)
                if getattr(self, "skip_last_body", False) and si == len(subs) - 1:
                    self.dbg_src = "hnT"
                    break
                if kind == "ffn":
                    self.ffn(l, which)
                elif kind == "da":
                    self.diff_attn(l)
                else:
                    self.gla(l)
            if self.debug:
                self.dump_h()
            else:
                g = self.load_gain("final_norm", 0)
                for tt in range(1, NT):
                    self.final(tt, g)
            P.emit(final_wait_dma="out")
        return nc


_CACHE = {}


def kernel(**inputs):
    n = 8
    if "nc" not in _CACHE:
        _CACHE["nc"] = Builder().build()
    nc = _CACHE["nc"]
    consts = host_consts()
    shared = {}
    for name, shape in IN_SPECS:
        if name == "x" or name.startswith("c_"):
            continue
        shared[name] = np.ascontiguousarray(np.asarray(inputs[name], dtype=np.float32).reshape(shape))
    x = np.asarray(inputs["x"], dtype=np.float32)
    in_maps = []
    for c in range(n):
        m = dict(shared)
        m.update(consts)
        m["x"] = np.ascontiguousarray(x[c])
        in_maps.append(m)
    res = run_bass_kernel_spmd(nc, in_maps, core_ids=list(range(n)))
    return np.stack([np.asarray(r["out"], dtype=np.float32) for r in res.results], axis=0)
```

```python
import contextlib
import math

import numpy as np
import concourse.bass as bass
import concourse.mybir as mybir
from concourse.bass_utils import run_bass_kernel_spmd
from concourse.alu_op_type import AluOpType as ALU

F32 = mybir.dt.float32
BF16 = mybir.dt.bfloat16
AF = mybir.ActivationFunctionType

ENG_ATTR = {"pe": "tensor", "act": "scalar", "dve": "vector", "pool": "gpsimd", "sp": "sync"}

T = 2176
NT = 17
D = 1024
DC = 8
DFF = 2816
NFC = 22
HALF = 11
NPAD = 112
TG = [(112, 128), (128, 640), (640, 1152), (1152, 1664), (1664, 2176)]
EPS = 1e-6


def tiles_of(c0, c1):
    return range(c0 // 128, (c1 + 127) // 128)


class Prog:
    def __init__(self, nc):
        self.nc = nc
        self.instrs = []
        self.last_writer = {}
        self.readers = {}

    def op(self, eng, fn, reads=(), writes=(), dma=None, noembed=False):
        idx = len(self.instrs)
        deps = {}
        for r in reads:
            w = self.last_writer.get(r)
            if w is not None:
                deps[w] = "raw"
        for wr in writes:
            w = self.last_writer.get(wr)
            if w is not None and w not in deps:
                deps[w] = "waw"
            for rd in self.readers.get(wr, ()):
                if rd not in deps:
                    deps[rd] = "war"
        deps.pop(idx, None)
        for r in reads:
            self.readers.setdefault(r, []).append(idx)
        for wr in writes:
            self.last_writer[wr] = idx
            self.readers[wr] = []
        self.instrs.append(dict(eng=eng, fn=fn, deps=deps, dma=dma, signal=False, cnt=None, noembed=noembed,
                                semkey=(("dma", dma) if dma is not None else ("eng", eng))))
        return idx

    def emit(self, final_wait_dma="out"):
        nc = self.nc
        ins = self.instrs
        need = []
        for i, I in enumerate(ins):
            per = {}
            for d, kind in I["deps"].items():
                Dd = ins[d]
                if Dd["eng"] == I["eng"] and Dd["dma"] is None and I["dma"] is None:
                    if kind != "raw" or I["eng"] == "pe":
                        continue
                k = Dd["semkey"]
                if d > per.get(k, -1):
                    per[k] = d
            need.append(per)
            for d in per.values():
                ins[d]["signal"] = True
        cnt = {}
        for I in ins:
            key = I["semkey"]
            if I["dma"] is not None:
                cnt[key] = cnt.get(key, 0) + 16
            elif I["signal"]:
                cnt[key] = cnt.get(key, 0) + 1
            I["cnt"] = cnt.get(key, 0)
        st = contextlib.ExitStack()
        sems = {}
        for key in cnt:
            sems[key] = st.enter_context(nc.semaphore("s_" + "_".join(map(str, key))))
        per_eng = {e: [] for e in ENG_ATTR}
        for i, I in enumerate(ins):
            per_eng[I["eng"]].append(i)
        with st, nc.Block() as block:
            def make(engname):
                def body(eng):
                    seen = {}
                    for i in per_eng[engname]:
                        I = ins[i]
                        waits = {}
                        for k, d in need[i].items():
                            v = ins[d]["cnt"]
                            if v > seen.get(k, 0) and v > waits.get(k, (0, 0))[0]:
                                waits[k] = (v, d)
                        if I["dma"] is not None:
                            prev = I["cnt"] - 16
                            k = I["semkey"]
                            if prev > seen.get(k, 0) and prev > waits.get(k, (0, 0))[0]:
                                waits[k] = (prev, -1)
                        embed = None
                        if waits and I["dma"] is None and not I["noembed"]:
                            embed = max(waits, key=lambda k: waits[k][1])
                        for k, (v, d) in waits.items():
                            if k != embed:
                                eng.wait_ge(sems[k], v)
                            seen[k] = v
                        r = I["fn"](eng)
                        if embed is not None:
                            r = r._wait_ge(sems[embed], waits[embed][0])
                        if I["dma"] is not None:
                            r.then_inc(sems[I["semkey"]], 16)
                        elif I["signal"]:
                            r.then_inc(sems[I["semkey"]], 1)
                    if engname == "sp":
                        for k, v in cnt.items():
                            if k[0] == "dma" and k[1].startswith(final_wait_dma):
                                eng.wait_ge(sems[k], v)
                return body
            for engname, attr in ENG_ATTR.items():
                if per_eng[engname] or engname == "sp":
                    getattr(block, attr)(make(engname))


IN_SPECS = [
    ("x", [2048, 1024]), ("meta", [16, 1024]),
    ("ffn1_norm", [2, 1024]), ("ffn1_w_in", [2, 1024, 5632]), ("ffn1_w_out", [2, 2816, 1024]),
    ("mix_norm", [2, 1024]), ("ffn2_norm", [2, 1024]), ("ffn2_w_in", [2, 1024, 5632]),
    ("ffn2_w_out", [2, 2816, 1024]),
    ("da_w_in", [1, 1024, 3072]), ("da_lambda", [1, 4, 64]), ("da_subln", [1, 128]),
    ("da_w_out", [1, 1024, 1024]),
    ("gla_w_in", [1, 1024, 3088]), ("gla_w_gate2", [1, 16, 512]), ("gla_b_gate2", [1, 512]),
    ("gla_norm", [1, 256]), ("gla_w_out", [1, 1024, 1024]), ("final_norm", [1, 1024]),
    ("c_ident", [128, 128]), ("c_tri", [128, 128]), ("c_cind", [128, 2]), ("c_valid", [128, 128]),
]


def host_consts():
    ident = np.eye(128, dtype=np.float32)
    s = np.arange(128)[:, None]
    t = np.arange(128)[None, :]
    tri = ((s > t) & ((s // 64) == (t // 64))).astype(np.float32) * (-1.0 / 16.0)
    cind = np.zeros((128, 2), np.float32)
    cind[:64, 0] = -1.0 / 16.0
    cind[64:, 1] = -1.0 / 16.0
    valid = np.zeros((128, 128), np.float32)
    valid[NPAD:, :] = 1.0
    return dict(c_ident=ident, c_tri=tri, c_cind=cind, c_valid=valid)


class Builder:
    def __init__(self, n_sub=6, debug=False):
        self.n_sub = n_sub
        self.debug = debug
        nc = bass.Bass("TRN2", target_bir_lowering=False)
        self.nc = nc
        self.din = {}
        for name, shape in IN_SPECS:
            self.din[name] = nc.dram_tensor(name, shape, F32, kind="ExternalInput")
        if debug:
            self.dout = nc.dram_tensor("out", [T, D], F32, kind="ExternalOutput")
            self.ddbg = nc.dram_tensor("dbg", [128, HALF * T], F32, kind="ExternalOutput")
        else:
            self.dout = nc.dram_tensor("out", [2048, D], F32, kind="ExternalOutput")
        self.P = Prog(nc)
        self.gbi = 0
        self.ring_i = 0
        self.psT_i = 0
        self.mm1_i = 0
        self.mm2_i = 0

    def dram(self, name, off, dims):
        return bass.AP(self.din[name], off, dims)

    def alloc(self, st):
        nc = self.nc
        sb = lambda n, s, d: st.enter_context(nc.sbuf_tensor(n, s, d))
        self.h = sb("h", [128, NT, D], F32)
        self.hnT = sb("hnT", [128, DC, T], BF16)
        self.arB = sb("arB", [128, HALF * T], BF16)
        self.wout = sb("wout", [128, HALF * D], BF16)
        self.ring = sb("ring", [128, 4, 2048], BF16)
        self.gb = sb("gb", [128, 1, D], F32)
        self.hntok = sb("hntok", [128, 2, D], BF16)
        self.sqj = sb("sqj", [128, D], BF16)
        self.stats = sb("stats", [128, 80], F32)
        self.stage = sb("stage", [128, 2, 512], F32)
        self.scr = sb("scr", [128, 3072], BF16)
        self.ident = sb("ident", [128, 128], BF16)
        self.cst = sb("cst", [128, 256], BF16)
        self.psall = st.enter_context(nc.psum_tensor("psall", [128, 8, 512], F32))
        self.ps = [self.psall[:, i, :] for i in range(8)]
        self.actT = self.arB[:, :].rearrange("p (f t) -> p f t", t=T)

    def load_inputs(self):
        P, h = self.P, self.h
        P.op("pool", lambda e: e.dma_start(out=self.ident[:, :], in_=self.din["c_ident"].ap()),
             writes=["ident"], dma="const_1")
        P.op("dve", lambda e: e.memset(h[:, 0, :], 0.0), writes=[("h", 0)])
        P.op("sp", lambda e: e.dma_start(out=h[NPAD:128, 0, :], in_=self.din["meta"].ap()),
             writes=[("h", 0)], dma="x0")
        xv = self.din["x"].ap().rearrange("(j p) d -> p j d", p=128)
        for j in range(4):
            P.op("sp", lambda e, j=j: e.dma_start(out=h[:, 1 + 4 * j:5 + 4 * j, :], in_=xv[:, 4 * j:4 * j + 4, :]),
                 writes=[("h", 1 + 4 * j + i) for i in range(4)], dma=f"x{j + 1}")

    def load_gain(self, name, l):
        b = 0
        self.P.op("sp", lambda e: e.dma_start(out=self.gb[:, b, :], in_=self.dram(name, l * D, [[0, 128], [1, D]])),
                  writes=[("gb", b)], dma=f"gb{b}")
        return b

    def norm_front(self, tt, g):
        P, h = self.P, self.h
        ss = self.stats[:, tt:tt + 1]
        sd = self.stats[:, 20 + tt:21 + tt]
        rs = self.stats[:, 40 + tt:41 + tt]
        P.op("act", lambda e: e.activation(out=self.sqj[:, :], in_=h[:, tt, :], func=AF.Square, accum_out=ss),
             reads=[("h", tt)], writes=["sqj", ("ss", tt)], noembed=True)
        P.op("act", lambda e: e.activation(out=sd, in_=ss, func=AF.Sqrt, scale=1.0 / D, bias=self.eps_ap),
             reads=[("ss", tt), "eps"], writes=[("sd", tt)])
        P.op("dve", lambda e: e.reciprocal(out=rs, in_=sd), reads=[("sd", tt)], writes=[("rs", tt)])
        b = tt % 2
        P.op("dve", lambda e: e.scalar_tensor_tensor(out=self.hntok[:, b, :], in0=h[:, tt, :], scalar=rs,
                                                     in1=self.gb[:, g, :], op0=ALU.mult, op1=ALU.mult),
             reads=[("h", tt), ("rs", tt), ("gb", g)], writes=[("hntok", b)])

    def norm_back(self, tt):
        P = self.P
        b = tt % 2
        pb = 0 if self.psT_i == 0 else 2
        self.psT_i ^= 1
        psT = self.ps[pb].bitcast(BF16)
        for dc in range(DC):
            P.op("pe", lambda e, dc=dc: e.transpose(out=psT[:, dc * 128:(dc + 1) * 128],
                                                    in_=self.hntok[:, b, dc * 128:(dc + 1) * 128],
                                                    identity=self.ident[:, :]),
                 reads=[("hntok", b), "ident"], writes=[("ps", pb)])
        P.op("act", lambda e: e.copy(out=self.hnT[:, :, tt * 128:(tt + 1) * 128],
                                     in_=psT.rearrange("p (c t) -> p c t", c=DC)),
             reads=[("ps", pb)], writes=[("hnT", tt)])

    def norm_tile(self, tt, g):
        self.norm_front(tt, g)
        self.norm_back(tt)

    def ffn(self, l, which, after_tile=None):
        P = self.P
        w_in, w_out = f"ffn{which}_w_in", f"ffn{which}_w_out"
        actT, ring, wout, hnT, ps = self.actT, self.ring, self.wout, self.hnT, self.ps
        woutv = wout[:, :].rearrange("p (f d) -> p f d", d=D)
        self.fence_arB()
        for half in range(2):
            for fcl in range(HALF):
                fc = half * HALF + fcl
                slot = self.ring_i
                self.ring_i = (self.ring_i + 1) % 4
                for part in range(2):
                    src = self.dram(w_in, l * D * 2 * DFF + part * DFF + fc * 128,
                                    [[2 * DFF, 128], [128 * 2 * DFF, DC], [1, 128]])
                    P.op("pool", lambda e, slot=slot, part=part, src=src: e.dma_start(
                        out=ring[:, slot, part * 1024:(part + 1) * 1024].rearrange("p (c f) -> p c f", f=128), in_=src),
                        writes=[("ring", slot, part)], dma=f"ring{slot}p{part}")
                src = self.dram(w_out, l * DFF * D + fc * 128 * D, [[D, 128], [1, D]])
                P.op("pool", lambda e, fcl=fcl, src=src: e.dma_start(out=woutv[:, fcl, :], in_=src),
                     writes=[("wout", fcl)], dma=f"wout{fcl}")
                for (c0, c1) in TG:
                    n = c1 - c0
                    tl = list(tiles_of(c0, c1))
                    pg = 0 if self.mm1_i == 0 else 2
                    pu = pg + 1
                    sgb = self.mm1_i
                    self.mm1_i ^= 1
                    for part, pbank in ((0, pg), (1, pu)):
                        for dc in range(DC):
                            P.op("pe", lambda e, dc=dc, part=part, pbank=pbank, slot=slot, c0=c0, c1=c1, n=n: e.matmul(
                                ps[pbank][:, 0:n], lhsT=ring[:, slot, part * 1024 + dc * 128:part * 1024 + (dc + 1) * 128],
                                rhs=hnT[:, dc, c0:c1], start=(dc == 0), stop=(dc == DC - 1)),
                                reads=[("ring", slot, part)] + [("hnT", t) for t in tl], writes=[("ps", pbank)])
                    sg = self.stage[:, sgb, 0:n]
                    P.op("act", lambda e, pg=pg, n=n, sg=sg: e.activation(out=sg, in_=ps[pg][:, 0:n], func=AF.Silu),
                         reads=[("ps", pg)], writes=[("stage", sgb)])
                    P.op("dve", lambda e, pu=pu, n=n, sg=sg, fcl=fcl, c0=c0, c1=c1: e.tensor_tensor(
                        out=actT[:, fcl, c0:c1], in0=sg, in1=ps[pu][:, 0:n], op=ALU.mult),
                        reads=[("stage", sgb), ("ps", pu)], writes=[("actT", fcl, t) for t in tl])
            for tt in range(NT):
                pa = 4 if self.mm2_i == 0 else 6
                self.mm2_i ^= 1
                for dh in range(2):
                    for fcl in range(HALF):
                        P.op("pe", lambda e, dh=dh, fcl=fcl, tt=tt, pa=pa: e.matmul(
                            ps[pa + dh][:, :], lhsT=actT[:, fcl, tt * 128:(tt + 1) * 128],
                            rhs=woutv[:, fcl, dh * 512:(dh + 1) * 512], start=(fcl == 0), stop=(fcl == HALF - 1)),
                            reads=[("actT", fcl, tt), ("wout", fcl)], writes=[("ps", pa + dh)])
                for dh in range(2):
                    hs = self.h[:, tt, dh * 512:(dh + 1) * 512]
                    P.op("dve", lambda e, dh=dh, pa=pa, hs=hs: e.scalar_tensor_tensor(
                        out=hs, in0=ps[pa + dh][:, :], scalar=0.5, in1=hs, op0=ALU.mult, op1=ALU.add),
                        reads=[("ps", pa + dh), ("h", tt)], writes=[("h", tt)])
                if half == 1 and after_tile is not None:
                    after_tile(tt)


    def fence_arB(self, extra_writes=()):
        keys = [("actT", f, 0) for f in range(HALF)] + [("oT", c, 0) for c in range(8)] + ["qT", "kT", "vh"]
        self.P.op("pool", lambda e: e.memset(self.actT[:, :, 0:NPAD], 0.0), writes=keys + list(extra_writes))

    def diff_attn(self, l, after_tile=None):
        P, ps, hnT, ring = self.P, self.ps, self.hnT, self.ring
        lam_init = 0.8 - 0.6 * math.exp(-0.3 * l)
        oT = self.arB[:, 0:8 * T].rearrange("p (c t) -> p c t", t=T)
        qT = self.arB[:, 8 * T:9 * T]
        kT = self.arB[:, 9 * T:10 * T]
        vh = self.arB[:, 10 * T:11 * T].rearrange("p (j v) -> p j v", v=128)
        woutv = self.wout[:, :].rearrange("p (f d) -> p f d", d=D)
        pT = self.scr[:, 0:2048].rearrange("p (s m n) -> p s m n", s=2, m=2)
        sqs = self.scr[:, 2048:2560]
        ones = self.cst[:, 0:128]
        valid = self.cst[:, 128:256]
        st = self.stats
        TGA = [(0, 512), (512, 1024), (1024, 1536), (1536, 2048), (2048, 2176)]
        self.fence_arB()
        P.op("pool", lambda e: e.memset(ones, 1.0), writes=["ones"])
        P.op("pool", lambda e: e.dma_start(out=valid, in_=self.din["c_valid"].ap()), writes=["valid"], dma="const_2")
        lamb = self.sqj.bitcast(F32)[:, 0:256]
        P.op("sp", lambda e: e.dma_start(out=lamb, in_=self.dram("da_lambda", 0, [[0, 128], [1, 256]])),
             writes=["sqj"], dma="misc_3")
        pr = self.sqj.bitcast(F32)[:, 256:384]
        P.op("dve", lambda e: e.tensor_tensor(out=pr[:, 0:64], in0=lamb[:, 0:64], in1=lamb[:, 64:128], op=ALU.mult),
             reads=["sqj"], writes=["lam_p1"])
        P.op("dve", lambda e: e.tensor_tensor(out=pr[:, 64:128], in0=lamb[:, 128:192], in1=lamb[:, 192:256], op=ALU.mult),
             reads=["sqj"], writes=["lam_p2"])
        P.op("dve", lambda e: e.reduce_sum(out=st[:, 58:60], in_=pr.rearrange("p (a b) -> p a b", a=2),
                                           axis=mybir.AxisListType.X), reads=["lam_p1", "lam_p2"], writes=["lam_s"])
        P.op("act", lambda e: e.activation(out=st[:, 60:62], in_=st[:, 58:60], func=AF.Exp), reads=["lam_s"], writes=["lam_e"])
        P.op("dve", lambda e: e.tensor_tensor(out=st[:, 62:63], in0=st[:, 61:62], in1=st[:, 60:61], op=ALU.subtract),
             reads=["lam_e"], writes=["lam_d"])
        nlam = st[:, 57:58]
        P.op("dve", lambda e: e.tensor_scalar(out=nlam, in0=st[:, 62:63], scalar1=-lam_init, scalar2=None, op0=ALU.add),
             reads=["lam_d"], writes=["nlam"])
        gcol = st[:, 65:66]
        P.op("sp", lambda e: e.dma_start(out=st[:, 64:65], in_=self.dram("da_subln", 0, [[1, 128], [1, 1]])),
             writes=["gcol_raw"], dma="misc_4")
        P.op("dve", lambda e: e.tensor_scalar(out=gcol, in0=st[:, 64:65], scalar1=(1.0 - lam_init) / math.sqrt(EPS), scalar2=None, op0=ALU.mult),
             reads=["gcol_raw"], writes=["gcol"])
        for c in range(8):
            src = self.dram("da_w_out", c * 128 * D, [[D, 128], [1, D]])
            P.op("pool", lambda e, c=c, src=src: e.dma_start(out=woutv[:, c, :], in_=src),
                 writes=[("wout", c)], dma=f"wout{c}")
        WROW = 3072
        pbank = [0]

        def nextbank():
            b = pbank[0]
            pbank[0] = (b + 1) % 4
            return b

        for hd in range(8):
            sa = self.ring_i
            self.ring_i = (self.ring_i + 1) % 4
            sb_ = self.ring_i
            self.ring_i = (self.ring_i + 1) % 4
            for part, col0 in ((0, hd * 128), (1, 1024 + hd * 128)):
                src = self.dram("da_w_in", col0, [[WROW, 128], [128 * WROW, DC], [1, 128]])
                P.op("pool", lambda e, part=part, src=src, sa=sa: e.dma_start(
                    out=ring[:, sa, part * 1024:(part + 1) * 1024].rearrange("p (c f) -> p c f", f=128), in_=src),
                    writes=[("ring", sa, part)], dma=f"ring{sa}p{part}")
            src = self.dram("da_w_in", 2048 + hd * 128, [[WROW, 128], [128 * WROW, DC], [1, 128]])
            P.op("pool", lambda e, src=src, sb_=sb_: e.dma_start(
                out=ring[:, sb_, 0:1024].rearrange("p (c f) -> p c f", f=128), in_=src),
                writes=[("ring", sb_, 0)], dma=f"ring{sb_}p0")
            for part, dst, key in ((1, kT, "kT"), (0, qT, "qT")):
                for (c0, c1) in TGA:
                    n = c1 - c0
                    b = nextbank()
                    for dc in range(DC):
                        P.op("pe", lambda e, dc=dc, part=part, b=b, c0=c0, c1=c1, n=n, sa=sa: e.matmul(
                            ps[b][:, 0:n], lhsT=ring[:, sa, part * 1024 + dc * 128:part * 1024 + (dc + 1) * 128],
                            rhs=hnT[:, dc, c0:c1], start=(dc == 0), stop=(dc == DC - 1)),
                            reads=[("ring", sa, part)] + [("hnT", t) for t in tiles_of(c0, c1)], writes=[("ps", b)])
                    if part == 0:
                        P.op("act", lambda e, b=b, n=n, c0=c0, c1=c1: e.activation(
                            out=qT[:, c0:c1], in_=ps[b][:, 0:n], func=AF.Copy, scale=0.125),
                            reads=[("ps", b)], writes=["qT"])
                    else:
                        P.op("dve", lambda e, b=b, n=n, c0=c0, c1=c1: e.tensor_copy(out=kT[:, c0:c1], in_=ps[b][:, 0:n]),
                             reads=[("ps", b)], writes=["kT"])
            for t0 in range(0, NT, 4):
                tl = list(range(t0, min(t0 + 4, NT)))
                b = nextbank()
                for j, tt in enumerate(tl):
                    for dc in range(DC):
                        P.op("pe", lambda e, dc=dc, b=b, j=j, tt=tt, sb_=sb_: e.matmul(
                            ps[b][:, j * 128:(j + 1) * 128], lhsT=hnT[:, dc, tt * 128:(tt + 1) * 128],
                            rhs=ring[:, sb_, dc * 128:(dc + 1) * 128], start=(dc == 0), stop=(dc == DC - 1)),
                            reads=[("ring", sb_, 0), ("hnT", tt)], writes=[("ps", b)])
                nn = len(tl)
                P.op("dve", lambda e, b=b, t0=t0, nn=nn: e.tensor_copy(
                    out=vh[:, t0:t0 + nn, :], in_=ps[b][:, 0:nn * 128].rearrange("p (j v) -> p j v", v=128)),
                    reads=[("ps", b)], writes=["vh"])
            pairs = []
            for g in range(5):
                q0 = max(g * 512, NPAD)
                q1 = min((g + 1) * 512, T)
                nkb = q1 // 128
                for kb in range(nkb):
                    pairs.append((g, kb, q0, q1, nkb))

            def issue_S(i):
                g, kb, q0, q1, nkb = pairs[i]
                sset = i % 2
                c0 = max(kb * 128, q0)
                n = q1 - c0
                for m in range(2):
                    P.op("pe", lambda e, m=m, sset=sset, kb=kb, c0=c0, q1=q1, n=n: e.matmul(
                        ps[2 * sset + m][:, 0:n], lhsT=kT[m * 64:(m + 1) * 64, kb * 128:(kb + 1) * 128],
                        rhs=qT[m * 64:(m + 1) * 64, c0:q1], start=True, stop=True),
                        reads=["kT", "qT"], writes=[("ps", 2 * sset + m)])
                P.op("act", lambda e, sset=sset, n=n: e.activation(
                    out=pT[:, sset, :, 0:n], in_=self.psall[:, 2 * sset:2 * sset + 2, 0:n], func=AF.Exp),
                    reads=[("ps", 2 * sset), ("ps", 2 * sset + 1)], writes=[("pT", sset)])
                if kb >= 1 and kb * 128 >= q0:
                    P.op("pool", lambda e, sset=sset: e.memset(pT[64:128, sset, :, 0:64], 0.0), writes=[("pT", sset)])

            def issue_PV(i):
                g, kb, q0, q1, nkb = pairs[i]
                sset = i % 2
                c0 = max(kb * 128, q0)
                n = q1 - c0
                off = c0 - q0
                den_l = valid if kb == 0 else ones
                den_key = "valid" if kb == 0 else "ones"
                diag = kb >= 1 and kb * 128 >= q0
                segs = [(0, n, 128)]
                for which in range(2):
                    for m in range(2):
                        bank = (4 if which == 0 else 6) + m
                        for (a0, a1, kk) in segs:
                            if a1 <= a0:
                                continue
                            lhs = vh[0:kk, kb, :] if which == 0 else den_l[0:kk, :]
                            P.op("pe", lambda e, m=m, sset=sset, kb=kb, off=off, nkb=nkb, bank=bank, a0=a0, a1=a1, kk=kk, lhs=lhs:
                                 e.matmul(ps[bank][:, off + a0:off + a1], lhsT=lhs, rhs=pT[0:kk, sset, m, a0:a1],
                                          start=(kb == 0), stop=(kb == nkb - 1), skip_group_check=True),
                                 reads=["vh" if which == 0 else den_key, ("pT", sset)], writes=[("ps", bank)])

            def issue_evac_a(i, hd=hd):
                g, kb, q0, q1, nkb = pairs[i]
                nq = q1 - q0
                d0s = self.hntok[:, 0, :].bitcast(F32)[:, 0:nq]
                d1s = self.hntok[:, 1, :].bitcast(F32)[:, 0:nq]
                o1s = self.sqj.bitcast(F32)[:, 0:nq]
                wv = self.stage[:, 1, 0:nq]
                t0 = self.stage[:, 0, 0:nq]
                P.op("act", lambda e: e.copy(out=d1s, in_=ps[7][:, 0:nq]), reads=[("ps", 7)], writes=[("hntok", 1)])
                P.op("act", lambda e: e.copy(out=d0s, in_=ps[6][:, 0:nq]), reads=[("ps", 6)], writes=[("hntok", 0)])
                P.op("dve", lambda e: e.tensor_copy(out=o1s, in_=ps[5][:, 0:nq]), reads=[("ps", 5)], writes=["sqj"])
                P.op("dve", lambda e: e.tensor_tensor(out=t0, in0=ps[4][:, 0:nq], in1=d1s, op=ALU.mult),
                     reads=[("ps", 4), ("hntok", 1)], writes=[("stage", 0)])
                P.op("dve", lambda e: e.tensor_tensor(out=wv, in0=o1s, in1=d0s, op=ALU.mult),
                     reads=["sqj", ("hntok", 0)], writes=[("stage", 1)])
                P.op("dve", lambda e: e.scalar_tensor_tensor(out=wv, in0=wv, scalar=nlam, in1=t0, op0=ALU.mult, op1=ALU.add),
                     reads=[("stage", 0), ("stage", 1), "nlam"], writes=[("stage", 1)])
                P.op("pool", lambda e: e.tensor_tensor(out=d0s, in0=d0s, in1=d1s, op=ALU.mult),
                     reads=[("hntok", 0), ("hntok", 1)], writes=[("hntok", 0)])
                P.op("pool", lambda e: e.tensor_tensor(out=d0s, in0=d0s, in1=d0s, op=ALU.mult),
                     reads=[("hntok", 0)], writes=[("hntok", 0)])
                P.op("pool", lambda e: e.tensor_tensor(out=sqs[:, 0:nq], in0=wv, in1=wv, op=ALU.mult),
                     reads=[("stage", 1)], writes=["sqs"])

            def issue_evac_b(i, hd=hd):
                g, kb, q0, q1, nkb = pairs[i]
                nq = q1 - q0
                tl = list(tiles_of(q0, q1))
                d0s = self.hntok[:, 0, :].bitcast(F32)[:, 0:nq]
                wv = self.stage[:, 1, 0:nq]
                t0 = self.stage[:, 0, 0:nq]
                b = nextbank()
                P.op("pe", lambda e: e.matmul(ps[b][:, 0:nq], lhsT=ones, rhs=sqs[:, 0:nq], start=True, stop=True),
                     reads=["ones", "sqs"], writes=[("ps", b)])
                P.op("dve", lambda e: e.tensor_scalar(out=t0, in0=ps[b][:, 0:nq], scalar1=1.0 / (128 * EPS), scalar2=None, op0=ALU.mult),
                     reads=[("ps", b)], writes=[("stage", 0)])
                P.op("dve", lambda e: e.tensor_tensor(out=t0, in0=t0, in1=d0s, op=ALU.add),
                     reads=[("stage", 0), ("hntok", 0)], writes=[("stage", 0)])
                P.op("act", lambda e: e.activation(out=t0, in_=t0, func=AF.Ln), reads=[("stage", 0)], writes=[("stage", 0)])
                P.op("act", lambda e: e.activation(out=t0, in_=t0, func=AF.Exp, scale=-0.5), reads=[("stage", 0)], writes=[("stage", 0)])
                P.op("dve", lambda e: e.scalar_tensor_tensor(out=oT[:, hd, q0:q1], in0=wv, scalar=gcol, in1=t0,
                                                             op0=ALU.mult, op1=ALU.mult),
                     reads=[("stage", 0), ("stage", 1), "gcol"], writes=[("oT", hd, t) for t in tl])

            issue_S(0)
            pend = None
            for i in range(len(pairs)):
                if i + 1 < len(pairs):
                    issue_S(i + 1)
                issue_PV(i)
                if pend is not None and i >= pend + 3:
                    issue_evac_b(pend)
                    pend = None
                if pairs[i][1] == pairs[i][4] - 1:
                    if pend is not None:
                        issue_evac_b(pend)
                    issue_evac_a(i)
                    pend = i
            if pend is not None:
                issue_evac_b(pend)
        for tt in range(NT):
            pa = 4 if self.mm2_i == 0 else 6
            self.mm2_i ^= 1
            for dh in range(2):
                for c in range(8):
                    P.op("pe", lambda e, dh=dh, c=c, tt=tt, pa=pa: e.matmul(
                        ps[pa + dh][:, :], lhsT=oT[:, c, tt * 128:(tt + 1) * 128],
                        rhs=woutv[:, c, dh * 512:(dh + 1) * 512], start=(c == 0), stop=(c == 7)),
                        reads=[("oT", c, tt), ("wout", c)], writes=[("ps", pa + dh)])
            for dh in range(2):
                hs = self.h[:, tt, dh * 512:(dh + 1) * 512]
                P.op("dve", lambda e, dh=dh, pa=pa, hs=hs: e.tensor_tensor(out=hs, in0=ps[pa + dh][:, :], in1=hs, op=ALU.add),
                     reads=[("ps", pa + dh), ("h", tt)], writes=[("h", tt)])
            if after_tile is not None:
                after_tile(tt)


    def gla(self, l, after_tile=None):
        P, ps, hnT, ring = self.P, self.ps, self.hnT, self.ring
        oT = self.arB[:, 0:8 * T].rearrange("p (c t) -> p c t", t=T)
        qT = self.arB[:, 8 * T:9 * T]
        kdec = self.arB[:, 9 * T:10 * T].rearrange("p (j v) -> p j v", v=128)
        woutv = self.wout[:, :].rearrange("p (f d) -> p f d", d=D)
        vh = self.wout[:, 0:NT * 256].rearrange("p (j v) -> p j v", v=256)
        VHK = [("wout", f) for f in range(5)]
        S32 = self.wout[:, 5120:6144].bitcast(F32).rearrange("p (s v) -> p s v", s=2)
        Sb = self.wout[:, 6144:6912].rearrange("p (s v) -> p s v", s=3)
        glrT = self.scr[0:32, 0:T]
        wglr = self.scr[:, T:T + 128].rearrange("p (c r) -> p c r", r=16)
        sqs = self.scr[:, 2304:2816]
        wg2 = self.hntok[0:17, 0, 0:512]
        gbf = self.gb[:, 0, :]
        tri = gbf[:, 0:128]
        cind = gbf[:, 128:130]
        decay = gbf[:, 130:130 + 4 * 34].rearrange("p (h n) -> p h n", h=4)
        ones = self.cst[:, 0:128]
        st = self.stats
        one_col = st[:, 69:70]
        gcol = st[:, 66:68]
        stage = self.stage
        WROW = 3088
        TGA = [(0, 512), (512, 1024), (1024, 1536), (1536, 2048), (2048, 2176)]
        GB = ("gb", 0)
        self.fence_arB()
        P.op("pool", lambda e: e.memset(ones, 1.0), writes=["ones"])
        P.op("dve", lambda e: e.memset(one_col, 1.0), writes=["one_col"])
        P.op("dve", lambda e: e.memset(Sb[:, :, :], 0.0), writes=[("Sb", 0), ("Sb", 1), ("Sb", 2), ("wout", 6)])
        P.op("sp", lambda e: e.dma_start(out=tri, in_=self.din["c_tri"].ap()), writes=["tri", GB], dma="misc_5")
        P.op("sp", lambda e: e.dma_start(out=cind, in_=self.din["c_cind"].ap()), writes=["cind", GB], dma="misc_6")
        for a in range(2):
            P.op("sp", lambda e, a=a: e.dma_start(out=gcol[:, a:a + 1], in_=self.dram("gla_norm", a * 128, [[1, 128], [1, 1]])),
                 writes=["gcolg"], dma="misc_7")
        P.op("pool", lambda e: e.memset(glrT, 1.0), writes=["glrT", ("pT", 0), ("pT", 1), "sqs"])
        P.op("pool", lambda e: e.dma_start(out=wglr, in_=self.dram("gla_w_in", 3072, [[WROW, 128], [128 * WROW, DC], [1, 16]])),
             writes=["wglr", ("pT", 0), ("pT", 1), "sqs"], dma="const_8")
        P.op("pool", lambda e: e.dma_start(out=wg2[0:16, :], in_=self.din["gla_w_gate2"].ap()[0]),
             writes=[("hntok", 0)], dma="const_9")
        P.op("pool", lambda e: e.dma_start(out=wg2[16:17, :], in_=self.din["gla_b_gate2"].ap()),
             writes=[("hntok", 0)], dma="const_10")
        pbank = [0]

        def nextbank():
            b = pbank[0]
            pbank[0] = (b + 1) % 4
            return b

        for (c0, c1) in TGA:
            n = c1 - c0
            b = nextbank()
            for dc in range(DC):
                P.op("pe", lambda e, dc=dc, b=b, c0=c0, c1=c1, n=n: e.matmul(
                    ps[b][0:16, 0:n], lhsT=wglr[:, dc, :], rhs=hnT[:, dc, c0:c1], start=(dc == 0), stop=(dc == DC - 1)),
                    reads=["wglr"] + [("hnT", t) for t in tiles_of(c0, c1)], writes=[("ps", b)])
            P.op("dve", lambda e, b=b, n=n, c0=c0, c1=c1: e.tensor_copy(out=glrT[0:16, c0:c1], in_=ps[b][0:16, 0:n]),
                 reads=[("ps", b)], writes=["glrT"])
        for hd in range(4):
            sa = self.ring_i
            self.ring_i = (self.ring_i + 1) % 4
            sb_ = self.ring_i
            self.ring_i = (self.ring_i + 1) % 4
            for part, col0 in ((0, hd * 128), (1, 512 + hd * 128)):
                src = self.dram("gla_w_in", col0, [[WROW, 128], [128 * WROW, DC], [1, 128]])
                P.op("pool", lambda e, part=part, src=src, sa=sa: e.dma_start(
                    out=ring[:, sa, part * 1024:(part + 1) * 1024].rearrange("p (c f) -> p c f", f=128), in_=src),
                    writes=[("ring", sa, part)], dma=f"ring{sa}p{part}")
            src = self.dram("gla_w_in", 1024 + hd * 256, [[WROW, 128], [128 * WROW, DC], [1, 256]])
            P.op("pool", lambda e, src=src, sb_=sb_: e.dma_start(
                out=ring[:, sb_, :].rearrange("p (c f) -> p c f", f=256), in_=src),
                writes=[("ring", sb_, 0), ("ring", sb_, 1)], dma=f"ring{sb_}p0")
            for (c0, c1) in TGA:
                n = c1 - c0
                b = nextbank()
                for dc in range(DC):
                    P.op("pe", lambda e, dc=dc, b=b, c0=c0, c1=c1, n=n, sa=sa: e.matmul(
                        ps[b][:, 0:n], lhsT=ring[:, sa, dc * 128:(dc + 1) * 128], rhs=hnT[:, dc, c0:c1],
                        start=(dc == 0), stop=(dc == DC - 1)),
                        reads=[("ring", sa, 0)] + [("hnT", t) for t in tiles_of(c0, c1)], writes=[("ps", b)])
                P.op("act", lambda e, b=b, n=n, c0=c0, c1=c1: e.activation(
                    out=qT[:, c0:c1], in_=ps[b][:, 0:n], func=AF.Copy, scale=128 ** -0.5),
                    reads=[("ps", b)], writes=["qT"])
            def front(tt, hd=hd, sa=sa, sb_=sb_):
                b = tt % 3
                par = tt % 2
                for dc in range(DC):
                    P.op("pe", lambda e, dc=dc: e.matmul(
                        ps[b][:, 0:128], lhsT=hnT[:, dc, tt * 128:(tt + 1) * 128],
                        rhs=ring[:, sa, 1024 + dc * 128:1024 + (dc + 1) * 128], start=(dc == 0), stop=(dc == DC - 1)),
                        reads=[("ring", sa, 1), ("hnT", tt)], writes=[("ps", b)])
                for dc in range(DC):
                    P.op("pe", lambda e, dc=dc: e.matmul(
                        ps[b][:, 128:384], lhsT=hnT[:, dc, tt * 128:(tt + 1) * 128],
                        rhs=ring[:, sb_, dc * 256:(dc + 1) * 256], start=(dc == 0), stop=(dc == DC - 1)),
                        reads=[("ring", sb_, 0), ("ring", sb_, 1), ("hnT", tt)], writes=[("ps", b)])
                P.op("pe", lambda e: e.matmul(
                    ps[b][:, 384:512], lhsT=glrT[0:17, tt * 128:(tt + 1) * 128], rhs=wg2[:, hd * 128:(hd + 1) * 128],
                    start=True, stop=True), reads=["glrT", ("hntok", 0)], writes=[("ps", b)])
                sp_ = stage[:, par, 0:128]
                P.op("act", lambda e: e.activation(out=sp_, in_=ps[b][:, 384:512], func=AF.Exp, scale=-1.0),
                     reads=[("ps", b)], writes=[("stage", par)])
                P.op("act", lambda e: e.activation(out=sp_, in_=sp_, func=AF.Ln, bias=one_col),
                     reads=[("stage", par), "one_col"], writes=[("stage", par)])

            def back(tt, hd=hd):
                b = tt % 3
                par = tt % 2
                b2 = 3
                sp_ = stage[:, par, 0:128]
                eR = stage[:, par, 128:256]
                P.op("pe", lambda e: e.matmul(ps[b2][:, 0:128], lhsT=tri, rhs=sp_, start=True, stop=True),
                     reads=["tri", GB, ("stage", par)], writes=[("ps", b2)])
                P.op("pe", lambda e: e.matmul(ps[b2][:, 128:130], lhsT=sp_, rhs=cind, start=True, stop=True),
                     reads=["cind", GB, ("stage", par)], writes=[("ps", b2)])
                P.op("act", lambda e: e.activation(out=eR, in_=ps[b2][:, 0:128], func=AF.Exp),
                     reads=[("ps", b2)], writes=[("stage_e", par)])
                P.op("act", lambda e: e.activation(out=decay[:, hd, 2 * tt:2 * tt + 2], in_=ps[b2][:, 128:130], func=AF.Exp),
                     reads=[("ps", b2)], writes=["decay"])
                P.op("dve", lambda e: e.tensor_tensor(out=kdec[:, tt, :], in0=ps[b][:, 0:128], in1=eR, op=ALU.mult),
                     reads=[("ps", b), ("stage_e", par)], writes=["kT"])
                P.op("dve", lambda e: e.tensor_copy(out=vh[:, tt, :], in_=ps[b][:, 128:384]),
                     reads=[("ps", b)], writes=VHK)

            for tt in range(NT):
                front(tt)
                if tt >= 1:
                    back(tt - 1)
            back(NT - 1)
            P.op("dve", lambda e: e.memset(S32[:, 1, :], 0.0), writes=[("S32", 1), ("wout", 5)])

            def kvmm(n_):
                tt, hf = n_ // 2, n_ % 2
                kb = 3 + (n_ % 3)
                P.op("pe", lambda e: e.matmul(
                    ps[kb][:, 0:256], lhsT=kdec[hf * 64:(hf + 1) * 64, tt, :], rhs=vh[hf * 64:(hf + 1) * 64, tt, :],
                    start=True, stop=True), reads=["kT"] + VHK, writes=[("ps", kb)])

            def state(n_, hd=hd):
                cur, prev = n_ % 2, 1 - (n_ % 2)
                kb = 3 + (n_ % 3)
                sbi = n_ % 3
                P.op("dve", lambda e: e.scalar_tensor_tensor(
                    out=S32[:, cur, :], in0=S32[:, prev, :], scalar=decay[:, hd, n_:n_ + 1], in1=ps[kb][:, 0:256],
                    op0=ALU.mult, op1=ALU.add),
                    reads=[("S32", prev), "decay", GB, ("ps", kb)], writes=[("S32", cur)])
                P.op("act", lambda e: e.copy(out=Sb[:, sbi, :], in_=S32[:, cur, :]),
                     reads=[("S32", cur)], writes=[("Sb", sbi)])

            def omm(n_, hd=hd):
                sbi = n_ % 3
                grp = n_ // 8
                ob = 6 if grp % 2 == 0 else 0
                for a in range(2):
                    P.op("pe", lambda e, a=a: e.matmul(
                        ps[ob + a][:, (n_ % 8) * 64:(n_ % 8) * 64 + 64], lhsT=Sb[:, sbi, a * 128:(a + 1) * 128],
                        rhs=qT[:, n_ * 64:(n_ + 1) * 64], start=True, stop=True),
                        reads=[("Sb", sbi), "qT"], writes=[("ps", ob + a)])
                if n_ % 8 == 7 or n_ == 33:
                    c0 = grp * 512
                    c1 = (n_ + 1) * 64
                    n = c1 - c0
                    for a in range(2):
                        P.op("dve" if a == 0 else "act", (lambda e, a=a:
                             (e.tensor_copy(out=oT[:, hd * 2 + a, c0:c1], in_=ps[ob + a][:, 0:n]) if a == 0 else
                              e.copy(out=oT[:, hd * 2 + a, c0:c1], in_=ps[ob + a][:, 0:n]))),
                             reads=[("ps", ob + a)], writes=[("oT", hd * 2 + a, t) for t in tiles_of(c0, c1)])

            kvmm(0)
            kvmm(1)
            for n_ in range(34):
                state(n_)
                if n_ + 2 < 34:
                    kvmm(n_ + 2)
                omm(n_)
        extra = [("S32", 0), ("S32", 1), ("Sb", 0), ("Sb", 1), ("Sb", 2)]
        for c in range(8):
            src = self.dram("gla_w_out", c * 128 * D, [[D, 128], [1, D]])
            P.op("pool", lambda e, c=c, src=src: e.dma_start(out=woutv[:, c, :], in_=src),
                 writes=[("wout", c)] + (extra if c in (5, 6) else []), dma=f"wout{c}")
        for hd in range(4):
            for (c0, c1) in TGA:
                n = c1 - c0
                tl = list(tiles_of(c0, c1))
                b = nextbank()
                for a in range(2):
                    c = hd * 2 + a
                    P.op("pool", lambda e, c=c, c0=c0, c1=c1, n=n: e.tensor_tensor(
                        out=sqs[:, 0:n], in0=oT[:, c, c0:c1], in1=oT[:, c, c0:c1], op=ALU.mult),
                        reads=[("oT", c, t) for t in tl], writes=["sqs"])
                    P.op("pe", lambda e, a=a, b=b, n=n: e.matmul(ps[b][:, 0:n], lhsT=ones, rhs=sqs[:, 0:n],
                                                                start=(a == 0), stop=(a == 1)),
                         reads=["ones", "sqs"], writes=[("ps", b)])
                P.op("act", lambda e, b=b, n=n: e.activation(out=stage[:, 0, 0:n], in_=ps[b][:, 0:n], func=AF.Ln,
                                                             scale=1.0 / 256, bias=self.eps_ap),
                     reads=[("ps", b), "eps"], writes=[("stage", 0)])
                P.op("act", lambda e, n=n: e.activation(out=stage[:, 0, 0:n], in_=stage[:, 0, 0:n], func=AF.Exp, scale=-0.5),
                     reads=[("stage", 0)], writes=[("stage", 0)])
                for a in range(2):
                    c = hd * 2 + a
                    P.op("dve", lambda e, a=a, c=c, c0=c0, c1=c1, n=n: e.scalar_tensor_tensor(
                        out=oT[:, c, c0:c1], in0=oT[:, c, c0:c1], scalar=gcol[:, a:a + 1], in1=stage[:, 0, 0:n],
                        op0=ALU.mult, op1=ALU.mult),
                        reads=[("oT", c, t) for t in tl] + [("stage", 0), "gcolg"], writes=[("oT", c, t) for t in tl])
        for c in range(8):
            if c % 2 == 0:
                sg_ = self.ring_i
                self.ring_i = (self.ring_i + 1) % 4
            part = c % 2
            src = self.dram("gla_w_in", 2048 + c * 128, [[WROW, 128], [128 * WROW, DC], [1, 128]])
            P.op("pool", lambda e, part=part, src=src, sg_=sg_: e.dma_start(
                out=ring[:, sg_, part * 1024:(part + 1) * 1024].rearrange("p (c f) -> p c f", f=128), in_=src),
                writes=[("ring", sg_, part)], dma=f"ring{sg_}p{part}")
            for gi, (c0, c1) in enumerate(TGA):
                n = c1 - c0
                tl = list(tiles_of(c0, c1))
                b = nextbank()
                for dc in range(DC):
                    P.op("pe", lambda e, dc=dc, b=b, c0=c0, c1=c1, n=n, sg_=sg_, part=part: e.matmul(
                        ps[b][:, 0:n], lhsT=ring[:, sg_, part * 1024 + dc * 128:part * 1024 + (dc + 1) * 128],
                        rhs=hnT[:, dc, c0:c1], start=(dc == 0), stop=(dc == DC - 1)),
                        reads=[("ring", sg_, part)] + [("hnT", t) for t in tl], writes=[("ps", b)])
                sgb = gi % 2
                P.op("act", lambda e, b=b, n=n, sgb=sgb: e.activation(out=stage[:, sgb, 0:n], in_=ps[b][:, 0:n], func=AF.Silu),
                     reads=[("ps", b)], writes=[("stage", sgb)])
                P.op("pool", lambda e, c=c, c0=c0, c1=c1, n=n, sgb=sgb: e.tensor_tensor(
                    out=oT[:, c, c0:c1], in0=oT[:, c, c0:c1], in1=stage[:, sgb, 0:n], op=ALU.mult),
                    reads=[("oT", c, t) for t in tl] + [("stage", sgb)], writes=[("oT", c, t) for t in tl])
        for tt in range(NT):
            pa = 4 if self.mm2_i == 0 else 6
            self.mm2_i ^= 1
            for dh in range(2):
                for c in range(8):
                    P.op("pe", lambda e, dh=dh, c=c, tt=tt, pa=pa: e.matmul(
                        ps[pa + dh][:, :], lhsT=oT[:, c, tt * 128:(tt + 1) * 128],
                        rhs=woutv[:, c, dh * 512:(dh + 1) * 512], start=(c == 0), stop=(c == 7)),
                        reads=[("oT", c, tt), ("wout", c)], writes=[("ps", pa + dh)])
            for dh in range(2):
                hs = self.h[:, tt, dh * 512:(dh + 1) * 512]
                P.op("dve", lambda e, dh=dh, pa=pa, hs=hs: e.tensor_tensor(out=hs, in0=ps[pa + dh][:, :], in1=hs, op=ALU.add),
                     reads=[("ps", pa + dh), ("h", tt)], writes=[("h", tt)])
            if after_tile is not None:
                after_tile(tt)

    def final(self, tt, g):
        P, h = self.P, self.h
        ss = self.stats[:, tt:tt + 1]
        sd = self.stats[:, 20 + tt:21 + tt]
        rs = self.stats[:, 40 + tt:41 + tt]
        P.op("act", lambda e: e.activation(out=self.sqj[:, :], in_=h[:, tt, :], func=AF.Square, accum_out=ss),
             reads=[("h", tt)], writes=["sqj", ("ss", tt)], noembed=True)
        P.op("act", lambda e: e.activation(out=sd, in_=ss, func=AF.Sqrt, scale=1.0 / D, bias=self.eps_ap),
             reads=[("ss", tt), "eps"], writes=[("sd", tt)])
        P.op("dve", lambda e: e.reciprocal(out=rs, in_=sd), reads=[("sd", tt)], writes=[("rs", tt)])
        b = tt % 2
        P.op("dve", lambda e: e.scalar_tensor_tensor(out=h[:, tt, :], in0=h[:, tt, :], scalar=rs,
                                                     in1=self.gb[:, g, :], op0=ALU.mult, op1=ALU.mult),
             reads=[("h", tt), ("rs", tt), ("gb", g)], writes=[("h", tt)])
        P.op("sp", lambda e: e.dma_start(out=self.dout.ap()[(tt - 1) * 128:tt * 128, :], in_=h[:, tt, :]),
             reads=[("h", tt)], dma=f"out{tt}")

    def dump_h(self):
        hv = self.dout.ap().rearrange("(j p) d -> p j d", p=128)
        self.P.op("sp", lambda e: e.dma_start(out=hv, in_=self.h[:, :, :]),
                  reads=[("h", t) for t in range(NT)], dma="out")
        keys = [("actT", f, t) for f in range(HALF) for t in range(NT)] + [("oT", c, t) for c in range(8) for t in range(NT)] + ["qT", "kT", "vh"]
        if getattr(self, "dbg_src", "arB") == "hnT":
            for f in range(DC):
                self.P.op("pool", lambda e, f=f: e.dma_start(out=self.ddbg.ap()[:, f * T:(f + 1) * T], in_=self.hnT[:, f, :]),
                          reads=[("hnT", t) for t in range(NT)], dma="out")
            self.P.op("pool", lambda e: e.dma_start(out=self.ddbg.ap()[:, 8 * T:8 * T + 80], in_=self.stats[:, 0:80]),
                      reads=[("rs", t) for t in range(NT)], dma="out")
            self.P.op("pool", lambda e: e.dma_start(out=self.ddbg.ap()[:, 9 * T:9 * T + 1024], in_=self.gb[:, 0, :]),
                      reads=[("gb", 0)], dma="out")
            return
        for f in range(HALF):
            self.P.op("pool", lambda e, f=f: e.dma_start(out=self.ddbg.ap()[:, f * T:(f + 1) * T], in_=self.arB[:, f * T:(f + 1) * T]),
                      reads=keys, dma="out")

    def build(self):
        nc = self.nc
        st = contextlib.ExitStack()
        with st:
            self.alloc(st)
            P = self.P
            self.eps_ap = self.stats[:, 63:64]
            P.op("dve", lambda e: e.memset(self.eps_ap, EPS), writes=["eps"])
            self.load_inputs()
            subs = [("ffn", 0, 1), ("da", 0, 0), ("ffn", 0, 2), ("ffn", 1, 1), ("gla", 1, 0), ("ffn", 1, 2)]
            norm_name = {("ffn", 1): "ffn1_norm", ("ffn", 2): "ffn2_norm", ("da", 0): "mix_norm", ("gla", 0): "mix_norm"}
            subs = subs[:self.n_sub]
            for si, (kind, l, which) in enumerate(subs):
                if si == 0:
                    g0 = self.load_gain(norm_name[(kind, which)], l)
                    for tt in range(NT):
                        self.norm_tile(tt, g0)
                if getattr(self, "skip_last_body", False) and si == len(subs) - 1:
                    self.dbg_src = "hnT"
                    break
                if si + 1 < len(subs):
                    nk, nl, nw = subs[si + 1]

                    def cb(tt, nk=nk, nl=nl, nw=nw):
                        if tt == 0:
                            self.load_gain(norm_name[(nk, nw)], nl)
                        self.norm_front(tt, 0)
                        if tt >= 1:
                            self.norm_back(tt - 1)
                        if tt == NT - 1:
                            self.norm_back(tt)
                elif not self.debug:
                    def cb(tt):
                        if tt == 0:
                            self.load_gain("final_norm", 0)
                        else:
                            self.final(tt, 0)
                else:
                    cb = None
                if kind == "ffn":
                    self.ffn(l, which, after_tile=cb)
                elif kind == "da":
                    self.diff_attn(l, after_tile=cb)
                else:
                    self.gla(l, after_tile=cb)
            if self.debug:
                self.dump_h()
            P.emit(final_wait_dma="out")
        return nc


_CACHE = {}


def kernel(**inputs):
    n = 8
    if "nc" not in _CACHE:
        _CACHE["nc"] = Builder().build()
    nc = _CACHE["nc"]
    consts = host_consts()
    shared = {}
    for name, shape in IN_SPECS:
        if name == "x" or name.startswith("c_"):
            continue
        shared[name] = np.ascontiguousarray(np.asarray(inputs[name], dtype=np.float32).reshape(shape))
    x = np.asarray(inputs["x"], dtype=np.float32)
    in_maps = []
    for c in range(n):
        m = dict(shared)
        m.update(consts)
        m["x"] = np.ascontiguousarray(x[c])
        in_maps.append(m)
    res = run_bass_kernel_spmd(nc, in_maps, core_ids=list(range(n)))
    return np.stack([np.asarray(r["out"], dtype=np.float32) for r in res.results], axis=0)
```

```python
import contextlib
import math

import numpy as np
import concourse.bass as bass
import concourse.mybir as mybir
from concourse.bass_utils import run_bass_kernel_spmd
from concourse.alu_op_type import AluOpType as ALU

F32 = mybir.dt.float32
BF16 = mybir.dt.bfloat16
AF = mybir.ActivationFunctionType

ENG_ATTR = {"pe": "tensor", "act": "scalar", "dve": "vector", "pool": "gpsimd", "sp": "sync"}

T = 2176
NT = 17
D = 1024
DC = 8
DFF = 2816
NFC = 22
HALF = 11
NPAD = 112
TG = [(112, 128), (128, 640), (640, 1152), (1152, 1664), (1664, 2176)]
EPS = 1e-6


def tiles_of(c0, c1):
    return range(c0 // 128, (c1 + 127) // 128)


class Prog:
    def __init__(self, nc):
        self.nc = nc
        self.instrs = []
        self.last_writer = {}
        self.readers = {}

    def op(self, eng, fn, reads=(), writes=(), dma=None, noembed=False):
        idx = len(self.instrs)
        deps = {}
        for r in reads:
            w = self.last_writer.get(r)
            if w is not None:
                deps[w] = "raw"
        for wr in writes:
            w = self.last_writer.get(wr)
            if w is not None and w not in deps:
                deps[w] = "waw"
            for rd in self.readers.get(wr, ()):
                if rd not in deps:
                    deps[rd] = "war"
        deps.pop(idx, None)
        for r in reads:
            self.readers.setdefault(r, []).append(idx)
        for wr in writes:
            self.last_writer[wr] = idx
            self.readers[wr] = []
        self.instrs.append(dict(eng=eng, fn=fn, deps=deps, dma=dma, signal=False, cnt=None, noembed=noembed,
                                semkey=(("dma", dma) if dma is not None else ("eng", eng))))
        return idx

    def emit(self, final_wait_dma="out"):
        nc = self.nc
        ins = self.instrs
        need = []
        for i, I in enumerate(ins):
            per = {}
            for d, kind in I["deps"].items():
                Dd = ins[d]
                if Dd["eng"] == I["eng"] and Dd["dma"] is None and I["dma"] is None:
                    if kind != "raw" or I["eng"] == "pe":
                        continue
                k = Dd["semkey"]
                if d > per.get(k, -1):
                    per[k] = d
            need.append(per)
            for d in per.values():
                ins[d]["signal"] = True
        cnt = {}
        for I in ins:
            key = I["semkey"]
            if I["dma"] is not None:
                cnt[key] = cnt.get(key, 0) + 16
            elif I["signal"]:
                cnt[key] = cnt.get(key, 0) + 1
            I["cnt"] = cnt.get(key, 0)
        st = contextlib.ExitStack()
        sems = {}
        for key in cnt:
            sems[key] = st.enter_context(nc.semaphore("s_" + "_".join(map(str, key))))
        per_eng = {e: [] for e in ENG_ATTR}
        for i, I in enumerate(ins):
            per_eng[I["eng"]].append(i)
        with st, nc.Block() as block:
            def make(engname):
                def body(eng):
                    seen = {}
                    for i in per_eng[engname]:
                        I = ins[i]
                        waits = {}
                        for k, d in need[i].items():
                            v = ins[d]["cnt"]
                            if v > seen.get(k, 0) and v > waits.get(k, (0, 0))[0]:
                                waits[k] = (v, d)
                        if I["dma"] is not None:
                            prev = I["cnt"] - 16
                            k = I["semkey"]
                            if prev > seen.get(k, 0) and prev > waits.get(k, (0, 0))[0]:
                                waits[k] = (prev, -1)
                        I["waits"] = dict(waits)
                        embed = None
                        if waits and I["dma"] is None and not I["noembed"]:
                            embed = max(waits, key=lambda k: waits[k][1])
                        for k, (v, d) in waits.items():
                            if k != embed:
                                eng.wait_ge(sems[k], v)
                            seen[k] = v
                        r = I["fn"](eng)
                        if embed is not None:
                            r = r._wait_ge(sems[embed], waits[embed][0])
                        if I["dma"] is not None:
                            r.then_inc(sems[I["semkey"]], 16)
                        elif I["signal"]:
                            r.then_inc(sems[I["semkey"]], 1)
                    if engname == "sp":
                        for k, v in cnt.items():
                            if k[0] == "dma" and k[1].startswith(final_wait_dma):
                                eng.wait_ge(sems[k], v)
                return body
            for engname, attr in ENG_ATTR.items():
                if per_eng[engname] or engname == "sp":
                    getattr(block, attr)(make(engname))


IN_SPECS = [
    ("x", [2048, 1024]), ("meta", [16, 1024]),
    ("ffn1_norm", [2, 1024]), ("ffn1_w_in", [2, 1024, 5632]), ("ffn1_w_out", [2, 2816, 1024]),
    ("mix_norm", [2, 1024]), ("ffn2_norm", [2, 1024]), ("ffn2_w_in", [2, 1024, 5632]),
    ("ffn2_w_out", [2, 2816, 1024]),
    ("da_w_in", [1, 1024, 3072]), ("da_lambda", [1, 4, 64]), ("da_subln", [1, 128]),
    ("da_w_out", [1, 1024, 1024]),
    ("gla_w_in", [1, 1024, 3088]), ("gla_w_gate2", [1, 16, 512]), ("gla_b_gate2", [1, 512]),
    ("gla_norm", [1, 256]), ("gla_w_out", [1, 1024, 1024]), ("final_norm", [1, 1024]),
    ("c_ident", [128, 128]), ("c_tri", [128, 128]), ("c_cind", [128, 2]), ("c_valid", [128, 128]),
]


def host_consts():
    ident = np.eye(128, dtype=np.float32)
    s = np.arange(128)[:, None]
    t = np.arange(128)[None, :]
    tri = ((s > t) & ((s // 64) == (t // 64))).astype(np.float32) * (-1.0 / 16.0)
    cind = np.zeros((128, 2), np.float32)
    cind[:64, 0] = -1.0 / 16.0
    cind[64:, 1] = -1.0 / 16.0
    valid = np.zeros((128, 128), np.float32)
    valid[NPAD:, :] = 1.0
    return dict(c_ident=ident, c_tri=tri, c_cind=cind, c_valid=valid)


class Builder:
    def __init__(self, n_sub=6, debug=False):
        self.n_sub = n_sub
        self.debug = debug
        nc = bass.Bass("TRN2", target_bir_lowering=False)
        self.nc = nc
        self.din = {}
        for name, shape in IN_SPECS:
            self.din[name] = nc.dram_tensor(name, shape, F32, kind="ExternalInput")
        if debug:
            self.dout = nc.dram_tensor("out", [T, D], F32, kind="ExternalOutput")
            self.ddbg = nc.dram_tensor("dbg", [128, HALF * T], F32, kind="ExternalOutput")
        else:
            self.dout = nc.dram_tensor("out", [2048, D], F32, kind="ExternalOutput")
        self.P = Prog(nc)
        self.gbi = 0
        self.ring_i = 0
        self.psT_i = 0
        self.mm1_i = 0
        self.mm2_i = 0

    def dram(self, name, off, dims):
        return bass.AP(self.din[name], off, dims)

    def alloc(self, st):
        nc = self.nc
        sb = lambda n, s, d: st.enter_context(nc.sbuf_tensor(n, s, d))
        self.h = sb("h", [128, NT, D], F32)
        self.hnT = sb("hnT", [128, DC, T], BF16)
        self.arB = sb("arB", [128, HALF * T], BF16)
        self.wout = sb("wout", [128, HALF * D], BF16)
        self.ring = sb("ring", [128, 4, 2048], BF16)
        self.gb = sb("gb", [128, 1, D], F32)
        self.hntok = sb("hntok", [128, 2, D], BF16)
        self.sqj = sb("sqj", [128, D], BF16)
        self.stats = sb("stats", [128, 80], F32)
        self.stage = sb("stage", [128, 2, 512], F32)
        self.scr = sb("scr", [128, 3072], BF16)
        self.ident = sb("ident", [128, 128], BF16)
        self.cst = sb("cst", [128, 256], BF16)
        self.psall = st.enter_context(nc.psum_tensor("psall", [128, 8, 512], F32))
        self.ps = [self.psall[:, i, :] for i in range(8)]
        self.actT = self.arB[:, :].rearrange("p (f t) -> p f t", t=T)

    def load_inputs(self):
        P, h = self.P, self.h
        P.op("pool", lambda e: e.dma_start(out=self.ident[:, :], in_=self.din["c_ident"].ap()),
             writes=["ident"], dma="const_1")
        P.op("dve", lambda e: e.memset(h[:, 0, :], 0.0), writes=[("h", 0)])
        P.op("sp", lambda e: e.dma_start(out=h[NPAD:128, 0, :], in_=self.din["meta"].ap()),
             writes=[("h", 0)], dma="x0")
        self.load_gain("ffn1_norm", 0)
        xv = self.din["x"].ap().rearrange("(j p) d -> p j d", p=128)
        for j in range(4):
            P.op("sp", lambda e, j=j: e.dma_start(out=h[:, 1 + 4 * j:5 + 4 * j, :], in_=xv[:, 4 * j:4 * j + 4, :]),
                 writes=[("h", 1 + 4 * j + i) for i in range(4)], dma=f"x{j + 1}")

    def load_gain(self, name, l):
        b = 0
        self.P.op("sp", lambda e: e.dma_start(out=self.gb[:, b, :], in_=self.dram(name, l * D, [[0, 128], [1, D]])),
                  writes=[("gb", b)], dma=f"gb{b}")
        return b

    def norm_front(self, tt, g):
        P, h = self.P, self.h
        ss = self.stats[:, tt:tt + 1]
        sd = self.stats[:, 20 + tt:21 + tt]
        rs = self.stats[:, 40 + tt:41 + tt]
        P.op("act", lambda e: e.activation(out=self.sqj[:, :], in_=h[:, tt, :], func=AF.Square, accum_out=ss),
             reads=[("h", tt)], writes=["sqj", ("ss", tt)], noembed=True)
        P.op("act", lambda e: e.activation(out=sd, in_=ss, func=AF.Sqrt, scale=1.0 / D, bias=self.eps_ap),
             reads=[("ss", tt), "eps"], writes=[("sd", tt)])
        P.op("dve", lambda e: e.reciprocal(out=rs, in_=sd), reads=[("sd", tt)], writes=[("rs", tt)])
        b = tt % 2
        P.op("dve", lambda e: e.scalar_tensor_tensor(out=self.hntok[:, b, :], in0=h[:, tt, :], scalar=rs,
                                                     in1=self.gb[:, g, :], op0=ALU.mult, op1=ALU.mult),
             reads=[("h", tt), ("rs", tt), ("gb", g)], writes=[("hntok", b)])

    def norm_back(self, tt):
        P = self.P
        b = tt % 2
        pb = 0 if self.psT_i == 0 else 2
        self.psT_i ^= 1
        psT = self.ps[pb].bitcast(BF16)
        for dc in range(DC):
            P.op("pe", lambda e, dc=dc: e.transpose(out=psT[:, dc * 128:(dc + 1) * 128],
                                                    in_=self.hntok[:, b, dc * 128:(dc + 1) * 128],
                                                    identity=self.ident[:, :]),
                 reads=[("hntok", b), "ident"], writes=[("ps", pb)])
        P.op("act", lambda e: e.copy(out=self.hnT[:, :, tt * 128:(tt + 1) * 128],
                                     in_=psT.rearrange("p (c t) -> p c t", c=DC)),
             reads=[("ps", pb)], writes=[("hnT", tt)])

    def norm_tile(self, tt, g):
        self.norm_front(tt, g)
        self.norm_back(tt)

    def ffn(self, l, which, after_tile=None):
        P = self.P
        w_in, w_out = f"ffn{which}_w_in", f"ffn{which}_w_out"
        actT, ring, wout, hnT, ps = self.actT, self.ring, self.wout, self.hnT, self.ps
        woutv = wout[:, :].rearrange("p (f d) -> p f d", d=D)
        self.fence_arB()
        for half in range(2):
            for fcl in range(HALF):
                fc = half * HALF + fcl
                slot = self.ring_i
                self.ring_i = (self.ring_i + 1) % 4
                for part in range(2):
                    src = self.dram(w_in, l * D * 2 * DFF + part * DFF + fc * 128,
                                    [[2 * DFF, 128], [128 * 2 * DFF, DC], [1, 128]])
                    first = (l == 0 and which == 1 and fc == 0)
                    P.op("pool", lambda e, slot=slot, part=part, src=src: e.dma_start(
                        out=ring[:, slot, part * 1024:(part + 1) * 1024].rearrange("p (c f) -> p c f", f=128), in_=src),
                        writes=[("ring", slot, part)], dma=f"ring{slot}p{part}")
                src = self.dram(w_out, l * DFF * D + fc * 128 * D, [[D, 128], [1, D]])
                P.op("pool", lambda e, fcl=fcl, src=src: e.dma_start(out=woutv[:, fcl, :], in_=src),
                     writes=[("wout", fcl)], dma=f"wout{fcl}")
                for (c0, c1) in TG:
                    n = c1 - c0
                    tl = list(tiles_of(c0, c1))
                    pg = 0 if self.mm1_i == 0 else 2
                    pu = pg + 1
                    sgb = self.mm1_i
                    self.mm1_i ^= 1
                    for part, pbank in ((0, pg), (1, pu)):
                        for dc in range(DC):
                            P.op("pe", lambda e, dc=dc, part=part, pbank=pbank, slot=slot, c0=c0, c1=c1, n=n: e.matmul(
                                ps[pbank][:, 0:n], lhsT=ring[:, slot, part * 1024 + dc * 128:part * 1024 + (dc + 1) * 128],
                                rhs=hnT[:, dc, c0:c1], start=(dc == 0), stop=(dc == DC - 1)),
                                reads=[("ring", slot, part)] + [("hnT", t) for t in tl], writes=[("ps", pbank)])
                    sg = self.stage[:, sgb, 0:n]
                    P.op("act", lambda e, pg=pg, n=n, sg=sg: e.activation(out=sg, in_=ps[pg][:, 0:n], func=AF.Silu),
                         reads=[("ps", pg)], writes=[("stage", sgb)])
                    P.op("dve", lambda e, pu=pu, n=n, sg=sg, fcl=fcl, c0=c0, c1=c1: e.tensor_tensor(
                        out=actT[:, fcl, c0:c1], in0=sg, in1=ps[pu][:, 0:n], op=ALU.mult),
                        reads=[("stage", sgb), ("ps", pu)], writes=[("actT", fcl, t) for t in tl])
            for tt in range(NT):
                pa = 4 if self.mm2_i == 0 else 6
                self.mm2_i ^= 1
                for dh in range(2):
                    for fcl in range(HALF):
                        P.op("pe", lambda e, dh=dh, fcl=fcl, tt=tt, pa=pa: e.matmul(
                            ps[pa + dh][:, :], lhsT=actT[:, fcl, tt * 128:(tt + 1) * 128],
                            rhs=woutv[:, fcl, dh * 512:(dh + 1) * 512], start=(fcl == 0), stop=(fcl == HALF - 1)),
                            reads=[("actT", fcl, tt), ("wout", fcl)], writes=[("ps", pa + dh)])
                for dh in range(2):
                    hs = self.h[:, tt, dh * 512:(dh + 1) * 512]
                    P.op("dve", lambda e, dh=dh, pa=pa, hs=hs: e.scalar_tensor_tensor(
                        out=hs, in0=ps[pa + dh][:, :], scalar=0.5, in1=hs, op0=ALU.mult, op1=ALU.add),
                        reads=[("ps", pa + dh), ("h", tt)], writes=[("h", tt)])
                if half == 1 and after_tile is not None:
                    after_tile(tt)


    def fence_arB(self, extra_writes=()):
        keys = [("actT", f, 0) for f in range(HALF)] + [("oT", c, 0) for c in range(8)] + ["qT", "kT", "vh"]
        self.P.op("pool", lambda e: e.memset(self.actT[:, :, 0:NPAD], 0.0), writes=keys + list(extra_writes))

    def diff_attn(self, l, after_tile=None):
        P, ps, hnT, ring = self.P, self.ps, self.hnT, self.ring
        lam_init = 0.8 - 0.6 * math.exp(-0.3 * l)
        oT = self.arB[:, 0:8 * T].rearrange("p (c t) -> p c t", t=T)
        qT = self.arB[:, 8 * T:9 * T]
        kT = self.arB[:, 9 * T:10 * T]
        vh = self.arB[:, 10 * T:11 * T].rearrange("p (j v) -> p j v", v=128)
        woutv = self.wout[:, :].rearrange("p (f d) -> p f d", d=D)
        pT = self.scr[:, 0:2048].rearrange("p (s m n) -> p s m n", s=2, m=2)
        sqs = self.scr[:, 2048:2560]
        ones = self.cst[:, 0:128]
        valid = self.cst[:, 128:256]
        st = self.stats
        TGA = [(0, 512), (512, 1024), (1024, 1536), (1536, 2048), (2048, 2176)]
        self.fence_arB()
        P.op("pool", lambda e: e.memset(ones, 1.0), writes=["ones"])
        P.op("pool", lambda e: e.dma_start(out=valid, in_=self.din["c_valid"].ap()), writes=["valid"], dma="const_2")
        lamb = self.sqj.bitcast(F32)[:, 0:256]
        P.op("sp", lambda e: e.dma_start(out=lamb, in_=self.dram("da_lambda", 0, [[0, 128], [1, 256]])),
             writes=["sqj"], dma="misc_3")
        pr = self.sqj.bitcast(F32)[:, 256:384]
        P.op("dve", lambda e: e.tensor_tensor(out=pr[:, 0:64], in0=lamb[:, 0:64], in1=lamb[:, 64:128], op=ALU.mult),
             reads=["sqj"], writes=["lam_p1"])
        P.op("dve", lambda e: e.tensor_tensor(out=pr[:, 64:128], in0=lamb[:, 128:192], in1=lamb[:, 192:256], op=ALU.mult),
             reads=["sqj"], writes=["lam_p2"])
        P.op("dve", lambda e: e.reduce_sum(out=st[:, 58:60], in_=pr.rearrange("p (a b) -> p a b", a=2),
                                           axis=mybir.AxisListType.X), reads=["lam_p1", "lam_p2"], writes=["lam_s"])
        P.op("act", lambda e: e.activation(out=st[:, 60:62], in_=st[:, 58:60], func=AF.Exp), reads=["lam_s"], writes=["lam_e"])
        P.op("dve", lambda e: e.tensor_tensor(out=st[:, 62:63], in0=st[:, 61:62], in1=st[:, 60:61], op=ALU.subtract),
             reads=["lam_e"], writes=["lam_d"])
        nlam = st[:, 57:58]
        P.op("dve", lambda e: e.tensor_scalar(out=nlam, in0=st[:, 62:63], scalar1=-lam_init, scalar2=None, op0=ALU.add),
             reads=["lam_d"], writes=["nlam"])
        gcol = st[:, 65:66]
        P.op("sp", lambda e: e.dma_start(out=st[:, 64:65], in_=self.dram("da_subln", 0, [[1, 128], [1, 1]])),
             writes=["gcol_raw"], dma="misc_4")
        P.op("dve", lambda e: e.tensor_scalar(out=gcol, in0=st[:, 64:65], scalar1=(1.0 - lam_init) / math.sqrt(EPS), scalar2=None, op0=ALU.mult),
             reads=["gcol_raw"], writes=["gcol"])
        for c in range(8):
            src = self.dram("da_w_out", c * 128 * D, [[D, 128], [1, D]])
            P.op("pool", lambda e, c=c, src=src: e.dma_start(out=woutv[:, c, :], in_=src),
                 writes=[("wout", c)], dma=f"wout{c}")
        WROW = 3072
        pbank = [0]

        def nextbank():
            b = pbank[0]
            pbank[0] = (b + 1) % 4
            return b

        for hd in range(8):
            sa = self.ring_i
            self.ring_i = (self.ring_i + 1) % 4
            sb_ = self.ring_i
            self.ring_i = (self.ring_i + 1) % 4
            for part, col0 in ((0, hd * 128), (1, 1024 + hd * 128)):
                src = self.dram("da_w_in", col0, [[WROW, 128], [128 * WROW, DC], [1, 128]])
                P.op("pool", lambda e, part=part, src=src, sa=sa: e.dma_start(
                    out=ring[:, sa, part * 1024:(part + 1) * 1024].rearrange("p (c f) -> p c f", f=128), in_=src),
                    writes=[("ring", sa, part)], dma=f"ring{sa}p{part}")
            src = self.dram("da_w_in", 2048 + hd * 128, [[WROW, 128], [128 * WROW, DC], [1, 128]])
            P.op("pool", lambda e, src=src, sb_=sb_: e.dma_start(
                out=ring[:, sb_, 0:1024].rearrange("p (c f) -> p c f", f=128), in_=src),
                writes=[("ring", sb_, 0)], dma=f"ring{sb_}p0")
            for part, dst, key in ((1, kT, "kT"), (0, qT, "qT")):
                for (c0, c1) in TGA:
                    n = c1 - c0
                    b = nextbank()
                    for dc in range(DC):
                        P.op("pe", lambda e, dc=dc, part=part, b=b, c0=c0, c1=c1, n=n, sa=sa: e.matmul(
                            ps[b][:, 0:n], lhsT=ring[:, sa, part * 1024 + dc * 128:part * 1024 + (dc + 1) * 128],
                            rhs=hnT[:, dc, c0:c1], start=(dc == 0), stop=(dc == DC - 1)),
                            reads=[("ring", sa, part)] + [("hnT", t) for t in tiles_of(c0, c1)], writes=[("ps", b)])
                    if part == 0:
                        P.op("act", lambda e, b=b, n=n, c0=c0, c1=c1: e.activation(
                            out=qT[:, c0:c1], in_=ps[b][:, 0:n], func=AF.Copy, scale=0.125),
                            reads=[("ps", b)], writes=["qT"])
                    else:
                        P.op("dve", lambda e, b=b, n=n, c0=c0, c1=c1: e.tensor_copy(out=kT[:, c0:c1], in_=ps[b][:, 0:n]),
                             reads=[("ps", b)], writes=["kT"])
            for t0 in range(0, NT, 4):
                tl = list(range(t0, min(t0 + 4, NT)))
                b = nextbank()
                for j, tt in enumerate(tl):
                    for dc in range(DC):
                        P.op("pe", lambda e, dc=dc, b=b, j=j, tt=tt, sb_=sb_: e.matmul(
                            ps[b][:, j * 128:(j + 1) * 128], lhsT=hnT[:, dc, tt * 128:(tt + 1) * 128],
                            rhs=ring[:, sb_, dc * 128:(dc + 1) * 128], start=(dc == 0), stop=(dc == DC - 1)),
                            reads=[("ring", sb_, 0), ("hnT", tt)], writes=[("ps", b)])
                nn = len(tl)
                P.op("dve", lambda e, b=b, t0=t0, nn=nn: e.tensor_copy(
                    out=vh[:, t0:t0 + nn, :], in_=ps[b][:, 0:nn * 128].rearrange("p (j v) -> p j v", v=128)),
                    reads=[("ps", b)], writes=["vh"])
            pairs = []
            for g in range(5):
                q0 = max(g * 512, NPAD)
                q1 = min((g + 1) * 512, T)
                nkb = q1 // 128
                for kb in range(nkb):
                    pairs.append((g, kb, q0, q1, nkb))

            def issue_S(i):
                g, kb, q0, q1, nkb = pairs[i]
                sset = i % 2
                c0 = max(kb * 128, q0)
                n = q1 - c0
                for m in range(2):
                    P.op("pe", lambda e, m=m, sset=sset, kb=kb, c0=c0, q1=q1, n=n: e.matmul(
                        ps[2 * sset + m][:, 0:n], lhsT=kT[m * 64:(m + 1) * 64, kb * 128:(kb + 1) * 128],
                        rhs=qT[m * 64:(m + 1) * 64, c0:q1], start=True, stop=True),
                        reads=["kT", "qT"], writes=[("ps", 2 * sset + m)])
                P.op("act", lambda e, sset=sset, n=n: e.activation(
                    out=pT[:, sset, :, 0:n], in_=self.psall[:, 2 * sset:2 * sset + 2, 0:n], func=AF.Exp),
                    reads=[("ps", 2 * sset), ("ps", 2 * sset + 1)], writes=[("pT", sset)])
                if kb >= 1 and kb * 128 >= q0:
                    P.op("pool", lambda e, sset=sset: e.memset(pT[64:128, sset, :, 0:64], 0.0), writes=[("pT", sset)])

            def issue_PV(i):
                g, kb, q0, q1, nkb = pairs[i]
                sset = i % 2
                c0 = max(kb * 128, q0)
                n = q1 - c0
                off = c0 - q0
                den_l = valid if kb == 0 else ones
                den_key = "valid" if kb == 0 else "ones"
                diag = kb >= 1 and kb * 128 >= q0
                segs = [(0, n, 128)]
                for which in range(2):
                    for m in range(2):
                        bank = (4 if which == 0 else 6) + m
                        for (a0, a1, kk) in segs:
                            if a1 <= a0:
                                continue
                            lhs = vh[0:kk, kb, :] if which == 0 else den_l[0:kk, :]
                            P.op("pe", lambda e, m=m, sset=sset, kb=kb, off=off, nkb=nkb, bank=bank, a0=a0, a1=a1, kk=kk, lhs=lhs:
                                 e.matmul(ps[bank][:, off + a0:off + a1], lhsT=lhs, rhs=pT[0:kk, sset, m, a0:a1],
                                          start=(kb == 0), stop=(kb == nkb - 1), skip_group_check=True),
                                 reads=["vh" if which == 0 else den_key, ("pT", sset)], writes=[("ps", bank)])

            def issue_evac_a(i, hd=hd):
                g, kb, q0, q1, nkb = pairs[i]
                nq = q1 - q0
                d0s = self.hntok[:, 0, :].bitcast(F32)[:, 0:nq]
                d1s = self.hntok[:, 1, :].bitcast(F32)[:, 0:nq]
                o1s = self.sqj.bitcast(F32)[:, 0:nq]
                wv = self.stage[:, 1, 0:nq]
                t0 = self.stage[:, 0, 0:nq]
                P.op("act", lambda e: e.copy(out=d1s, in_=ps[7][:, 0:nq]), reads=[("ps", 7)], writes=[("hntok", 1)])
                P.op("act", lambda e: e.copy(out=d0s, in_=ps[6][:, 0:nq]), reads=[("ps", 6)], writes=[("hntok", 0)])
                P.op("dve", lambda e: e.tensor_copy(out=o1s, in_=ps[5][:, 0:nq]), reads=[("ps", 5)], writes=["sqj"])
                P.op("dve", lambda e: e.tensor_tensor(out=t0, in0=ps[4][:, 0:nq], in1=d1s, op=ALU.mult),
                     reads=[("ps", 4), ("hntok", 1)], writes=[("stage", 0)])
                P.op("dve", lambda e: e.tensor_tensor(out=wv, in0=o1s, in1=d0s, op=ALU.mult),
                     reads=["sqj", ("hntok", 0)], writes=[("stage", 1)])
                P.op("dve", lambda e: e.scalar_tensor_tensor(out=wv, in0=wv, scalar=nlam, in1=t0, op0=ALU.mult, op1=ALU.add),
                     reads=[("stage", 0), ("stage", 1), "nlam"], writes=[("stage", 1)])
                P.op("pool", lambda e: e.tensor_tensor(out=d0s, in0=d0s, in1=d1s, op=ALU.mult),
                     reads=[("hntok", 0), ("hntok", 1)], writes=[("hntok", 0)])
                P.op("pool", lambda e: e.tensor_tensor(out=d0s, in0=d0s, in1=d0s, op=ALU.mult),
                     reads=[("hntok", 0)], writes=[("hntok", 0)])
                P.op("pool", lambda e: e.tensor_tensor(out=sqs[:, 0:nq], in0=wv, in1=wv, op=ALU.mult),
                     reads=[("stage", 1)], writes=["sqs"])

            def issue_evac_b(i, hd=hd):
                g, kb, q0, q1, nkb = pairs[i]
                nq = q1 - q0
                tl = list(tiles_of(q0, q1))
                d0s = self.hntok[:, 0, :].bitcast(F32)[:, 0:nq]
                wv = self.stage[:, 1, 0:nq]
                t0 = self.stage[:, 0, 0:nq]
                b = nextbank()
                P.op("pe", lambda e: e.matmul(ps[b][:, 0:nq], lhsT=ones, rhs=sqs[:, 0:nq], start=True, stop=True),
                     reads=["ones", "sqs"], writes=[("ps", b)])
                P.op("dve", lambda e: e.tensor_scalar(out=t0, in0=ps[b][:, 0:nq], scalar1=1.0 / (128 * EPS), scalar2=None, op0=ALU.mult),
                     reads=[("ps", b)], writes=[("stage", 0)])
                P.op("dve", lambda e: e.tensor_tensor(out=t0, in0=t0, in1=d0s, op=ALU.add),
                     reads=[("stage", 0), ("hntok", 0)], writes=[("stage", 0)])
                P.op("act", lambda e: e.activation(out=t0, in_=t0, func=AF.Ln), reads=[("stage", 0)], writes=[("stage", 0)])
                P.op("act", lambda e: e.activation(out=t0, in_=t0, func=AF.Exp, scale=-0.5), reads=[("stage", 0)], writes=[("stage", 0)])
                P.op("dve", lambda e: e.scalar_tensor_tensor(out=oT[:, hd, q0:q1], in0=wv, scalar=gcol, in1=t0,
                                                             op0=ALU.mult, op1=ALU.mult),
                     reads=[("stage", 0), ("stage", 1), "gcol"], writes=[("oT", hd, t) for t in tl])

            issue_S(0)
            pend = None
            for i in range(len(pairs)):
                if i + 1 < len(pairs):
                    issue_S(i + 1)
                issue_PV(i)
                if pend is not None and i >= pend + 3:
                    issue_evac_b(pend)
                    pend = None
                if pairs[i][1] == pairs[i][4] - 1:
                    if pend is not None:
                        issue_evac_b(pend)
                    issue_evac_a(i)
                    pend = i
            if pend is not None:
                issue_evac_b(pend)
        for tt in range(NT):
            pa = 4 if self.mm2_i == 0 else 6
            self.mm2_i ^= 1
            for dh in range(2):
                for c in range(8):
                    P.op("pe", lambda e, dh=dh, c=c, tt=tt, pa=pa: e.matmul(
                        ps[pa + dh][:, :], lhsT=oT[:, c, tt * 128:(tt + 1) * 128],
                        rhs=woutv[:, c, dh * 512:(dh + 1) * 512], start=(c == 0), stop=(c == 7)),
                        reads=[("oT", c, tt), ("wout", c)], writes=[("ps", pa + dh)])
            for dh in range(2):
                hs = self.h[:, tt, dh * 512:(dh + 1) * 512]
                P.op("dve", lambda e, dh=dh, pa=pa, hs=hs: e.tensor_tensor(out=hs, in0=ps[pa + dh][:, :], in1=hs, op=ALU.add),
                     reads=[("ps", pa + dh), ("h", tt)], writes=[("h", tt)])
            if after_tile is not None:
                after_tile(tt)


    def gla(self, l, after_tile=None):
        P, ps, hnT, ring = self.P, self.ps, self.hnT, self.ring
        oT = self.arB[:, 0:8 * T].rearrange("p (c t) -> p c t", t=T)
        qT = self.arB[:, 8 * T:9 * T]
        kdec = self.arB[:, 9 * T:10 * T].rearrange("p (j v) -> p j v", v=128)
        woutv = self.wout[:, :].rearrange("p (f d) -> p f d", d=D)
        vh = self.wout[:, 0:NT * 256].rearrange("p (j v) -> p j v", v=256)
        VHK = [("wout", f) for f in range(5)]
        S32 = self.wout[:, 5120:6144].bitcast(F32).rearrange("p (s v) -> p s v", s=2)
        Sb = self.wout[:, 6144:6912].rearrange("p (s v) -> p s v", s=3)
        glrT = self.scr[0:32, 0:T]
        wglr = self.scr[:, T:T + 128].rearrange("p (c r) -> p c r", r=16)
        sqs = self.scr[:, 2304:2816]
        wg2 = self.hntok[0:17, 0, 0:512]
        gbf = self.gb[:, 0, :]
        tri = gbf[:, 0:128]
        cind = gbf[:, 128:130]
        decay = gbf[:, 130:130 + 4 * 34].rearrange("p (h n) -> p h n", h=4)
        ones = self.cst[:, 0:128]
        st = self.stats
        one_col = st[:, 69:70]
        gcol = st[:, 66:68]
        stage = self.stage
        WROW = 3088
        TGA = [(0, 512), (512, 1024), (1024, 1536), (1536, 2048), (2048, 2176)]
        GB = ("gb", 0)
        self.fence_arB()
        P.op("pool", lambda e: e.memset(ones, 1.0), writes=["ones"])
        P.op("dve", lambda e: e.memset(one_col, 1.0), writes=["one_col"])
        P.op("dve", lambda e: e.memset(Sb[:, :, :], 0.0), writes=[("Sb", 0), ("Sb", 1), ("Sb", 2), ("wout", 6)])
        P.op("sp", lambda e: e.dma_start(out=tri, in_=self.din["c_tri"].ap()), writes=["tri", GB], dma="misc_5")
        P.op("sp", lambda e: e.dma_start(out=cind, in_=self.din["c_cind"].ap()), writes=["cind", GB], dma="misc_6")
        for a in range(2):
            P.op("sp", lambda e, a=a: e.dma_start(out=gcol[:, a:a + 1], in_=self.dram("gla_norm", a * 128, [[1, 128], [1, 1]])),
                 writes=["gcolg"], dma="misc_7")
        P.op("pool", lambda e: e.memset(glrT, 1.0), writes=["glrT", ("pT", 0), ("pT", 1), "sqs"])
        P.op("pool", lambda e: e.dma_start(out=wglr, in_=self.dram("gla_w_in", 3072, [[WROW, 128], [128 * WROW, DC], [1, 16]])),
             writes=["wglr", ("pT", 0), ("pT", 1), "sqs"], dma="const_8")
        P.op("pool", lambda e: e.dma_start(out=wg2[0:16, :], in_=self.din["gla_w_gate2"].ap()[0]),
             writes=[("hntok", 0)], dma="const_9")
        P.op("pool", lambda e: e.dma_start(out=wg2[16:17, :], in_=self.din["gla_b_gate2"].ap()),
             writes=[("hntok", 0)], dma="const_10")
        pbank = [0]

        def nextbank():
            b = pbank[0]
            pbank[0] = (b + 1) % 4
            return b

        for (c0, c1) in TGA:
            n = c1 - c0
            b = nextbank()
            for dc in range(DC):
                P.op("pe", lambda e, dc=dc, b=b, c0=c0, c1=c1, n=n: e.matmul(
                    ps[b][0:16, 0:n], lhsT=wglr[:, dc, :], rhs=hnT[:, dc, c0:c1], start=(dc == 0), stop=(dc == DC - 1)),
                    reads=["wglr"] + [("hnT", t) for t in tiles_of(c0, c1)], writes=[("ps", b)])
            P.op("dve", lambda e, b=b, n=n, c0=c0, c1=c1: e.tensor_copy(out=glrT[0:16, c0:c1], in_=ps[b][0:16, 0:n]),
                 reads=[("ps", b)], writes=["glrT"])
        for hd in range(4):
            sa = self.ring_i
            self.ring_i = (self.ring_i + 1) % 4
            sb_ = self.ring_i
            self.ring_i = (self.ring_i + 1) % 4
            for part, col0 in ((0, hd * 128), (1, 512 + hd * 128)):
                src = self.dram("gla_w_in", col0, [[WROW, 128], [128 * WROW, DC], [1, 128]])
                P.op("pool", lambda e, part=part, src=src, sa=sa: e.dma_start(
                    out=ring[:, sa, part * 1024:(part + 1) * 1024].rearrange("p (c f) -> p c f", f=128), in_=src),
                    writes=[("ring", sa, part)], dma=f"ring{sa}p{part}")
            src = self.dram("gla_w_in", 1024 + hd * 256, [[WROW, 128], [128 * WROW, DC], [1, 256]])
            P.op("pool", lambda e, src=src, sb_=sb_: e.dma_start(
                out=ring[:, sb_, :].rearrange("p (c f) -> p c f", f=256), in_=src),
                writes=[("ring", sb_, 0), ("ring", sb_, 1)], dma=f"ring{sb_}p0")
            for (c0, c1) in TGA:
                n = c1 - c0
                b = nextbank()
                for dc in range(DC):
                    P.op("pe", lambda e, dc=dc, b=b, c0=c0, c1=c1, n=n, sa=sa: e.matmul(
                        ps[b][:, 0:n], lhsT=ring[:, sa, dc * 128:(dc + 1) * 128], rhs=hnT[:, dc, c0:c1],
                        start=(dc == 0), stop=(dc == DC - 1)),
                        reads=[("ring", sa, 0)] + [("hnT", t) for t in tiles_of(c0, c1)], writes=[("ps", b)])
                P.op("act", lambda e, b=b, n=n, c0=c0, c1=c1: e.activation(
                    out=qT[:, c0:c1], in_=ps[b][:, 0:n], func=AF.Copy, scale=128 ** -0.5),
                    reads=[("ps", b)], writes=["qT"])
            def front(tt, hd=hd, sa=sa, sb_=sb_):
                b = tt % 3
                par = tt % 2
                for dc in range(DC):
                    P.op("pe", lambda e, dc=dc: e.matmul(
                        ps[b][:, 0:128], lhsT=hnT[:, dc, tt * 128:(tt + 1) * 128],
                        rhs=ring[:, sa, 1024 + dc * 128:1024 + (dc + 1) * 128], start=(dc == 0), stop=(dc == DC - 1)),
                        reads=[("ring", sa, 1), ("hnT", tt)], writes=[("ps", b)])
                for dc in range(DC):
                    P.op("pe", lambda e, dc=dc: e.matmul(
                        ps[b][:, 128:384], lhsT=hnT[:, dc, tt * 128:(tt + 1) * 128],
                        rhs=ring[:, sb_, dc * 256:(dc + 1) * 256], start=(dc == 0), stop=(dc == DC - 1)),
                        reads=[("ring", sb_, 0), ("ring", sb_, 1), ("hnT", tt)], writes=[("ps", b)])
                P.op("pe", lambda e: e.matmul(
                    ps[b][:, 384:512], lhsT=glrT[0:17, tt * 128:(tt + 1) * 128], rhs=wg2[:, hd * 128:(hd + 1) * 128],
                    start=True, stop=True), reads=["glrT", ("hntok", 0)], writes=[("ps", b)])
                sp_ = stage[:, par, 0:128]
                P.op("act", lambda e: e.activation(out=sp_, in_=ps[b][:, 384:512], func=AF.Exp, scale=-1.0),
                     reads=[("ps", b)], writes=[("stage", par)])
                P.op("act", lambda e: e.activation(out=sp_, in_=sp_, func=AF.Ln, bias=one_col),
                     reads=[("stage", par), "one_col"], writes=[("stage", par)])

            def back(tt, hd=hd):
                b = tt % 3
                par = tt % 2
                b2 = 3
                sp_ = stage[:, par, 0:128]
                eR = stage[:, par, 128:256]
                P.op("pe", lambda e: e.matmul(ps[b2][:, 0:128], lhsT=tri, rhs=sp_, start=True, stop=True),
                     reads=["tri", GB, ("stage", par)], writes=[("ps", b2)])
                P.op("pe", lambda e: e.matmul(ps[b2][:, 128:130], lhsT=sp_, rhs=cind, start=True, stop=True),
                     reads=["cind", GB, ("stage", par)], writes=[("ps", b2)])
                P.op("act", lambda e: e.activation(out=eR, in_=ps[b2][:, 0:128], func=AF.Exp),
                     reads=[("ps", b2)], writes=[("stage_e", par)])
                P.op("act", lambda e: e.activation(out=decay[:, hd, 2 * tt:2 * tt + 2], in_=ps[b2][:, 128:130], func=AF.Exp),
                     reads=[("ps", b2)], writes=["decay"])
                P.op("dve", lambda e: e.tensor_tensor(out=kdec[:, tt, :], in0=ps[b][:, 0:128], in1=eR, op=ALU.mult),
                     reads=[("ps", b), ("stage_e", par)], writes=["kT"])
                P.op("dve", lambda e: e.tensor_copy(out=vh[:, tt, :], in_=ps[b][:, 128:384]),
                     reads=[("ps", b)], writes=VHK)

            for tt in range(NT):
                front(tt)
                if tt >= 1:
                    back(tt - 1)
            back(NT - 1)
            P.op("dve", lambda e: e.memset(S32[:, 1, :], 0.0), writes=[("S32", 1), ("wout", 5)])

            def kvmm(n_):
                tt, hf = n_ // 2, n_ % 2
                kb = 3 + (n_ % 3)
                P.op("pe", lambda e: e.matmul(
                    ps[kb][:, 0:256], lhsT=kdec[hf * 64:(hf + 1) * 64, tt, :], rhs=vh[hf * 64:(hf + 1) * 64, tt, :],
                    start=True, stop=True), reads=["kT"] + VHK, writes=[("ps", kb)])

            def state(n_, hd=hd):
                cur, prev = n_ % 2, 1 - (n_ % 2)
                kb = 3 + (n_ % 3)
                sbi = n_ % 3
                P.op("dve", lambda e: e.scalar_tensor_tensor(
                    out=S32[:, cur, :], in0=S32[:, prev, :], scalar=decay[:, hd, n_:n_ + 1], in1=ps[kb][:, 0:256],
                    op0=ALU.mult, op1=ALU.add),
                    reads=[("S32", prev), "decay", GB, ("ps", kb)], writes=[("S32", cur)])
                P.op("act", lambda e: e.copy(out=Sb[:, sbi, :], in_=S32[:, cur, :]),
                     reads=[("S32", cur)], writes=[("Sb", sbi)])

            def omm(n_, hd=hd):
                sbi = n_ % 3
                grp = n_ // 8
                ob = 6 if grp % 2 == 0 else 0
                for a in range(2):
                    P.op("pe", lambda e, a=a: e.matmul(
                        ps[ob + a][:, (n_ % 8) * 64:(n_ % 8) * 64 + 64], lhsT=Sb[:, sbi, a * 128:(a + 1) * 128],
                        rhs=qT[:, n_ * 64:(n_ + 1) * 64], start=True, stop=True),
                        reads=[("Sb", sbi), "qT"], writes=[("ps", ob + a)])
                if n_ % 8 == 7 or n_ == 33:
                    c0 = grp * 512
                    c1 = (n_ + 1) * 64
                    n = c1 - c0
                    for a in range(2):
                        P.op("dve" if a == 0 else "act", (lambda e, a=a:
                             (e.tensor_copy(out=oT[:, hd * 2 + a, c0:c1], in_=ps[ob + a][:, 0:n]) if a == 0 else
                              e.copy(out=oT[:, hd * 2 + a, c0:c1], in_=ps[ob + a][:, 0:n]))),
                             reads=[("ps", ob + a)], writes=[("oT", hd * 2 + a, t) for t in tiles_of(c0, c1)])

            kvmm(0)
            kvmm(1)
            for n_ in range(34):
                state(n_)
                if n_ + 2 < 34:
                    kvmm(n_ + 2)
                omm(n_)
        extra = [("S32", 0), ("S32", 1), ("Sb", 0), ("Sb", 1), ("Sb", 2)]
        for c in range(8):
            src = self.dram("gla_w_out", c * 128 * D, [[D, 128], [1, D]])
            P.op("pool", lambda e, c=c, src=src: e.dma_start(out=woutv[:, c, :], in_=src),
                 writes=[("wout", c)] + (extra if c in (5, 6) else []), dma=f"wout{c}")
        for hd in range(4):
            for (c0, c1) in TGA:
                n = c1 - c0
                tl = list(tiles_of(c0, c1))
                b = nextbank()
                for a in range(2):
                    c = hd * 2 + a
                    P.op("pool", lambda e, c=c, c0=c0, c1=c1, n=n: e.tensor_tensor(
                        out=sqs[:, 0:n], in0=oT[:, c, c0:c1], in1=oT[:, c, c0:c1], op=ALU.mult),
                        reads=[("oT", c, t) for t in tl], writes=["sqs"])
                    P.op("pe", lambda e, a=a, b=b, n=n: e.matmul(ps[b][:, 0:n], lhsT=ones, rhs=sqs[:, 0:n],
                                                                start=(a == 0), stop=(a == 1)),
                         reads=["ones", "sqs"], writes=[("ps", b)])
                P.op("act", lambda e, b=b, n=n: e.activation(out=stage[:, 0, 0:n], in_=ps[b][:, 0:n], func=AF.Ln,
                                                             scale=1.0 / 256, bias=self.eps_ap),
                     reads=[("ps", b), "eps"], writes=[("stage", 0)])
                P.op("act", lambda e, n=n: e.activation(out=stage[:, 0, 0:n], in_=stage[:, 0, 0:n], func=AF.Exp, scale=-0.5),
                     reads=[("stage", 0)], writes=[("stage", 0)])
                for a in range(2):
                    c = hd * 2 + a
                    P.op("dve", lambda e, a=a, c=c, c0=c0, c1=c1, n=n: e.scalar_tensor_tensor(
                        out=oT[:, c, c0:c1], in0=oT[:, c, c0:c1], scalar=gcol[:, a:a + 1], in1=stage[:, 0, 0:n],
                        op0=ALU.mult, op1=ALU.mult),
                        reads=[("oT", c, t) for t in tl] + [("stage", 0), "gcolg"], writes=[("oT", c, t) for t in tl])
        for c in range(8):
            if c % 2 == 0:
                sg_ = self.ring_i
                self.ring_i = (self.ring_i + 1) % 4
            part = c % 2
            src = self.dram("gla_w_in", 2048 + c * 128, [[WROW, 128], [128 * WROW, DC], [1, 128]])
            P.op("pool", lambda e, part=part, src=src, sg_=sg_: e.dma_start(
                out=ring[:, sg_, part * 1024:(part + 1) * 1024].rearrange("p (c f) -> p c f", f=128), in_=src),
                writes=[("ring", sg_, part)], dma=f"ring{sg_}p{part}")
            for gi, (c0, c1) in enumerate(TGA):
                n = c1 - c0
                tl = list(tiles_of(c0, c1))
                b = nextbank()
                for dc in range(DC):
                    P.op("pe", lambda e, dc=dc, b=b, c0=c0, c1=c1, n=n, sg_=sg_, part=part: e.matmul(
                        ps[b][:, 0:n], lhsT=ring[:, sg_, part * 1024 + dc * 128:part * 1024 + (dc + 1) * 128],
                        rhs=hnT[:, dc, c0:c1], start=(dc == 0), stop=(dc == DC - 1)),
                        reads=[("ring", sg_, part)] + [("hnT", t) for t in tl], writes=[("ps", b)])
                sgb = gi % 2
                P.op("act", lambda e, b=b, n=n, sgb=sgb: e.activation(out=stage[:, sgb, 0:n], in_=ps[b][:, 0:n], func=AF.Silu),
                     reads=[("ps", b)], writes=[("stage", sgb)])
                P.op("pool", lambda e, c=c, c0=c0, c1=c1, n=n, sgb=sgb: e.tensor_tensor(
                    out=oT[:, c, c0:c1], in0=oT[:, c, c0:c1], in1=stage[:, sgb, 0:n], op=ALU.mult),
                    reads=[("oT", c, t) for t in tl] + [("stage", sgb)], writes=[("oT", c, t) for t in tl])
        for tt in range(NT):
            pa = 4 if self.mm2_i == 0 else 6
            self.mm2_i ^= 1
            for dh in range(2):
                for c in range(8):
                    P.op("pe", lambda e, dh=dh, c=c, tt=tt, pa=pa: e.matmul(
                        ps[pa + dh][:, :], lhsT=oT[:, c, tt * 128:(tt + 1) * 128],
                        rhs=woutv[:, c, dh * 512:(dh + 1) * 512], start=(c == 0), stop=(c == 7)),
                        reads=[("oT", c, tt), ("wout", c)], writes=[("ps", pa + dh)])
            for dh in range(2):
                hs = self.h[:, tt, dh * 512:(dh + 1) * 512]
                P.op("dve", lambda e, dh=dh, pa=pa, hs=hs: e.tensor_tensor(out=hs, in0=ps[pa + dh][:, :], in1=hs, op=ALU.add),
                     reads=[("ps", pa + dh), ("h", tt)], writes=[("h", tt)])
            if after_tile is not None:
                after_tile(tt)

    def final(self, tt, g):
        P, h = self.P, self.h
        ss = self.stats[:, tt:tt + 1]
        sd = self.stats[:, 20 + tt:21 + tt]
        rs = self.stats[:, 40 + tt:41 + tt]
        P.op("act", lambda e: e.activation(out=self.sqj[:, :], in_=h[:, tt, :], func=AF.Square, accum_out=ss),
             reads=[("h", tt)], writes=["sqj", ("ss", tt)], noembed=True)
        P.op("act", lambda e: e.activation(out=sd, in_=ss, func=AF.Sqrt, scale=1.0 / D, bias=self.eps_ap),
             reads=[("ss", tt), "eps"], writes=[("sd", tt)])
        P.op("dve", lambda e: e.reciprocal(out=rs, in_=sd), reads=[("sd", tt)], writes=[("rs", tt)])
        b = tt % 2
        P.op("dve", lambda e: e.scalar_tensor_tensor(out=h[:, tt, :], in0=h[:, tt, :], scalar=rs,
                                                     in1=self.gb[:, g, :], op0=ALU.mult, op1=ALU.mult),
             reads=[("h", tt), ("rs", tt), ("gb", g)], writes=[("h", tt)])
        P.op("sp", lambda e: e.dma_start(out=self.dout.ap()[(tt - 1) * 128:tt * 128, :], in_=h[:, tt, :]),
             reads=[("h", tt)], dma=f"out{tt}")

    def dump_h(self):
        hv = self.dout.ap().rearrange("(j p) d -> p j d", p=128)
        self.P.op("sp", lambda e: e.dma_start(out=hv, in_=self.h[:, :, :]),
                  reads=[("h", t) for t in range(NT)], dma="out")
        keys = [("actT", f, t) for f in range(HALF) for t in range(NT)] + [("oT", c, t) for c in range(8) for t in range(NT)] + ["qT", "kT", "vh"]
        if getattr(self, "dbg_src", "arB") == "hnT":
            for f in range(DC):
                self.P.op("pool", lambda e, f=f: e.dma_start(out=self.ddbg.ap()[:, f * T:(f + 1) * T], in_=self.hnT[:, f, :]),
                          reads=[("hnT", t) for t in range(NT)], dma="out")
            self.P.op("pool", lambda e: e.dma_start(out=self.ddbg.ap()[:, 8 * T:8 * T + 80], in_=self.stats[:, 0:80]),
                      reads=[("rs", t) for t in range(NT)], dma="out")
            self.P.op("pool", lambda e: e.dma_start(out=self.ddbg.ap()[:, 9 * T:9 * T + 1024], in_=self.gb[:, 0, :]),
                      reads=[("gb", 0)], dma="out")
            return
        for f in range(HALF):
            self.P.op("pool", lambda e, f=f: e.dma_start(out=self.ddbg.ap()[:, f * T:(f + 1) * T], in_=self.arB[:, f * T:(f + 1) * T]),
                      reads=keys, dma="out")

    def build(self):
        nc = self.nc
        st = contextlib.ExitStack()
        with st:
            self.alloc(st)
            P = self.P
            self.eps_ap = self.stats[:, 63:64]
            P.op("dve", lambda e: e.memset(self.eps_ap, EPS), writes=["eps"])
            self.load_inputs()
            subs = [("ffn", 0, 1), ("da", 0, 0), ("ffn", 0, 2), ("ffn", 1, 1), ("gla", 1, 0), ("ffn", 1, 2)]
            norm_name = {("ffn", 1): "ffn1_norm", ("ffn", 2): "ffn2_norm", ("da", 0): "mix_norm", ("gla", 0): "mix_norm"}
            subs = subs[:self.n_sub]
            for si, (kind, l, which) in enumerate(subs):
                if si == 0:
                    for tt in range(NT):
                        self.norm_tile(tt, 0)
                if getattr(self, "skip_last_body", False) and si == len(subs) - 1:
                    self.dbg_src = "hnT"
                    break
                if si + 1 < len(subs):
                    nk, nl, nw = subs[si + 1]

                    def cb(tt, nk=nk, nl=nl, nw=nw):
                        if tt == 0:
                            self.load_gain(norm_name[(nk, nw)], nl)
                        self.norm_front(tt, 0)
                        if tt >= 1:
                            self.norm_back(tt - 1)
                        if tt == NT - 1:
                            self.norm_back(tt)
                elif not self.debug:
                    def cb(tt):
                        if tt == 0:
                            self.load_gain("final_norm", 0)
                        else:
                            self.final(tt, 0)
                else:
                    cb = None
                if kind == "ffn":
                    self.ffn(l, which, after_tile=cb)
                elif kind == "da":
                    self.diff_attn(l, after_tile=cb)
                else:
                    self.gla(l, after_tile=cb)
            if self.debug:
                self.dump_h()
            P.emit(final_wait_dma="out")
        return nc


_CACHE = {}


def kernel(**inputs):
    n = 8
    if "nc" not in _CACHE:
        _CACHE["nc"] = Builder().build()
    nc = _CACHE["nc"]
    consts = host_consts()
    shared = {}
    for name, shape in IN_SPECS:
        if name == "x" or name.startswith("c_"):
            continue
        shared[name] = np.ascontiguousarray(np.asarray(inputs[name], dtype=np.float32).reshape(shape))
    x = np.asarray(inputs["x"], dtype=np.float32)
    in_maps = []
    for c in range(n):
        m = dict(shared)
        m.update(consts)
        m["x"] = np.ascontiguousarray(x[c])
        in_maps.append(m)
    res = run_bass_kernel_spmd(nc, in_maps, core_ids=list(range(n)))
    return np.stack([np.asarray(r["out"], dtype=np.float32) for r in res.results], axis=0)
```
